# Optimizing a Trainium2 kernel written in Bass

```python
import math
import jax, jax.numpy as jnp
from jax import lax
import numpy as np

D_MODEL = 2048
BATCH = 2
SEQ = 4096
DEPTH = 1

CHUNK = 64
Q_BLOCK = 128
EPS = 1e-6
ROPE_THETA = 10000.0

RET_HEADS = 8
RET_QK_DIM = 128
RET_V_DIM = 256
RET_QK_W = RET_HEADS * RET_QK_DIM
RET_V_W = RET_HEADS * RET_V_DIM

MLA_HEADS = 16
Q_LORA = 512
KV_LORA = 512
QK_NOPE = 128
QK_ROPE = 64
V_HEAD = 128
MLA_QK_DIM = QK_NOPE + QK_ROPE
MLA_V_W = MLA_HEADS * V_HEAD

D_FF = 4 * D_MODEL

IN_SPLITS = (RET_QK_W, RET_QK_W, RET_V_W, RET_V_W, Q_LORA, KV_LORA, QK_ROPE, D_MODEL, D_MODEL)
N_IN = sum(IN_SPLITS)

kernel_name = "hybrid_retention_mla_gated_encoder"


def _rmsnorm(x, g):
    x32 = x.astype(jnp.float32)
    y = x32 * lax.rsqrt(jnp.mean(x32 * x32, axis=-1, keepdims=True) + EPS)
    return (y * g.astype(jnp.float32)).astype(x.dtype)


def _rope(t, positions):
    half = t.shape[-1] // 2
    inv = ROPE_THETA ** (-jnp.arange(half, dtype=jnp.float32) / half)
    ang = positions.astype(jnp.float32)[..., None] * inv
    cos = jnp.cos(ang)[:, :, None, :]
    sin = jnp.sin(ang)[:, :, None, :]
    t32 = t.astype(jnp.float32)
    t1, t2 = t32[..., :half], t32[..., half:]
    out = jnp.concatenate([t1 * cos - t2 * sin, t2 * cos + t1 * sin], axis=-1)
    return out.astype(t.dtype)


def _retention(q, k, v):
    B, S, H, dk = q.shape
    dv = v.shape[-1]
    nc = S // CHUNK
    log_gamma = jnp.log(1.0 - 2.0 ** (-5.0 - jnp.arange(H, dtype=jnp.float32)))

    def to_chunks(t):
        return t.reshape(B, nc, CHUNK, H, t.shape[-1]).transpose(1, 0, 3, 2, 4)

    pos = jnp.arange(CHUNK, dtype=jnp.float32)
    lg = log_gamma[:, None]
    intra = jnp.exp(lg[..., None] * jnp.abs(pos[:, None] - pos[None, :]))
    q_dec = jnp.exp(lg * (pos + 1.0))
    k_dec = jnp.exp(lg * (CHUNK - 1.0 - pos))
    c_dec = jnp.exp(log_gamma * CHUNK)

    def step(state, inp):
        qi, ki, vi = inp
        s = jnp.einsum('bhnd,bhmd->bhnm', qi, ki) * intra
        o = (jnp.einsum('bhnm,bhme->bhne', s, vi)
             + jnp.einsum('bhnd,bhde->bhne', qi * q_dec[..., None], state))
        state = state * c_dec[:, None, None] + jnp.einsum(
            'bhmd,bhme->bhde', ki * k_dec[..., None], vi)
        return state, o

    state0 = jnp.zeros((B, H, dk, dv), jnp.float32)
    _, o = lax.scan(step, state0, (to_chunks(q), to_chunks(k), to_chunks(v)))
    return o.transpose(1, 0, 3, 2, 4).reshape(B, S, H, dv)


def _block_causal_attention(q, k, v, scale):
    B, S, H, dqk = q.shape
    nb = S // Q_BLOCK
    qb = q.reshape(B, nb, Q_BLOCK, H, dqk).transpose(1, 0, 2, 3, 4)
    key_chunk = jnp.arange(S) // CHUNK

    def one(args):
        qi, bi = args
        q_chunk = (bi * Q_BLOCK + jnp.arange(Q_BLOCK)) // CHUNK
        mask = key_chunk[None, :] <= q_chunk[:, None]
        s = jnp.einsum('bqhd,bkhd->bhqk', qi, k,
                       preferred_element_type=jnp.float32) * scale
        s = jnp.where(mask, s, -jnp.inf)
        p = jax.nn.softmax(s, axis=-1).astype(v.dtype)
        return jnp.einsum('bhqk,bkhe->bqhe', p, v)

    o = lax.map(one, (qb, jnp.arange(nb)))
    return o.transpose(1, 0, 2, 3, 4).reshape(B, S, H, v.shape[-1])


def _mixer(u, positions, w_in, ret_norm_g, w_ret_o, q_a_norm_g, w_q_b,
           kv_a_norm_g, w_kv_b, w_mla_o, w_out):
    B, S, _ = u.shape
    proj = u @ w_in
    offs = []
    acc = 0
    for w in IN_SPLITS[:-1]:
        acc += w
        offs.append(acc)
    (r_q, r_k, r_v, r_g, c_q, c_kv, k_pe, g_ret, g_mla) = jnp.split(proj, offs, axis=-1)

    rq = _rope(r_q.reshape(B, S, RET_HEADS, RET_QK_DIM), positions).astype(jnp.float32)
    rk = _rope(r_k.reshape(B, S, RET_HEADS, RET_QK_DIM), positions).astype(jnp.float32)
    rk = rk * (RET_QK_DIM ** -0.5)
    rv = r_v.reshape(B, S, RET_HEADS, RET_V_DIM).astype(jnp.float32)
    ry = _retention(rq, rk, rv)
    mu = jnp.mean(ry, axis=-1, keepdims=True)
    var = jnp.mean(jnp.square(ry - mu), axis=-1, keepdims=True)
    ry = ((ry - mu) * lax.rsqrt(var + EPS)).reshape(B, S, RET_V_W) * ret_norm_g.astype(jnp.float32)
    ry = ry.astype(u.dtype) * jax.nn.silu(r_g)
    y_ret = ry @ w_ret_o

    q = (_rmsnorm(c_q, q_a_norm_g) @ w_q_b).reshape(B, S, MLA_HEADS, MLA_QK_DIM)
    q_nope, q_pe = q[..., :QK_NOPE], _rope(q[..., QK_NOPE:], positions)
    kv = (_rmsnorm(c_kv, kv_a_norm_g) @ w_kv_b).reshape(B, S, MLA_HEADS, QK_NOPE + V_HEAD)
    k_nope, v = kv[..., :QK_NOPE], kv[..., QK_NOPE:]
    k_pe = _rope(k_pe.reshape(B, S, 1, QK_ROPE), positions)
    qf = jnp.concatenate([q_nope, q_pe], axis=-1)
    kf = jnp.concatenate([k_nope, jnp.broadcast_to(k_pe, (B, S, MLA_HEADS, QK_ROPE))], axis=-1)
    my = _block_causal_attention(qf, kf, v, MLA_QK_DIM ** -0.5)
    y_mla = my.reshape(B, S, MLA_V_W) @ w_mla_o

    merged = jax.nn.sigmoid(g_ret) * y_ret + jax.nn.sigmoid(g_mla) * y_mla
    return merged @ w_out


def setup_inputs(seed: int = 0) -> dict:
    key = jax.random.key(seed)
    ks = jax.random.split(key, 20)
    f32 = jnp.float32

    def nrm(k, shape, fan_in):
        return jax.random.normal(k, shape, f32) * (fan_in ** -0.5)

    def gain(k, shape):
        return 1.0 + 0.02 * jax.random.normal(k, shape, f32)

    x = jax.random.normal(ks[0], (BATCH, SEQ, D_MODEL), f32)
    start = jax.random.randint(ks[1], (BATCH, 1), 0, 4096, dtype=jnp.int32)
    positions = (start + jnp.arange(SEQ, dtype=jnp.int32)[None, :]).astype(jnp.int32)
    return {
        "x": x,
        "positions": positions,
        "norm_mix_g": gain(ks[2], (DEPTH, D_MODEL)),
        "w_in": nrm(ks[3], (DEPTH, D_MODEL, N_IN), D_MODEL),
        "ret_norm_g": gain(ks[4], (DEPTH, RET_V_W)),
        "w_ret_o": nrm(ks[5], (DEPTH, RET_V_W, D_MODEL), RET_V_W),
        "q_a_norm_g": gain(ks[6], (DEPTH, Q_LORA)),
        "w_q_b": nrm(ks[7], (DEPTH, Q_LORA, MLA_HEADS * MLA_QK_DIM), Q_LORA),
        "kv_a_norm_g": gain(ks[8], (DEPTH, KV_LORA)),
        "w_kv_b": nrm(ks[9], (DEPTH, KV_LORA, MLA_HEADS * (QK_NOPE + V_HEAD)), KV_LORA),
        "w_mla_o": nrm(ks[10], (DEPTH, MLA_V_W, D_MODEL), MLA_V_W),
        "w_out": nrm(ks[11], (DEPTH, D_MODEL, D_MODEL), D_MODEL),
        "norm_mlp_g": gain(ks[12], (DEPTH, D_MODEL)),
        "w_up": nrm(ks[13], (DEPTH, D_MODEL, D_FF), D_MODEL),
        "w_down": nrm(ks[14], (DEPTH, D_FF, D_MODEL), D_FF),
        "norm_f_g": gain(ks[15], (D_MODEL,)),
    }


def reference(x, positions, norm_mix_g, w_in, ret_norm_g, w_ret_o, q_a_norm_g, w_q_b,
              kv_a_norm_g, w_kv_b, w_mla_o, w_out, norm_mlp_g, w_up, w_down, norm_f_g):
    h = x
    for l in range(DEPTH):
        u = _rmsnorm(h, norm_mix_g[l])
        h = h + _mixer(u, positions, w_in[l], ret_norm_g[l], w_ret_o[l], q_a_norm_g[l],
                       w_q_b[l], kv_a_norm_g[l], w_kv_b[l], w_mla_o[l], w_out[l])
        z = _rmsnorm(h, norm_mlp_g[l]) @ w_up[l]
        h = h + jnp.square(jax.nn.relu(z)) @ w_down[l]
    return _rmsnorm(h, norm_f_g)
```

```python
import math
from contextlib import ExitStack

import numpy as np
import concourse.bass as bass
import concourse.mybir as mybir
from concourse.bass_utils import run_bass_kernel_spmd

F32 = mybir.dt.float32
BF16 = mybir.dt.bfloat16
I32 = mybir.dt.int32
AF = mybir.ActivationFunctionType
ALU = mybir.AluOpType

D = 2048
S_LEN = 4096
EPS = 1e-6
TWO_PI = 2.0 * math.pi
DEBUG2 = False
DBG_OUT = {}


class Buf:
    __slots__ = ("name", "w", "r", "dsem", "dcnt")

    def __init__(self, name):
        self.name = name
        self.w = None
        self.r = []
        self.dsem = None
        self.dcnt = 0


class Sched:
    ENG = ("pe", "act", "dve", "pool", "sp")

    def __init__(self, nc, stack):
        self.nc = nc
        self.stack = stack
        self.prog = {e: [] for e in self.ENG}
        self.cnt = {e: 0 for e in self.ENG}
        self.sem = {e: stack.enter_context(nc.semaphore("s_" + e)) for e in self.ENG}
        self.waited = {e: {} for e in self.ENG}
        self.dbufs = []

    def _dsem(self, b):
        if b.dsem is None:
            b.dsem = self.stack.enter_context(self.nc.semaphore("d_" + b.name))
            self.dbufs.append(b)
        return b.dsem

    def _collect(self, E, reads, writes):
        waits = {}

        def need(dep):
            if dep is None:
                return
            if dep[0] == "eng":
                _, e, idx = dep
                if e == E and e == "pe":
                    return
                key = ("eng", e)
                sem = self.sem[e]
                val = idx
            else:
                _, sem, val, key = dep
            if self.waited[E].get(key, 0) >= val:
                return
            if key not in waits or waits[key][1] < val:
                waits[key] = (sem, val)

        for b in reads:
            need(b.w)
        for b in writes:
            need(b.w)
            for r in b.r:
                need(r)
        for key, (sem, val) in waits.items():
            self.waited[E][key] = val
            self.prog[E].append(lambda eng, sem=sem, val=val: eng.wait_ge(sem, val))

    def op(self, E, fn, reads=(), writes=()):
        self._collect(E, reads, writes)
        self.cnt[E] += 1
        idx = self.cnt[E]
        sem = self.sem[E]
        self.prog[E].append(lambda eng, fn=fn, sem=sem: fn(eng).then_inc(sem, 1))
        ev = ("eng", E, idx)
        for b in writes:
            b.w = ev
            b.r = []
        for b in reads:
            if b not in writes:
                b.r.append(ev)
        return ev

    def dma(self, Q, fn, reads=(), writes=(), sembuf=None):
        self._collect(Q, reads, writes)
        sb = sembuf if sembuf is not None else writes[0]
        sem = self._dsem(sb)
        sb.dcnt += 16
        val = sb.dcnt
        self.prog[Q].append(lambda eng, fn=fn, sem=sem: fn(eng).then_inc(sem, 16))
        ev = ("dma", sem, val, ("dma", sb.name))
        for b in writes:
            b.w = ev
            b.r = []
        for b in reads:
            if b not in writes:
                b.r.append(ev)
        return ev

    def cc(self, Q, fn, reads=(), writes=()):
        self._collect(Q, reads, writes)
        sb = writes[0]
        sem = self._dsem(sb)
        sb.dcnt += 1
        val = sb.dcnt
        self.prog[Q].append(lambda eng, fn=fn, sem=sem: fn(eng).then_inc(sem, 1))
        ev = ("dma", sem, val, ("dma", sb.name))
        for b in writes:
            b.w = ev
            b.r = []
        for b in reads:
            if b not in writes:
                b.r.append(ev)
        return ev

    def barrier(self):
        for E in self.ENG:
            for e in self.ENG:
                if e == E:
                    continue
                v = self.cnt[e]
                if v > 0 and self.waited[E].get(("eng", e), 0) < v:
                    self.waited[E][("eng", e)] = v
                    self.prog[E].append(lambda eng, sem=self.sem[e], val=v: eng.wait_ge(sem, val))
            for b in self.dbufs:
                key = ("dma", b.name)
                if b.dcnt > 0 and self.waited[E].get(key, 0) < b.dcnt:
                    self.waited[E][key] = b.dcnt
                    self.prog[E].append(lambda eng, sem=b.dsem, val=b.dcnt: eng.wait_ge(sem, val))

    def wait_all(self, E, bufs):
        self._collect(E, bufs, ())

    def emit(self):
        nc = self.nc
        with nc.Block() as block:
            @block.tensor
            def _(eng):
                for f in self.prog["pe"]:
                    f(eng)

            @block.scalar
            def _(eng):
                for f in self.prog["act"]:
                    f(eng)

            @block.vector
            def _(eng):
                for f in self.prog["dve"]:
                    f(eng)

            @block.gpsimd
            def _(eng):
                for f in self.prog["pool"]:
                    f(eng)

            @block.sync
            def _(eng):
                for f in self.prog["sp"]:
                    f(eng)


class Ctx:
    def __init__(self, nc, st):
        self.nc = nc
        self.st = st
        self.S = Sched(nc, st)
        self.nb = 0
        self.pfx = ""

    def dram(self, name, shape, dt, kind):
        return self.nc.dram_tensor(name, list(shape), dt, kind=kind).ap()

    def sb(self, stack, name, shape, dt):
        return stack.enter_context(self.nc.sbuf_tensor(self.pfx + name, list(shape), dt))

    def ps(self, stack, name, shape, dt=F32):
        return stack.enter_context(self.nc.psum_tensor(self.pfx + name, list(shape), dt))

    def buf(self, name):
        self.nb += 1
        return Buf(f"{name}_{self.nb}")


def mm_group(S, ps_ap, pairs, reads, writes):
    n = len(pairs)

    def fn(e):
        ins = None
        for i, (l, r) in enumerate(pairs):
            ins = e.matmul(ps_ap, lhsT=l, rhs=r, start=(i == 0), stop=(i == n - 1))
        return ins
    S.op("pe", fn, reads=reads, writes=writes)


def rope_tables(C, S, posf, b_posf, inv_col, sign_col, negpi, tmp, b_tmp, cos_out, sin_out, b_tab, P, N):
    v, vi, vf = tmp
    for which, off, dst in (("sin", 0.5, sin_out), ("cos", 0.75, cos_out)):
        S.op("dve", lambda e, off=off: e.tensor_scalar(out=v, in0=posf, scalar1=inv_col, scalar2=off, op0=ALU.mult, op1=ALU.add),
             reads=[b_posf], writes=[b_tmp])
        S.op("dve", lambda e: e.tensor_copy(out=vi, in_=v), reads=[b_tmp], writes=[b_tmp])
        S.op("dve", lambda e: e.tensor_copy(out=vf, in_=vi), reads=[b_tmp], writes=[b_tmp])
        S.op("dve", lambda e: e.tensor_sub(out=v, in0=v, in1=vf), reads=[b_tmp], writes=[b_tmp])
        S.op("dve", lambda e: e.scalar_tensor_tensor(out=vf, in0=v, scalar=0.0, in1=v, op0=ALU.is_lt, op1=ALU.add),
             reads=[b_tmp], writes=[b_tmp])
        S.op("dve", lambda e: e.tensor_scalar(out=vf, in0=vf, scalar1=0.0, scalar2=1.0, op0=ALU.max, op1=ALU.min),
             reads=[b_tmp], writes=[b_tmp])
        if which == "sin":
            S.op("act", lambda e, dst=dst: e.activation(out=dst, in_=vf, func=AF.Sin, bias=negpi, scale=TWO_PI),
                 reads=[b_tmp], writes=[b_tab])
            S.op("dve", lambda e, dst=dst: e.tensor_scalar(out=dst, in0=dst, scalar1=sign_col, scalar2=None, op0=ALU.mult),
                 reads=[b_tab], writes=[b_tab])
        else:
            S.op("act", lambda e, dst=dst: e.activation(out=dst, in_=vf, func=AF.Sin, bias=negpi, scale=TWO_PI),
                 reads=[b_tmp], writes=[b_tab])


def make_ut(C, S, xT_d, tb, NT, xt, b_xt, sq, b_sq, ps_ss, b_ps_ss, ones, b_const, epsc, std, rstd, b_rstd, gcol, ut, b_ut, inv_n, xw=()):
    S.dma("sp", lambda e: e.dma_start(out=xt, in_=xT_d.rearrange("(kc p) t -> p kc t", p=128)[:, :, tb * NT:(tb + 1) * NT]),
          writes=[b_xt] + list(xw))
    for q in range(4):
        S.op("act", lambda e, q=q: e.activation(out=sq[q % 2][:], in_=xt[:, 4 * q:4 * q + 4, :], func=AF.Square),
             reads=[b_xt], writes=[b_sq[q % 2]])

        def fn(e, q=q):
            ins = None
            for i in range(4):
                ins = e.matmul(ps_ss, lhsT=ones, rhs=sq[q % 2][:, i, :], start=(q == 0 and i == 0), stop=(q == 3 and i == 3))
            return ins
        S.op("pe", fn, reads=[b_sq[q % 2]] + b_const, writes=[b_ps_ss])
    S.op("act", lambda e: e.activation(out=std, in_=ps_ss, func=AF.Sqrt, bias=epsc, scale=inv_n), reads=[b_ps_ss] + b_const, writes=[b_rstd])
    S.op("dve", lambda e: e.reciprocal(out=rstd, in_=std), reads=[b_rstd], writes=[b_rstd])
    for kc in range(16):
        S.op("dve", lambda e, kc=kc: e.scalar_tensor_tensor(out=ut[:, kc, :], in0=xt[:, kc, :], scalar=gcol[:, kc:kc + 1], in1=rstd,
                                                            op0=ALU.mult, op1=ALU.mult),
             reads=[b_xt, b_rstd] + b_const, writes=[b_ut])


def _p1(nc, C, fused):
    S = C.S
    ex = {}
    with ExitStack() as st:
        NT = 512
        xT_d = C.dram("xT", [D, S_LEN], F32, "ExternalInput")
        pos_d = C.dram("pos", [128, S_LEN], I32, "ExternalInput")
        gmix_d = C.dram("gmix", [128, 16], F32, "ExternalInput")
        wra_d = C.dram("wra", [8, 128, 16 * 128], F32, "ExternalInput")
        wrb_d = C.dram("wrb", [2, 128, 16 * 512], F32, "ExternalInput")
        wl_d = C.dram("wl", [9, 128, 16 * 128], F32, "ExternalInput")
        wq_d = C.dram("wq", [512, 1024], F32, "ExternalInput")
        wkv_d = C.dram("wkv", [512, 1024], F32, "ExternalInput")
        cols_d = C.dram("cols", [128, 32], F32, "ExternalInput")
        ident_d = C.dram("ident", [128, 128], F32, "ExternalInput")
        mask2_d = C.dram("mask2", [128, 2, 128], F32, "ExternalInput")
        qdec_d = C.dram("qdec", [128, 2, 512], F32, "ExternalInput")
        gret_d = C.dram("gret", [128, 512], F32, "ExternalInput")
        if not fused:
            ryT_d = C.dram("ryT", [512, S_LEN], F32, "ExternalOutput")
            myT_d = C.dram("myT", [512, S_LEN], F32, "ExternalOutput")
        else:
            for nm in ("R", "M"):
                ex[nm + "src"] = [nc.dram_tensor(f"ex{nm}src{i}", [512, 512], BF16).ap() for i in range(8)]
                ex[nm + "dst"] = [nc.dram_tensor(f"ex{nm}dst{i}", [2048, 512], BF16).ap() for i in range(8)]
                ex["b" + nm + "src"] = [C.buf(f"ex{nm}src{i}") for i in range(8)]
                ex["b" + nm + "dst"] = [C.buf(f"ex{nm}dst{i}") for i in range(8)]
        XDT = BF16 if fused else F32
        GROUPS = [[0, 1, 2, 3], [4, 5, 6, 7]]

        def gather(nm, sidx):
            S.cc("pool", lambda e: e.collective_compute("AllGather", ALU.bypass, replica_groups=GROUPS,
                                                        ins=[ex[nm + "src"][sidx].opt()], outs=[ex[nm + "dst"][sidx].opt()]),
                 reads=[ex["b" + nm + "src"][sidx]], writes=[ex["b" + nm + "dst"][sidx]])

        cols = C.sb(st, "cols_s", [128, 32], F32)
        gmix = C.sb(st, "gmixs", [128, 16], F32)
        ident = C.sb(st, "idents", [128, 128], BF16)
        ones = C.sb(st, "oness", [128, 128], BF16)
        b_const = C.buf("const")
        S.dma("sp", lambda e: e.dma_start(out=cols[:], in_=cols_d), writes=[b_const])
        S.dma("sp", lambda e: e.dma_start(out=gmix[:], in_=gmix_d), writes=[b_const])
        S.dma("pool", lambda e: e.dma_start(out=ident[:], in_=ident_d), writes=[b_const])
        b_ones = C.buf("ones")
        S.op("dve", lambda e: e.memset(ones[:], 1.0), writes=[b_ones])
        inv128, sign128, inv64, sign64 = cols[:, 0:1], cols[:, 1:2], cols[:, 2:3], cols[:, 3:4]
        negpi, epsc = cols[:, 4:5], cols[:, 5:6]

        psA = [C.ps(st, f"psA{i}", [128, 512]) for i in range(7)]
        psB = C.ps(st, "psB", [128, 1024], BF16)
        b_psA = [C.buf(f"psA{i}") for i in range(7)]
        b_psB = C.buf("psB")

        def _phase1():
            with ExitStack() as sr:
                wra = C.sb(sr, "wra_s", [128, 8, 16, 128], BF16)
                wrb = C.sb(sr, "wrb_s", [128, 2, 16, 512], BF16)
                xt = C.sb(sr, "xt", [128, 16, NT], F32)
                ut = C.sb(sr, "ut", [128, 16, NT], BF16)
                sq = [C.sb(sr, f"sq{i}", [128, 4, NT], BF16) for i in range(2)]
                std = C.sb(sr, "std", [128, NT], F32)
                rstd = C.sb(sr, "rstd", [128, NT], F32)
                posi = C.sb(sr, "posi", [128, NT], I32)
                posf = C.sb(sr, "posf", [128, NT], F32)
                tv = C.sb(sr, "tv", [128, NT], F32)
                tvi = C.sb(sr, "tvi", [128, NT], I32)
                tvf = C.sb(sr, "tvf", [128, NT], F32)
                cosT = C.sb(sr, "cosT", [128, NT], F32)
                sinT = C.sb(sr, "sinT", [128, NT], F32)
                t1 = C.sb(sr, "t1", [128, NT], F32)
                t2 = C.sb(sr, "t2", [128, NT], F32)
                qT = C.sb(sr, "qT", [128, 2, NT], BF16)
                qdT = C.sb(sr, "qdT", [128, 2, NT], BF16)
                kT = C.sb(sr, "kT", [128, 2, NT], BF16)
                kd = C.sb(sr, "kd", [128, 4, 2, 128], BF16)
                vsb = C.sb(sr, "vsb", [128, 4, 512], BF16)
                sg = C.sb(sr, "sg", [128, 4, 512], BF16)
                mask2 = C.sb(sr, "mask2s", [128, 2, 128], F32)
                qdec = C.sb(sr, "qdecs", [128, 2, 512], F32)
                gret = C.sb(sr, "grets", [128, 512], F32)
                S32 = C.sb(sr, "S32", [128, 2, 256], F32)
                Sbf = [C.sb(sr, f"Sbf{i}", [128, 2, 256], BF16) for i in range(2)]
                sTm = [C.sb(sr, f"sTm{i}", [128, 128], BF16) for i in range(2)]
                stats = C.sb(sr, "stats", [128, 2, 6], F32)
                mv = C.sb(sr, "mv", [128, 2, 2], F32)
                sd = C.sb(sr, "sd", [128, 2], F32)
                rs = C.sb(sr, "rs", [128, 2], F32)
                nbias = C.sb(sr, "nbias", [128, 2], F32)
                on = C.sb(sr, "on", [128, 512], F32)
                rytm = C.sb(sr, "rytm", [128, 512], BF16)
                ryTs = C.sb(sr, "ryTs", [128, 4, NT], XDT)

                b_wra, b_wrb = C.buf("wra"), C.buf("wrb")
                b_xt, b_ut, b_rstd = C.buf("xt"), C.buf("ut"), C.buf("rstd")
                b_sq = [C.buf("sq0"), C.buf("sq1")]
                b_posf, b_tmp, b_tab = C.buf("posf"), C.buf("tmp"), C.buf("tab")
                b_t1, b_t2 = C.buf("t1"), C.buf("t2")
                b_qT, b_kT, b_kd, b_v, b_sg = C.buf("qT"), C.buf("kT"), C.buf("kd"), C.buf("v"), C.buf("sg")
                b_S32 = C.buf("S32")
                b_Sbf = [C.buf("Sbf0"), C.buf("Sbf1")]
                b_sTm = [C.buf("sTm0"), C.buf("sTm1")]
                b_st, b_on, b_rytm, b_ryTs = C.buf("stats"), C.buf("on"), C.buf("rytm"), C.buf("ryTs")
                b_ryout = C.buf("ryout")

                for ch in range(8):
                    S.dma("pool", lambda e, ch=ch: e.dma_start(out=wra[:, ch, :, :], in_=wra_d[ch].rearrange("p (k n) -> p k n", k=16)), writes=[b_wra])
                for j in range(2):
                    for hk in range(2):
                        S.dma("pool", lambda e, j=j, hk=hk: e.dma_start(out=wrb[:, j, 8 * hk:8 * hk + 8, :],
                                                                      in_=wrb_d[j].rearrange("p (k n) -> p k n", k=16)[:, 8 * hk:8 * hk + 8, :]),
                              writes=[b_wrb])
                S.dma("sp", lambda e: e.dma_start(out=mask2[:], in_=mask2_d), writes=[b_const])
                S.dma("sp", lambda e: e.dma_start(out=qdec[:], in_=qdec_d), writes=[b_const])
                S.dma("sp", lambda e: e.dma_start(out=gret[:], in_=gret_d), writes=[b_const])
                S.op("dve", lambda e: e.memset(S32[:], 0.0), writes=[b_S32])
                S.op("dve", lambda e: e.memset(Sbf[0][:], 0.0), writes=[b_Sbf[0]])

                pp = 0
                for tb in range(8):
                    make_ut(C, S, xT_d, tb, NT, xt[:], b_xt, sq, b_sq, psA[6][:], b_psA[6], ones[:], [b_const, b_ones], epsc, std[:], rstd[:], b_rstd,
                            gmix, ut, b_ut, 1.0 / D)
                    S.dma("sp", lambda e, tb=tb: e.dma_start(out=posi[:], in_=pos_d[:, tb * NT:(tb + 1) * NT]), writes=[b_posf])
                    S.op("dve", lambda e: e.tensor_copy(out=posf[:], in_=posi[:]), reads=[b_posf], writes=[b_posf])
                    rope_tables(C, S, posf[:], b_posf, inv128, sign128, negpi, (tv[:], tvi[:], tvf[:]), b_tmp, cosT[:], sinT[:], b_tab, 128, NT)
                    for hh in range(2):
                        for qk in range(2):
                            ch = hh * 4 + qk * 2
                            pa, pb = pp % 4, (pp + 1) % 4
                            pp += 2
                            mm_group(S, psA[pa][:], [(wra[:, ch, kc, :], ut[:, kc, :]) for kc in range(16)], [b_wra, b_ut], [b_psA[pa]])
                            mm_group(S, psA[pb][:], [(wra[:, ch + 1, kc, :], ut[:, kc, :]) for kc in range(16)], [b_wra, b_ut], [b_psA[pb]])
                            S.op("dve", lambda e, pa=pa: e.tensor_tensor(out=t1[:], in0=psA[pa][:], in1=cosT[:], op=ALU.mult),
                                 reads=[b_psA[pa], b_tab], writes=[b_t1])
                            S.op("dve", lambda e, pb=pb: e.tensor_tensor(out=t2[:], in0=psA[pb][:], in1=sinT[:], op=ALU.mult),
                                 reads=[b_psA[pb], b_tab], writes=[b_t2])
                            if qk == 0:
                                S.op("pool", lambda e: e.tensor_tensor(out=t1[:], in0=t1[:], in1=t2[:], op=ALU.add), reads=[b_t1, b_t2], writes=[b_t1])
                                S.op("pool", lambda e, hh=hh: e.tensor_copy(out=qT[:, hh, :], in_=t1[:]), reads=[b_t1], writes=[b_qT])
                                S.op("pool", lambda e, hh=hh: e.tensor_tensor(out=qdT[:, hh, :], in0=t1[:], in1=qdec[:, hh, :], op=ALU.mult),
                                     reads=[b_t1, b_const], writes=[b_qT])
                            else:
                                S.op("pool", lambda e, hh=hh: e.tensor_tensor(out=kT[:, hh, :], in0=t1[:], in1=t2[:], op=ALU.add),
                                     reads=[b_t1, b_t2], writes=[b_kT])
                    for tt in range(4):
                        pa, pb = pp % 4, (pp + 1) % 4
                        pp += 2
                        mm_group(S, psA[pa][:], [(ut[:, kc, tt * 128:(tt + 1) * 128], wrb[:, 0, kc, :]) for kc in range(16)], [b_wrb, b_ut], [b_psA[pa]])
                        mm_group(S, psA[pb][:], [(ut[:, kc, tt * 128:(tt + 1) * 128], wrb[:, 1, kc, :]) for kc in range(16)], [b_wrb, b_ut], [b_psA[pb]])
                        S.op("act", lambda e, pa=pa, tt=tt: e.activation(out=vsb[:, tt, :], in_=psA[pa][:], func=AF.Copy), reads=[b_psA[pa]], writes=[b_v])
                        S.op("act", lambda e, pb=pb, tt=tt: e.activation(out=sg[:, tt, :], in_=psA[pb][:], func=AF.Silu), reads=[b_psA[pb]], writes=[b_sg])
                    for tt in range(4):
                        for hh in range(2):
                            S.op("pe", lambda e, tt=tt, hh=hh: e.transpose(psB[:, 0:128], kT[:, hh, tt * 128:(tt + 1) * 128], ident[:]),
                                 reads=[b_kT, b_const], writes=[b_psB])
                            S.op("dve", lambda e, tt=tt, hh=hh: e.tensor_scalar(out=kd[:, tt, hh, :], in0=psB[:, 0:128], scalar1=cols[:, 6 + hh:7 + hh],
                                                                                  scalar2=None, op0=ALU.mult),
                                 reads=[b_psB, b_const], writes=[b_kd])
                    for tt in range(4):
                        T = tb * 4 + tt
                        cur, nxt = T % 2, (T + 1) % 2
                        tsl = slice(tt * 128, (tt + 1) * 128)
                        for hh in range(2):
                            vsl = slice(hh * 256, (hh + 1) * 256)
                            si = hh
                            mm_group(S, psA[4][:, 0:128], [(kT[:, hh, tsl], qT[:, hh, tsl])], [b_kT, b_qT], [b_psA[4]])
                            S.op("dve", lambda e, hh=hh, si=si: e.tensor_tensor(out=sTm[si][:], in0=psA[4][:, 0:128], in1=mask2[:, hh, :], op=ALU.mult),
                                 reads=[b_psA[4], b_const], writes=[b_sTm[si]])
                            mm_group(S, psA[5][:, vsl], [(sTm[si][:], vsb[:, tt, vsl]), (qdT[:, hh, tsl], Sbf[cur][:, hh, :])],
                                     [b_sTm[si], b_v, b_qT, b_Sbf[cur]], [b_psA[5]])
                            mm_group(S, psA[4][:, 256:512], [(kd[:, tt, hh, :], vsb[:, tt, vsl])], [b_kd, b_v], [b_psA[4]])
                            S.op("dve", lambda e, hh=hh: e.scalar_tensor_tensor(out=S32[:, hh, :], in0=S32[:, hh, :], scalar=cols[:, 8 + hh:9 + hh],
                                                                                 in1=psA[4][:, 256:512], op0=ALU.mult, op1=ALU.add),
                                 reads=[b_psA[4], b_const], writes=[b_S32])
                            S.op("act", lambda e, hh=hh, nxt=nxt: e.activation(out=Sbf[nxt][:, hh, :], in_=S32[:, hh, :], func=AF.Copy),
                                 reads=[b_S32], writes=[b_Sbf[nxt]])
                            S.op("dve", lambda e, hh=hh, vsl=vsl: e.bn_stats(out=stats[:, hh, :], in_=psA[5][:, vsl]), reads=[b_psA[5]], writes=[b_st])
                            S.op("dve", lambda e, hh=hh: e.bn_aggr(out=mv[:, hh, :], in_=stats[:, hh, :]), reads=[b_st], writes=[b_st])
                            S.op("act", lambda e, hh=hh: e.activation(out=sd[:, hh:hh + 1], in_=mv[:, hh, 1:2], func=AF.Sqrt, bias=epsc, scale=1.0),
                                 reads=[b_st, b_const], writes=[b_st])
                            S.op("dve", lambda e, hh=hh: e.reciprocal(out=rs[:, hh:hh + 1], in_=sd[:, hh:hh + 1]), reads=[b_st], writes=[b_st])
                            S.op("dve", lambda e, hh=hh: e.tensor_scalar(out=nbias[:, hh:hh + 1], in0=mv[:, hh, 0:1], scalar1=rs[:, hh:hh + 1], scalar2=-1.0,
                                                                          op0=ALU.mult, op1=ALU.mult), reads=[b_st], writes=[b_st])
                            S.op("act", lambda e, hh=hh, vsl=vsl: e.activation(out=on[:, vsl], in_=psA[5][:, vsl], func=AF.Identity,
                                                                                bias=nbias[:, hh:hh + 1], scale=rs[:, hh:hh + 1]),
                                 reads=[b_psA[5], b_st], writes=[b_on])
                        S.op("dve", lambda e: e.tensor_tensor(out=on[:], in0=on[:], in1=gret[:], op=ALU.mult), reads=[b_on, b_const], writes=[b_on])
                        S.op("pool", lambda e, tt=tt: e.tensor_tensor(out=rytm[:], in0=on[:], in1=sg[:, tt, :], op=ALU.mult), reads=[b_on, b_sg], writes=[b_rytm])
                        for fc in range(4):
                            S.op("pe", lambda e, fc=fc: e.transpose(psB[:, 128 + fc * 128:256 + fc * 128], rytm[:, fc * 128:(fc + 1) * 128], ident[:]),
                                 reads=[b_rytm, b_const], writes=[b_psB])
                        S.op("act", lambda e, tt=tt: e.activation(out=ryTs[:, :, tt * 128:(tt + 1) * 128],
                                                                   in_=psB[:, 128:640].rearrange("p (f t) -> p f t", f=4), func=AF.Copy),
                             reads=[b_psB], writes=[b_ryTs])
                    if not fused:
                        S.dma("sp", lambda e, tb=tb: e.dma_start(out=ryT_d.rearrange("(f p) t -> p f t", p=128)[:, :, tb * NT:(tb + 1) * NT], in_=ryTs[:]),
                              reads=[b_ryTs], writes=[b_ryout])
                    else:
                        qq_, s0 = tb // 2, (tb % 2) * 4
                        for si in range(4):
                            S.dma("sp", lambda e, si=si, qq_=qq_, s0=s0: e.dma_start(
                                out=ex["Rsrc"][s0 + si].rearrange("(f p) t -> p f t", p=128)[:, :, qq_ * 128:(qq_ + 1) * 128],
                                in_=ryTs[:, :, si * 128:(si + 1) * 128]), reads=[b_ryTs], writes=[ex["bRsrc"][s0 + si]])
                        if tb >= 6:
                            for si in range(4):
                                gather("R", s0 + si)
                S.barrier()

        _phase1()
        cqn = C.sb(st, "cqn", [128, 4, S_LEN], BF16)
        ckvn = C.sb(st, "ckvn", [128, 4, S_LEN], BF16)
        kpeT = C.sb(st, "kpeT", [64, S_LEN], BF16)
        cos64 = C.sb(st, "cos64", [64, S_LEN], BF16)
        sin64 = C.sb(st, "sin64", [64, S_LEN], BF16)
        b_cqn, b_ckvn, b_kpe, b_tab64 = C.buf("cqn"), C.buf("ckvn"), C.buf("kpe"), C.buf("tab64")

        def _phase2():
            with ExitStack() as sl:
                wl = C.sb(sl, "wl_s", [128, 9, 16, 128], BF16)
                xt = C.sb(sl, "xt2", [128, 16, NT], F32)
                ut = C.sb(sl, "ut2", [128, 16, NT], BF16)
                sq = [C.sb(sl, f"sq2{i}", [128, 4, NT], BF16) for i in range(2)]
                std = C.sb(sl, "std2", [128, NT], F32)
                rstd = C.sb(sl, "rstd2", [128, NT], F32)
                posi = C.sb(sl, "posi2", [64, NT], I32)
                posf = C.sb(sl, "posf2", [64, NT], F32)
                tv = C.sb(sl, "tv2", [64, NT], F32)
                tvi = C.sb(sl, "tvi2", [64, NT], I32)
                tvf = C.sb(sl, "tvf2", [64, NT], F32)
                c32 = C.sb(sl, "c32", [128, 4, NT], F32)
                csq = sq[0]
                std2 = std
                rstd2 = rstd
                t1 = tv
                t2 = tvf
                b_wl = C.buf("wl")
                b_xt, b_ut, b_rstd = C.buf("xt"), C.buf("ut"), C.buf("rstd")
                b_sq = [C.buf("sq0"), C.buf("sq1")]
                b_posf, b_tmp = C.buf("posf"), C.buf("tmp")
                b_c32, b_csq, b_rstd2 = C.buf("c32"), b_sq[0], b_rstd
                b_t1, b_t2 = b_tmp, b_tmp
                for ch in range(9):
                    S.dma("pool", lambda e, ch=ch: e.dma_start(out=wl[:, ch, :, :], in_=wl_d[ch].rearrange("p (k n) -> p k n", k=16)), writes=[b_wl])
                pp = 0
                for tb in range(8):
                    bsl = slice(tb * NT, (tb + 1) * NT)
                    make_ut(C, S, xT_d, tb, NT, xt[:], b_xt, sq, b_sq, psA[6][:], b_psA[6], ones[:], [b_const, b_ones], epsc, std[:], rstd[:], b_rstd,
                            gmix, ut, b_ut, 1.0 / D)
                    S.dma("sp", lambda e, bsl=bsl: e.dma_start(out=posi[:], in_=pos_d[0:64, bsl]), writes=[b_posf])
                    S.op("dve", lambda e: e.tensor_copy(out=posf[:], in_=posi[:]), reads=[b_posf], writes=[b_posf])
                    rope_tables(C, S, posf[:], b_posf, inv64[0:64, :], sign64[0:64, :], negpi[0:64, :], (tv[:], tvi[:], tvf[:]), b_tmp,
                                cos64[:, bsl], sin64[:, bsl], b_tab64, 64, NT)
                    for which in range(2):
                        dst, b_dst, gbase = (cqn, b_cqn, 10) if which == 0 else (ckvn, b_ckvn, 14)
                        for c4 in range(4):
                            ch = which * 4 + c4
                            pa = pp % 4
                            pp += 1
                            mm_group(S, psA[pa][:], [(wl[:, ch, kc, :], ut[:, kc, :]) for kc in range(16)], [b_wl, b_ut], [b_psA[pa]])
                            S.op("act", lambda e, pa=pa, c4=c4: e.activation(out=c32[:, c4, :], in_=psA[pa][:], func=AF.Copy), reads=[b_psA[pa]], writes=[b_c32])
                            S.op("pool", lambda e, c4=c4: e.tensor_tensor(out=csq[:, c4, :], in0=c32[:, c4, :], in1=c32[:, c4, :], op=ALU.mult),
                                 reads=[b_c32], writes=[b_csq])
                        mm_group(S, psA[5][:], [(ones[:], csq[:, c4, :]) for c4 in range(4)], [b_csq, b_ones], [b_psA[5]])
                        S.op("act", lambda e: e.activation(out=std2[:], in_=psA[5][:], func=AF.Sqrt, bias=epsc, scale=1.0 / 512), reads=[b_psA[5], b_const],
                             writes=[b_rstd2])
                        S.op("dve", lambda e: e.reciprocal(out=rstd2[:], in_=std2[:]), reads=[b_rstd2], writes=[b_rstd2])
                        for c4 in range(4):
                            S.op("dve", lambda e, c4=c4, dst=dst, gbase=gbase, bsl=bsl: e.scalar_tensor_tensor(
                                out=dst[:, c4, bsl], in0=c32[:, c4, :], scalar=cols[:, gbase + c4:gbase + c4 + 1], in1=rstd2[:], op0=ALU.mult, op1=ALU.mult),
                                reads=[b_c32, b_rstd2, b_const], writes=[b_dst])
                    mm_group(S, psA[4][0:64, :], [(wl[:, 8, kc, 0:64], ut[:, kc, :]) for kc in range(16)], [b_wl, b_ut], [b_psA[4]])
                    S.op("dve", lambda e, bsl=bsl: e.tensor_tensor(out=t1[:], in0=psA[4][0:64, :], in1=cos64[:, bsl], op=ALU.mult),
                         reads=[b_psA[4], b_tab64], writes=[b_t1])
                    mm_group(S, psA[4][0:64, :], [(wl[:, 8, kc, 64:128], ut[:, kc, :]) for kc in range(16)], [b_wl, b_ut], [b_psA[4]])
                    S.op("dve", lambda e, bsl=bsl: e.tensor_tensor(out=t2[:], in0=psA[4][0:64, :], in1=sin64[:, bsl], op=ALU.mult),
                         reads=[b_psA[4], b_tab64], writes=[b_t2])
                    S.op("pool", lambda e, bsl=bsl: e.tensor_tensor(out=kpeT[:, bsl], in0=t1[:], in1=t2[:], op=ALU.add), reads=[b_t1, b_t2], writes=[b_kpe])
                S.barrier()

        _phase2()
        def _phase3():
            with ExitStack() as sm:
                wq = C.sb(sm, "wq_s", [128, 4, 1024], BF16)
                wkv = C.sb(sm, "wkv_s", [128, 4, 1024], BF16)
                vall = C.sb(sm, "vall", [128, 32, 512], BF16)
                knT = [C.sb(sm, f"knT{i}", [128, S_LEN], BF16) for i in range(2)]
                qn = [C.sb(sm, f"qn{i}", [128, 512], BF16) for i in range(2)]
                qp = [C.sb(sm, f"qp{i}", [64, 512], BF16) for i in range(2)]
                t1 = C.sb(sm, "t1c", [64, 512], F32)
                t2 = C.sb(sm, "t2c", [64, 512], F32)
                pT = [C.sb(sm, f"pT{i}", [128, 512], BF16) for i in range(3)]
                rec = C.sb(sm, "rec", [128, 512], F32)
                stage = [C.sb(sm, f"stage{i}", [128, 512], XDT) for i in range(2)]
                b_wq, b_wkv, b_vall = C.buf("wq"), C.buf("wkv"), C.buf("vall")
                b_knT = [C.buf("knT0"), C.buf("knT1")]
                b_qn = [C.buf("qn0"), C.buf("qn1")]
                b_qp = [C.buf("qp0"), C.buf("qp1")]
                b_t1, b_t2, b_rec = C.buf("t1"), C.buf("t2"), C.buf("rec")
                b_pT = [C.buf("pT0"), C.buf("pT1"), C.buf("pT2")]
                b_stage = [C.buf("stage0"), C.buf("stage1")]
                b_myout = C.buf("myout")
                S.dma("pool", lambda e: e.dma_start(out=wq[:], in_=wq_d.rearrange("(k p) n -> p k n", p=128)), writes=[b_wq])
                S.dma("pool", lambda e: e.dma_start(out=wkv[:], in_=wkv_d.rearrange("(k p) n -> p k n", p=128)), writes=[b_wkv])
                for T in range(32):
                    pa = T % 2
                    mm_group(S, psA[pa][:], [(ckvn[:, kc, T * 128:(T + 1) * 128], wkv[:, kc, 512:1024]) for kc in range(4)], [b_ckvn, b_wkv], [b_psA[pa]])
                    S.op("act", lambda e, pa=pa, T=T: e.activation(out=vall[:, T, :], in_=psA[pa][:], func=AF.Copy), reads=[b_psA[pa]], writes=[b_vall])
                sc = 192.0 ** -0.5
                it = 0
                pi = 0
                for h in range(4):
                    kb_ = h % 2
                    for tb in range(8):
                        pa = tb % 2
                        mm_group(S, psA[pa][:], [(wkv[:, kc, h * 128:(h + 1) * 128], ckvn[:, kc, tb * 512:(tb + 1) * 512]) for kc in range(4)],
                                 [b_ckvn, b_wkv], [b_psA[pa]])
                        S.op("dve", lambda e, pa=pa, tb=tb, kb_=kb_: e.tensor_copy(out=knT[kb_][:, tb * 512:(tb + 1) * 512], in_=psA[pa][:]),
                             reads=[b_psA[pa]], writes=[b_knT[kb_]])
                    for qb in range(8):
                        qi = it % 2
                        it += 1
                        qsl = slice(qb * 512, (qb + 1) * 512)
                        mm_group(S, psA[2][:], [(wq[:, kc, h * 256:h * 256 + 128], cqn[:, kc, qsl]) for kc in range(4)], [b_cqn, b_wq], [b_psA[2]])
                        S.op("act", lambda e, qi=qi: e.activation(out=qn[qi][:], in_=psA[2][:], func=AF.Copy), reads=[b_psA[2]], writes=[b_qn[qi]])
                        mm_group(S, psA[3][0:64, :], [(wq[:, kc, h * 256 + 128:h * 256 + 192], cqn[:, kc, qsl]) for kc in range(4)], [b_cqn, b_wq], [b_psA[3]])
                        S.op("dve", lambda e, qsl=qsl: e.tensor_tensor(out=t1[:], in0=psA[3][0:64, :], in1=cos64[:, qsl], op=ALU.mult),
                             reads=[b_psA[3], b_tab64], writes=[b_t1])
                        mm_group(S, psA[3][0:64, :], [(wq[:, kc, h * 256 + 192:h * 256 + 256], cqn[:, kc, qsl]) for kc in range(4)], [b_cqn, b_wq], [b_psA[3]])
                        S.op("dve", lambda e, qsl=qsl: e.tensor_tensor(out=t2[:], in0=psA[3][0:64, :], in1=sin64[:, qsl], op=ALU.mult),
                             reads=[b_psA[3], b_tab64], writes=[b_t2])
                        S.op("pool", lambda e, qi=qi: e.tensor_tensor(out=qp[qi][:], in0=t1[:], in1=t2[:], op=ALU.add), reads=[b_t1, b_t2], writes=[b_qp[qi]])
                        nkb = 4 * qb + 4

                        def c0_of(kb, qb=qb):
                            j = kb - 4 * qb
                            return 128 * j if j > 0 else 0

                        def s_mm(kb, qi=qi, kb_=kb_):
                            c0 = c0_of(kb)
                            ksl = slice(kb * 128, (kb + 1) * 128)
                            ps = psA[4 + kb % 2]
                            mm_group(S, ps[:, c0:512], [(knT[kb_][:, ksl], qn[qi][:, c0:512]), (kpeT[:, ksl], qp[qi][:, c0:512])],
                                     [b_knT[kb_], b_qn[qi], b_kpe, b_qp[qi]], [b_psA[4 + kb % 2]])
                        s_mm(0)
                        for kb in range(nkb):
                            if kb + 1 < nkb:
                                s_mm(kb + 1)
                            c0 = c0_of(kb)
                            j = kb - 4 * qb
                            pidx = pi % 3
                            pi += 1
                            S.op("act", lambda e, kb=kb, c0=c0, pidx=pidx: e.activation(out=pT[pidx][:, c0:512], in_=psA[4 + kb % 2][:, c0:512], func=AF.Exp, scale=sc),
                                 reads=[b_psA[4 + kb % 2]], writes=[b_pT[pidx]])
                            if j >= 0:
                                S.op("pool", lambda e, c0=c0, pidx=pidx: e.memset(pT[pidx][64:128, c0:c0 + 64], 0.0), writes=[b_pT[pidx]])
                            first, last = (kb == 0), (kb == nkb - 1)

                            def pv(e, kb=kb, c0=c0, pidx=pidx, first=first, last=last, h=h):
                                e.matmul(psA[0][:, c0:512], lhsT=vall[:, kb, h * 128:(h + 1) * 128], rhs=pT[pidx][:, c0:512], start=first, stop=last)
                                return e.matmul(psA[1][:, c0:512], lhsT=ones[:], rhs=pT[pidx][:, c0:512], start=first, stop=last)
                            S.op("pe", pv, reads=[b_vall, b_pT[pidx], b_ones], writes=[b_psA[0], b_psA[1]])
                        si = (h * 8 + qb) % 2
                        S.op("dve", lambda e: e.reciprocal(out=rec[:], in_=psA[1][:]), reads=[b_psA[1]], writes=[b_rec])
                        S.op("dve", lambda e, si=si: e.tensor_tensor(out=stage[si][:], in0=psA[0][:], in1=rec[:], op=ALU.mult),
                             reads=[b_psA[0], b_rec], writes=[b_stage[si]])
                        if not fused:
                            S.dma("sp", lambda e, si=si, h=h, qsl=qsl: e.dma_start(out=myT_d[h * 128:(h + 1) * 128, qsl], in_=stage[si][:]),
                                  reads=[b_stage[si]], writes=[b_myout], sembuf=b_stage[si])
                        else:
                            qq_, s0 = qb // 2, (qb % 2) * 4
                            for sj in range(4):
                                S.dma("sp", lambda e, si=si, sj=sj, h=h, qq_=qq_, s0=s0: e.dma_start(
                                    out=ex["Msrc"][s0 + sj][h * 128:(h + 1) * 128, qq_ * 128:(qq_ + 1) * 128],
                                    in_=stage[si][:, sj * 128:(sj + 1) * 128]), reads=[b_stage[si]], writes=[ex["bMsrc"][s0 + sj]])
                            if h == 3 and qb >= 6:
                                for sj in range(4):
                                    gather("M", s0 + sj)
                S.barrier()
        _phase3()
    return ex


def build1():
    nc = bass.Bass("TRN2", target_bir_lowering=False)
    with ExitStack() as st0:
        C = Ctx(nc, st0)
        _p1(nc, C, False)
        C.S.emit()
    return nc


def _chunk_major(w):
    K, N = w.shape
    n = N // 128
    return np.ascontiguousarray(w.reshape(16, 128, n, 128).transpose(2, 1, 0, 3).reshape(n, 128, 16 * 128))


def _block_major(w, bw):
    K, N = w.shape
    n = N // bw
    return np.ascontiguousarray(w.reshape(16, 128, n, bw).transpose(2, 1, 0, 3).reshape(n, 128, 16 * bw))


def _swap_halves(w, d):
    K, N = w.shape
    w4 = w.reshape(K, N // d, 2, d // 2)
    return np.ascontiguousarray(w4[:, :, ::-1, :].reshape(K, N))


def _consts1(g):
    cols = np.zeros((128, 32), np.float64)
    p = np.arange(128)
    cols[:, 0] = (10000.0 ** (-(p % 64) / 64.0)) / (2 * np.pi)
    cols[:, 1] = np.where(p < 64, -1.0, 1.0)
    cols[:, 2] = (10000.0 ** (-(p % 32) / 32.0)) / (2 * np.pi)
    cols[:, 3] = np.where((p % 64) < 32, -1.0, 1.0)
    cols[:, 4] = -np.pi
    cols[:, 5] = EPS
    mask2 = np.zeros((128, 2, 128), np.float64)
    qdec = np.zeros((128, 2, 512), np.float64)
    for hh in range(2):
        h = 2 * g + hh
        gam = 1.0 - 2.0 ** (-5.0 - h)
        cols[:, 6 + hh] = gam ** (127 - p) * (128 ** -0.5)
        cols[:, 8 + hh] = gam ** 128
        m = p[:, None]
        n = p[None, :]
        same = (m // 64) == (n // 64)
        val = np.where(same, gam ** np.abs(n - m), np.where(m < n, gam ** (n - m).clip(0), 0.0))
        mask2[:, hh, :] = val * (128 ** -0.5)
        t = np.arange(512)
        qdec[:, hh, :] = (gam ** ((t % 128) + 1))[None, :]
    return cols, mask2.astype(np.float32), qdec.astype(np.float32)


def _launch1(inp, prep_only=False):
    x = inp["x"]
    pos = inp["positions"]
    w_in = inp["w_in"][0]
    w_q_b = inp["w_q_b"][0]
    w_kv_b = inp["w_kv_b"][0]
    in_maps = []
    xTs = [np.ascontiguousarray(x[b].T) for b in range(2)]
    for c in range(8):
        b, g = c // 4, c % 4
        cols, mask2, qdec = _consts1(g)
        cols[:, 10:14] = inp["q_a_norm_g"][0].reshape(4, 128).T
        cols[:, 14:18] = inp["kv_a_norm_g"][0].reshape(4, 128).T
        chunks = []
        for hh in range(2):
            h = 2 * g + hh
            wq_h = w_in[:, h * 128:(h + 1) * 128]
            wk_h = w_in[:, 1024 + h * 128:1024 + (h + 1) * 128]
            chunks += [wq_h, _swap_halves(wq_h, 128), wk_h, _swap_halves(wk_h, 128)]
        wra = _chunk_major(np.concatenate(chunks, axis=1))
        rv = w_in[:, 2048 + g * 512:2048 + (g + 1) * 512]
        rg = w_in[:, 4096 + g * 512:4096 + (g + 1) * 512]
        wrb = _block_major(np.concatenate([rv, rg], axis=1), 512)
        cq = w_in[:, 6144:6656]
        ckv = w_in[:, 6656:7168]
        kpe = w_in[:, 7168:7232]
        wl = _chunk_major(np.concatenate([cq, ckv, kpe, _swap_halves(kpe, 64)], axis=1))
        wq_parts = []
        for hm in range(4 * g, 4 * g + 4):
            qh = w_q_b[:, hm * 192:(hm + 1) * 192]
            wq_parts += [qh[:, 0:128], qh[:, 128:192], _swap_halves(qh[:, 128:192], 64)]
        wq = np.ascontiguousarray(np.concatenate(wq_parts, axis=1))
        kn = [w_kv_b[:, hm * 256:hm * 256 + 128] for hm in range(4 * g, 4 * g + 4)]
        vv = [w_kv_b[:, hm * 256 + 128:hm * 256 + 256] for hm in range(4 * g, 4 * g + 4)]
        wkv = np.ascontiguousarray(np.concatenate(kn + vv, axis=1))
        in_maps.append({
            "xT": xTs[b],
            "pos": np.ascontiguousarray(np.broadcast_to(pos[b].astype(np.int32)[None, :], (128, S_LEN))),
            "gmix": np.ascontiguousarray(inp["norm_mix_g"][0].reshape(16, 128).T),
            "wra": wra, "wrb": wrb, "wl": wl, "wq": wq, "wkv": wkv,
            "cols": cols.astype(np.float32),
            "ident": np.eye(128, dtype=np.float32),
            "mask2": mask2, "qdec": qdec,
            "gret": np.ascontiguousarray(np.broadcast_to(inp["ret_norm_g"][0][g * 512:(g + 1) * 512][None, :], (128, 512))),
        })
    if prep_only:
        return xTs, in_maps
    nc = build1()
    res = run_bass_kernel_spmd(nc, in_maps, core_ids=list(range(8)))
    ryT = [np.concatenate([res.results[b * 4 + g]["ryT"] for g in range(4)], axis=0) for b in range(2)]
    myT = [np.concatenate([res.results[b * 4 + g]["myT"] for g in range(4)], axis=0) for b in range(2)]
    return xTs, ryT, myT


def norm_from_sb(S, src, b_src, sq, b_sq, ps_ss, b_ps_ss, ones, b_cl, epsc, std, rstd, b_rstd, NT):
    for q in range(4):
        S.op("act", lambda e, q=q: e.activation(out=sq[q % 2][:], in_=src[:, 4 * q:4 * q + 4, :], func=AF.Square),
             reads=[b_src], writes=[b_sq[q % 2]])

        def fn(e, q=q):
            ins = None
            for i in range(4):
                ins = e.matmul(ps_ss, lhsT=ones, rhs=sq[q % 2][:, i, :], start=(q == 0 and i == 0), stop=(q == 3 and i == 3))
            return ins
        S.op("pe", fn, reads=[b_sq[q % 2]] + b_cl, writes=[b_ps_ss])
    S.op("act", lambda e: e.activation(out=std, in_=ps_ss, func=AF.Sqrt, bias=epsc, scale=1.0 / D), reads=[b_ps_ss] + b_cl, writes=[b_rstd])
    S.op("dve", lambda e: e.reciprocal(out=rstd, in_=std), reads=[b_rstd], writes=[b_rstd])


_JC = {}


def _jofs(e):
    if "v" not in _JC:
        _JC["v"] = (e.partition_id() % 4) * 128
    return _JC["v"]


def _p2(nc, C, ex):
    _JC.clear()
    S = C.S
    fused = ex is not None
    C.pfx = "p2_"
    with ExitStack() as st:
        NT = 512
        xT_d = C.dram("xTo" if fused else "xT", [D, 1024], F32, "ExternalInput")
        if not fused:
            ryT_d = C.dram("ryT", [D, 1024], F32, "ExternalInput")
            myT_d = C.dram("myT", [D, 1024], F32, "ExternalInput")
        gcols_d = C.dram("gcols", [128, 64], F32, "ExternalInput")
        wg_d = C.dram("wg", [32, 128, 2048], F32, "ExternalInput")
        wro_d = C.dram("wro", [16, 128, 2048], F32, "ExternalInput")
        wmo_d = C.dram("wmo", [16, 128, 2048], F32, "ExternalInput")
        wout_d = C.dram("wout", [16, 128, 2048], F32, "ExternalInput")
        wup_d = C.dram("wup", [64, 128, 2048], F32, "ExternalInput")
        wdn_d = C.dram("wdn", [16, 128, 8192], F32, "ExternalInput")
        outT_d = C.dram("outT", [D, 1024], F32, "ExternalOutput")
        if DEBUG2:
            hdbg_d = C.dram("hdbg", [D, 1024], F32, "ExternalOutput")
            hndbg_d = C.dram("hndbg", [D, 1024], BF16, "ExternalOutput")
            adbg_d = C.dram("adbg", [D, 1024], BF16, "ExternalOutput")
            b_dbg = C.buf("dbg")

        gcols = C.sb(st, "gcols_s", [128, 64], F32)
        ones = C.sb(st, "ones_s", [128, 128], BF16)
        b_const, b_ones = C.buf("const"), C.buf("ones")
        S.dma("sp", lambda e: e.dma_start(out=gcols[:], in_=gcols_d), writes=[b_const])
        S.op("dve", lambda e: e.memset(ones[:], 1.0), writes=[b_ones])
        b_cl = [b_const, b_ones]
        epsc = gcols[:, 48:49]
        ps = [C.ps(st, f"ps{i}", [128, 512]) for i in range(8)]
        b_ps = [C.buf(f"ps{i}") for i in range(8)]
        hT = C.sb(st, "hT", [128, 16, 1024], F32)
        b_hT = C.buf("hT")

        def phase_ac():
            with ExitStack() as sa:
                xt = C.sb(sa, "xt", [128, 16, NT], F32)
                ut = C.sb(sa, "ut", [128, 16, NT], BF16)
                sq = [C.sb(sa, f"sq{i}", [128, 4, NT], BF16) for i in range(2)]
                std = C.sb(sa, "std", [128, NT], F32)
                rstd = C.sb(sa, "rstd", [128, NT], F32)
                rys = C.sb(sa, "rys", [128, 4, 16, 128], BF16)
                mys = C.sb(sa, "mys", [128, 4, 16, 128], BF16)
                merged = C.sb(sa, "merged", [128, 16, NT], BF16)
                wB = [[C.sb(sa, f"wB{i}_{j}", [128, 16, 128], BF16) for j in range(4)] for i in range(2)]
                wC = [wB[0][0], wB[1][0]]
                s1 = xt[:, 0:1, :].rearrange("p a t -> p (a t)")
                s2 = xt[:, 1:2, :].rearrange("p a t -> p (a t)")
                m1 = s1
                m2 = s2
                xc = [xt[:, 2:3, :].rearrange("p a t -> p (a t)"), xt[:, 3:4, :].rearrange("p a t -> p (a t)")]
                b_xt, b_ut, b_rstd = C.buf("xt"), C.buf("ut"), C.buf("rstd")
                b_sq = [C.buf("sq0"), C.buf("sq1")]
                b_rys, b_mys, b_merged = C.buf("rys"), C.buf("mys"), C.buf("merged")
                b_wB = [[C.buf(f"wB{i}{j}") for j in range(4)] for i in range(2)]
                b_wC = [b_wB[0][0], b_wB[1][0]]
                b_s1, b_s2 = C.buf("s1"), C.buf("s2")
                b_m1, b_m2 = b_s1, b_s2
                b_xc = [C.buf("xc0"), C.buf("xc1")]
                for hf in range(2):
                    hsl = slice(hf * NT, (hf + 1) * NT)
                    make_ut(C, S, xT_d, hf, NT, xt[:], b_xt, sq, b_sq, ps[6][:], b_ps[6], ones[:], b_cl, epsc, std[:], rstd[:], b_rstd,
                            gcols, ut, b_ut, 1.0 / D, xw=[b_s1, b_s2, b_xc[0], b_xc[1]])
                    if not fused:
                        S.dma("pool", lambda e, hsl=hsl: e.dma_start(out=rys[:].rearrange("p s k t -> p k s t"), in_=ryT_d[:, hsl].rearrange("(k p) (s t) -> p k s t", p=128, s=4)), writes=[b_rys])
                        S.dma("pool", lambda e, hsl=hsl: e.dma_start(out=mys[:].rearrange("p s k t -> p k s t"), in_=myT_d[:, hsl].rearrange("(k p) (s t) -> p k s t", p=128, s=4)), writes=[b_mys])
                    else:
                        for nm, dstt, b_d in (("R", rys, b_rys), ("M", mys, b_mys)):
                            for si in range(4):
                                sidx = hf * 4 + si
                                S.dma("pool", lambda e, nm=nm, dstt=dstt, si=si, sidx=sidx: e.dma_start(
                                    out=dstt[:, si, :, :],
                                    in_=ex[nm + "dst"][sidx].rearrange("(k p) t -> p k t", p=128)[:, :, bass.ds(_jofs(e), 128)]),
                                    reads=[ex["b" + nm + "dst"][sidx]], writes=[b_d])
                    for oc in range(16):
                        r = oc % 2
                        srcs = (wg_d[oc], wg_d[16 + oc], wro_d[oc], wmo_d[oc])
                        for j in range(4):
                            S.dma("pool", lambda e, r=r, j=j, src=srcs[j]: e.dma_start(out=wB[r][j][:], in_=src.rearrange("p (k n) -> p k n", k=16)),
                                  writes=[b_wB[r][j]])
                        pb = 0 if r == 0 else 2
                        mm_group(S, ps[pb][:], [(wB[r][0][:, kc, :], ut[:, kc, :]) for kc in range(16)], [b_wB[r][0], b_ut], [b_ps[pb]])
                        S.op("act", lambda e, pb=pb: e.activation(out=s1, in_=ps[pb][:], func=AF.Sigmoid), reads=[b_ps[pb]], writes=[b_s1])
                        mm_group(S, ps[pb + 1][:], [(wB[r][1][:, kc, :], ut[:, kc, :]) for kc in range(16)], [b_wB[r][1], b_ut], [b_ps[pb + 1]])
                        S.op("act", lambda e, pb=pb: e.activation(out=s2, in_=ps[pb + 1][:], func=AF.Sigmoid), reads=[b_ps[pb + 1]], writes=[b_s2])
                        mm_group(S, ps[pb + 4][:].rearrange("p (s t) -> p s t", s=4), [(wB[r][2][:, kc, :], rys[:, :, kc, :]) for kc in range(16)], [b_wB[r][2], b_rys], [b_ps[pb + 4]])
                        S.op("dve", lambda e, pb=pb: e.tensor_tensor(out=m1, in0=ps[pb + 4][:], in1=s1, op=ALU.mult),
                             reads=[b_ps[pb + 4], b_s1], writes=[b_m1])
                        mm_group(S, ps[pb + 5][:].rearrange("p (s t) -> p s t", s=4), [(wB[r][3][:, kc, :], mys[:, :, kc, :]) for kc in range(16)], [b_wB[r][3], b_mys], [b_ps[pb + 5]])
                        S.op("dve", lambda e, pb=pb: e.tensor_tensor(out=m2, in0=ps[pb + 5][:], in1=s2, op=ALU.mult),
                             reads=[b_ps[pb + 5], b_s2], writes=[b_m2])
                        S.op("dve", lambda e, oc=oc: e.tensor_tensor(out=merged[:, oc, :], in0=m1, in1=m2, op=ALU.add),
                             reads=[b_m1, b_m2], writes=[b_merged])
                    for oc in range(16):
                        r = oc % 2
                        S.dma("pool", lambda e, r=r, oc=oc: e.dma_start(out=wC[r][:], in_=wout_d[oc].rearrange("p (k n) -> p k n", k=16)), writes=[b_wC[r]])
                        S.dma("sp", lambda e, r=r, oc=oc, hsl=hsl: e.dma_start(out=xc[r], in_=xT_d[oc * 128:(oc + 1) * 128, hsl]), writes=[b_xc[r], b_xt])
                        mm_group(S, ps[r][:], [(wC[r][:, kc, :], merged[:, kc, :]) for kc in range(16)], [b_wC[r], b_merged], [b_ps[r]])
                        S.op("dve", lambda e, r=r, oc=oc, hsl=hsl: e.tensor_tensor(out=hT[:, oc, hsl], in0=ps[r][:], in1=xc[r], op=ALU.add),
                             reads=[b_ps[r], b_xc[r]], writes=[b_hT])
                if DEBUG2:
                    S.dma("sp", lambda e: e.dma_start(out=hdbg_d.rearrange("(k p) t -> p k t", p=128), in_=hT[:]), reads=[b_hT], writes=[b_dbg])
                S.barrier()
        phase_ac()

        def phase_mlp():
            with ExitStack() as sa:
                hn = C.sb(sa, "hn", [128, 16, 1024], BF16)
                aT = C.sb(sa, "aT", [128, 16, 1024], BF16)
                sq = [C.sb(sa, f"sqm{i}", [128, 4, NT], BF16) for i in range(2)]
                std = C.sb(sa, "stdm", [128, NT], F32)
                rstd = C.sb(sa, "rstdm", [128, NT], F32)
                wU = [C.sb(sa, f"wU{i}", [128, 16, 128], BF16) for i in range(4)]
                wD = [C.sb(sa, f"wD{i}", [128, 16, 128], BF16) for i in range(3)]
                rl = [C.sb(sa, f"rl{i}", [128, NT], F32) for i in range(2)]
                b_hn, b_aT, b_rstd = C.buf("hn"), C.buf("aT"), C.buf("rstd")
                b_sq = [C.buf("sq0"), C.buf("sq1")]
                b_wU = [C.buf(f"wU{i}") for i in range(4)]
                b_wD = [C.buf(f"wD{i}") for i in range(3)]
                b_rl = [C.buf("rl0"), C.buf("rl1")]
                b_out = C.buf("out")
                for hf in range(2):
                    hsl = slice(hf * NT, (hf + 1) * NT)
                    norm_from_sb(S, hT[:, :, hsl], b_hT, sq, b_sq, ps[6][:], b_ps[6], ones[:], b_cl, epsc, std[:], rstd[:], b_rstd, NT)
                    for kc in range(16):
                        S.op("dve", lambda e, kc=kc, hsl=hsl: e.scalar_tensor_tensor(out=hn[:, kc, hsl], in0=hT[:, kc, hsl], scalar=gcols[:, 16 + kc:17 + kc],
                                                                                    in1=rstd[:], op0=ALU.mult, op1=ALU.mult),
                             reads=[b_hT, b_rstd, b_const], writes=[b_hn])
                ui = 0
                di = 0
                zi = 0
                for qq in range(4):
                    for fcl in range(16):
                        fc = qq * 16 + fcl
                        r = ui % 4
                        ui += 1
                        S.dma("pool", lambda e, r=r, fc=fc: e.dma_start(out=wU[r][:], in_=wup_d[fc].rearrange("p (k n) -> p k n", k=16)), writes=[b_wU[r]])
                        for hf in range(2):
                            hsl = slice(hf * NT, (hf + 1) * NT)
                            pz = zi % 4
                            zi += 1
                            mm_group(S, ps[pz][:], [(wU[r][:, kc, :], hn[:, kc, hsl]) for kc in range(16)], [b_wU[r], b_hn], [b_ps[pz]])
                            S.op("act", lambda e, pz=pz: e.activation(out=rl[pz % 2][:], in_=ps[pz][:], func=AF.Relu), reads=[b_ps[pz]], writes=[b_rl[pz % 2]])
                            S.op("dve", lambda e, pz=pz, fcl=fcl, hsl=hsl: e.tensor_tensor(out=aT[:, fcl, hsl], in0=rl[pz % 2][:], in1=rl[pz % 2][:], op=ALU.mult),
                                 reads=[b_rl[pz % 2]], writes=[b_aT])
                    for oc in range(16):
                        r = di % 3
                        di += 1
                        S.dma("pool", lambda e, r=r, oc=oc, qq=qq: e.dma_start(out=wD[r][:], in_=wdn_d[oc][:, qq * 2048:(qq + 1) * 2048].rearrange("p (k n) -> p k n", k=16)),
                              writes=[b_wD[r]])
                        for hf in range(2):
                            hsl = slice(hf * NT, (hf + 1) * NT)
                            pd = 4 + (oc * 2 + hf) % 2
                            mm_group(S, ps[pd][:], [(wD[r][:, fcl, :], aT[:, fcl, hsl]) for fcl in range(16)], [b_wD[r], b_aT], [b_ps[pd]])
                            S.op("dve", lambda e, pd=pd, oc=oc, hsl=hsl: e.tensor_tensor(out=hT[:, oc, hsl], in0=ps[pd][:], in1=hT[:, oc, hsl], op=ALU.add),
                                 reads=[b_ps[pd]], writes=[b_hT])
                if DEBUG2:
                    S.dma("sp", lambda e: e.dma_start(out=hndbg_d.rearrange("(k p) t -> p k t", p=128), in_=hn[:]), reads=[b_hn], writes=[b_dbg])
                    S.dma("sp", lambda e: e.dma_start(out=adbg_d.rearrange("(k p) t -> p k t", p=128), in_=aT[:]), reads=[b_aT], writes=[b_dbg])
                for hf in range(2):
                    hsl = slice(hf * NT, (hf + 1) * NT)
                    norm_from_sb(S, hT[:, :, hsl], b_hT, sq, b_sq, ps[6][:], b_ps[6], ones[:], b_cl, epsc, std[:], rstd[:], b_rstd, NT)
                    for kc in range(16):
                        S.op("dve", lambda e, kc=kc, hsl=hsl: e.scalar_tensor_tensor(out=hT[:, kc, hsl], in0=hT[:, kc, hsl], scalar=gcols[:, 32 + kc:33 + kc],
                                                                                    in1=rstd[:], op0=ALU.mult, op1=ALU.mult),
                             reads=[b_rstd, b_const], writes=[b_hT])
                for kq in range(4):
                    S.dma("sp", lambda e, kq=kq: e.dma_start(out=outT_d.rearrange("(k p) t -> p k t", p=128)[:, 4 * kq:4 * kq + 4, :], in_=hT[:, 4 * kq:4 * kq + 4, :]),
                          reads=[b_hT], writes=[b_out])
                S.barrier()
        phase_mlp()


def build2():
    nc = bass.Bass("TRN2", target_bir_lowering=False)
    with ExitStack() as st0:
        C = Ctx(nc, st0)
        _p2(nc, C, None)
        C.S.emit()
    return nc


def build_fused():
    nc = bass.Bass("TRN2", target_bir_lowering=False)
    with ExitStack() as st0:
        C = Ctx(nc, st0)
        ex = _p1(nc, C, True)
        _p2(nc, C, ex)
        C.S.emit()
    return nc


def _launch2(inp, xTs, ryT, myT, prep_only=False):
    w_in = inp["w_in"][0]
    gcols = np.zeros((128, 64), np.float32)
    gcols[:, 0:16] = inp["norm_mix_g"][0].reshape(16, 128).T
    gcols[:, 16:32] = inp["norm_mlp_g"][0].reshape(16, 128).T
    gcols[:, 32:48] = inp["norm_f_g"].reshape(16, 128).T
    gcols[:, 48] = EPS
    wg = _chunk_major(w_in[:, 7232:11328])
    wro = _chunk_major(inp["w_ret_o"][0])
    wmo = _chunk_major(inp["w_mla_o"][0])
    wout = _chunk_major(inp["w_out"][0])
    wup = _chunk_major(inp["w_up"][0])
    wd = inp["w_down"][0]
    wdn = np.ascontiguousarray(wd.reshape(64, 128, 16, 128).transpose(2, 1, 0, 3).reshape(16, 128, 8192))
    in_maps = []
    for c in range(8):
        b, j = c // 4, c % 4
        tsl = slice(j * 1024, (j + 1) * 1024)
        m2 = {"xTo": np.ascontiguousarray(xTs[b][:, tsl])} if prep_only else {
            "xT": np.ascontiguousarray(xTs[b][:, tsl]),
            "ryT": np.ascontiguousarray(ryT[b][:, tsl]),
            "myT": np.ascontiguousarray(myT[b][:, tsl])}
        in_maps.append({
            **m2,
            "gcols": gcols, "wg": wg, "wro": wro, "wmo": wmo, "wout": wout, "wup": wup, "wdn": wdn,
        })
    if prep_only:
        return in_maps
    nc = build2()
    res = run_bass_kernel_spmd(nc, in_maps, core_ids=list(range(8)))
    out = np.empty((2, S_LEN, D), np.float32)
    if DEBUG2:
        DBG_OUT.update({k: np.asarray(v) for k, v in res.results[0].items()})
    for c in range(8):
        b, j = c // 4, c % 4
        out[b, j * 1024:(j + 1) * 1024, :] = res.results[c]["outT"].T
    return out


FUSED = True


def kernel(**inp):
    inp = {k: np.asarray(v) for k, v in inp.items()}
    if not FUSED:
        xTs, ryT, myT = _launch1(inp)
        return _launch2(inp, xTs, ryT, myT)
    xTs, maps1 = _launch1(inp, prep_only=True)
    maps2 = _launch2(inp, xTs, None, None, prep_only=True)
    in_maps = [{**maps1[c], **maps2[c]} for c in range(8)]
    nc = build_fused()
    res = run_bass_kernel_spmd(nc, in_maps, core_ids=list(range(8)))
    out = np.empty((2, S_LEN, D), np.float32)
    for c in range(8):
        b, j = c // 4, c % 4
        out[b, j * 1024:(j + 1) * 1024, :] = res.results[c]["outT"].T
    return out
```

```python
import math
from contextlib import ExitStack

import numpy as np
import concourse.bass as bass
import concourse.mybir as mybir
from concourse.bass_utils import run_bass_kernel_spmd

F32 = mybir.dt.float32
BF16 = mybir.dt.bfloat16
I32 = mybir.dt.int32
AF = mybir.ActivationFunctionType
ALU = mybir.AluOpType

D = 2048
S_LEN = 4096
EPS = 1e-6
TWO_PI = 2.0 * math.pi
DEBUG2 = False
DBG_OUT = {}


class Buf:
    __slots__ = ("name", "w", "r", "dsem", "dcnt", "nobar")

    def __init__(self, name):
        self.name = name
        self.w = None
        self.r = []
        self.dsem = None
        self.dcnt = 0
        self.nobar = False


class Sched:
    ENG = ("pe", "act", "dve", "pool", "sp")

    def __init__(self, nc, stack):
        self.nc = nc
        self.stack = stack
        self.prog = {e: [] for e in self.ENG}
        self.cnt = {e: 0 for e in self.ENG}
        self.sem = {e: stack.enter_context(nc.semaphore("s_" + e)) for e in self.ENG}
        self.waited = {e: {} for e in self.ENG}
        self.dbufs = []

    def _dsem(self, b):
        if b.dsem is None:
            b.dsem = self.stack.enter_context(self.nc.semaphore("d_" + b.name))
            self.dbufs.append(b)
        return b.dsem

    def _collect(self, E, reads, writes):
        waits = {}

        def need(dep):
            if dep is None:
                return
            if dep[0] == "eng":
                _, e, idx = dep
                if e == E and e == "pe":
                    return
                key = ("eng", e)
                sem = self.sem[e]
                val = idx
            else:
                _, sem, val, key = dep
            if self.waited[E].get(key, 0) >= val:
                return
            if key not in waits or waits[key][1] < val:
                waits[key] = (sem, val)

        for b in reads:
            need(b.w)
        for b in writes:
            need(b.w)
            for r in b.r:
                need(r)
        for key, (sem, val) in waits.items():
            self.waited[E][key] = val
            self.prog[E].append(lambda eng, sem=sem, val=val: eng.wait_ge(sem, val))

    def op(self, E, fn, reads=(), writes=()):
        self._collect(E, reads, writes)
        self.cnt[E] += 1
        idx = self.cnt[E]
        sem = self.sem[E]
        self.prog[E].append(lambda eng, fn=fn, sem=sem: fn(eng).then_inc(sem, 1))
        ev = ("eng", E, idx)
        for b in writes:
            b.w = ev
            b.r = []
        for b in reads:
            if b not in writes:
                b.r.append(ev)
        return ev

    def dma(self, Q, fn, reads=(), writes=(), sembuf=None):
        self._collect(Q, reads, writes)
        sb = sembuf if sembuf is not None else writes[0]
        sem = self._dsem(sb)
        sb.dcnt += 16
        val = sb.dcnt
        self.prog[Q].append(lambda eng, fn=fn, sem=sem: fn(eng).then_inc(sem, 16))
        ev = ("dma", sem, val, ("dma", sb.name))
        for b in writes:
            b.w = ev
            b.r = []
        for b in reads:
            if b not in writes:
                b.r.append(ev)
        return ev

    def cc(self, Q, fn, reads=(), writes=()):
        self._collect(Q, reads, writes)
        sb = writes[0]
        sem = self._dsem(sb)
        sb.dcnt += 1
        val = sb.dcnt
        self.prog[Q].append(lambda eng, fn=fn, sem=sem: fn(eng).then_inc(sem, 1))
        ev = ("dma", sem, val, ("dma", sb.name))
        for b in writes:
            b.w = ev
            b.r = []
        for b in reads:
            if b not in writes:
                b.r.append(ev)
        return ev

    def barrier(self):
        for E in self.ENG:
            for e in self.ENG:
                if e == E:
                    continue
                v = self.cnt[e]
                if v > 0 and self.waited[E].get(("eng", e), 0) < v:
                    self.waited[E][("eng", e)] = v
                    self.prog[E].append(lambda eng, sem=self.sem[e], val=v: eng.wait_ge(sem, val))
            for b in self.dbufs:
                key = ("dma", b.name)
                if b.nobar:
                    continue
                if b.dcnt > 0 and self.waited[E].get(key, 0) < b.dcnt:
                    self.waited[E][key] = b.dcnt
                    self.prog[E].append(lambda eng, sem=b.dsem, val=b.dcnt: eng.wait_ge(sem, val))

    def wait_all(self, E, bufs):
        self._collect(E, bufs, ())

    def emit(self):
        nc = self.nc
        with nc.Block() as block:
            @block.tensor
            def _(eng):
                for f in self.prog["pe"]:
                    f(eng)

            @block.scalar
            def _(eng):
                for f in self.prog["act"]:
                    f(eng)

            @block.vector
            def _(eng):
                for f in self.prog["dve"]:
                    f(eng)

            @block.gpsimd
            def _(eng):
                for f in self.prog["pool"]:
                    f(eng)

            @block.sync
            def _(eng):
                for f in self.prog["sp"]:
                    f(eng)


class Ctx:
    def __init__(self, nc, st):
        self.nc = nc
        self.st = st
        self.S = Sched(nc, st)
        self.nb = 0
        self.pfx = ""

    def dram(self, name, shape, dt, kind):
        return self.nc.dram_tensor(name, list(shape), dt, kind=kind).ap()

    def sb(self, stack, name, shape, dt):
        return stack.enter_context(self.nc.sbuf_tensor(self.pfx + name, list(shape), dt))

    def ps(self, stack, name, shape, dt=F32):
        return stack.enter_context(self.nc.psum_tensor(self.pfx + name, list(shape), dt))

    def buf(self, name):
        self.nb += 1
        return Buf(f"{name}_{self.nb}")


def mm_group(S, ps_ap, pairs, reads, writes):
    n = len(pairs)

    def fn(e):
        ins = None
        for i, (l, r) in enumerate(pairs):
            ins = e.matmul(ps_ap, lhsT=l, rhs=r, start=(i == 0), stop=(i == n - 1))
        return ins
    S.op("pe", fn, reads=reads, writes=writes)


def rope_tables(C, S, posf, b_posf, inv_col, sign_col, negpi, tmp, b_tmp, cos_out, sin_out, b_tab, P, N):
    v, vi, vf = tmp
    for which, off, dst in (("sin", 0.5, sin_out), ("cos", 0.75, cos_out)):
        S.op("dve", lambda e, off=off: e.tensor_scalar(out=v, in0=posf, scalar1=inv_col, scalar2=off, op0=ALU.mult, op1=ALU.add),
             reads=[b_posf], writes=[b_tmp])
        S.op("dve", lambda e: e.tensor_copy(out=vi, in_=v), reads=[b_tmp], writes=[b_tmp])
        S.op("dve", lambda e: e.tensor_copy(out=vf, in_=vi), reads=[b_tmp], writes=[b_tmp])
        S.op("dve", lambda e: e.tensor_sub(out=v, in0=v, in1=vf), reads=[b_tmp], writes=[b_tmp])
        S.op("dve", lambda e: e.scalar_tensor_tensor(out=vf, in0=v, scalar=0.0, in1=v, op0=ALU.is_lt, op1=ALU.add),
             reads=[b_tmp], writes=[b_tmp])
        S.op("dve", lambda e: e.tensor_scalar(out=vf, in0=vf, scalar1=0.0, scalar2=1.0, op0=ALU.max, op1=ALU.min),
             reads=[b_tmp], writes=[b_tmp])
        if which == "sin":
            S.op("act", lambda e, dst=dst: e.activation(out=dst, in_=vf, func=AF.Sin, bias=negpi, scale=TWO_PI),
                 reads=[b_tmp], writes=[b_tab])
            S.op("dve", lambda e, dst=dst: e.tensor_scalar(out=dst, in0=dst, scalar1=sign_col, scalar2=None, op0=ALU.mult),
                 reads=[b_tab], writes=[b_tab])
        else:
            S.op("act", lambda e, dst=dst: e.activation(out=dst, in_=vf, func=AF.Sin, bias=negpi, scale=TWO_PI),
                 reads=[b_tmp], writes=[b_tab])


def make_ut(C, S, xT_d, tb, NT, xt, b_xt, sq, b_sq, ps_ss, b_ps_ss, ones, b_const, epsc, std, rstd, b_rstd, gcol, ut, b_ut, inv_n, xw=()):
    S.dma("sp", lambda e: e.dma_start(out=xt, in_=xT_d.rearrange("(kc p) t -> p kc t", p=128)[:, :, tb * NT:(tb + 1) * NT]),
          writes=[b_xt] + list(xw))
    for q in range(4):
        S.op("act", lambda e, q=q: e.activation(out=sq[q % 2][:], in_=xt[:, 4 * q:4 * q + 4, :], func=AF.Square),
             reads=[b_xt], writes=[b_sq[q % 2]])

        def fn(e, q=q):
            ins = None
            for i in range(4):
                ins = e.matmul(ps_ss, lhsT=ones, rhs=sq[q % 2][:, i, :], start=(q == 0 and i == 0), stop=(q == 3 and i == 3))
            return ins
        S.op("pe", fn, reads=[b_sq[q % 2]] + b_const, writes=[b_ps_ss])
    S.op("act", lambda e: e.activation(out=std, in_=ps_ss, func=AF.Sqrt, bias=epsc, scale=inv_n), reads=[b_ps_ss] + b_const, writes=[b_rstd])
    S.op("dve", lambda e: e.reciprocal(out=rstd, in_=std), reads=[b_rstd], writes=[b_rstd])
    for kc in range(16):
        S.op("dve", lambda e, kc=kc: e.scalar_tensor_tensor(out=ut[:, kc, :], in0=xt[:, kc, :], scalar=gcol[:, kc:kc + 1], in1=rstd,
                                                            op0=ALU.mult, op1=ALU.mult),
             reads=[b_xt, b_rstd] + b_const, writes=[b_ut])


def _p1(nc, C, fused):
    S = C.S
    ex = {}
    with ExitStack() as st:
        NT = 512
        xT_d = C.dram("xT", [D, S_LEN], F32, "ExternalInput")
        pos_d = C.dram("pos", [128, S_LEN], I32, "ExternalInput")
        gmix_d = C.dram("gmix", [128, 16], F32, "ExternalInput")
        wra_d = C.dram("wra", [8, 128, 16 * 128], F32, "ExternalInput")
        wrb_d = C.dram("wrb", [2, 128, 16 * 512], F32, "ExternalInput")
        wl_d = C.dram("wl", [9, 128, 16 * 128], F32, "ExternalInput")
        wq_d = C.dram("wq", [512, 1024], F32, "ExternalInput")
        wkv_d = C.dram("wkv", [512, 1024], F32, "ExternalInput")
        cols_d = C.dram("cols", [128, 32], F32, "ExternalInput")
        ident_d = C.dram("ident", [128, 128], F32, "ExternalInput")
        mask2_d = C.dram("mask2", [128, 2, 128], F32, "ExternalInput")
        qdec_d = C.dram("qdec", [128, 2, 512], F32, "ExternalInput")
        gret_d = C.dram("gret", [128, 512], F32, "ExternalInput")
        if not fused:
            ryT_d = C.dram("ryT", [512, S_LEN], F32, "ExternalOutput")
            myT_d = C.dram("myT", [512, S_LEN], F32, "ExternalOutput")
        else:
            for nm in ("R", "M"):
                ex[nm + "src"] = [nc.dram_tensor(f"ex{nm}src{i}", [512, 512], BF16).ap() for i in range(8)]
                ex[nm + "dst"] = [nc.dram_tensor(f"ex{nm}dst{i}", [2048, 512], BF16).ap() for i in range(8)]
                ex["b" + nm + "src"] = [C.buf(f"ex{nm}src{i}") for i in range(8)]
                ex["b" + nm + "dst"] = [C.buf(f"ex{nm}dst{i}") for i in range(8)]
                for b_ in ex["b" + nm + "dst"]:
                    b_.nobar = True
        XDT = BF16 if fused else F32
        GROUPS = [[0, 1, 2, 3], [4, 5, 6, 7]]

        def gather(nm, sidx):
            S.cc("pool", lambda e: e.collective_compute("AllGather", ALU.bypass, replica_groups=GROUPS,
                                                        ins=[ex[nm + "src"][sidx].opt()], outs=[ex[nm + "dst"][sidx].opt()]),
                 reads=[ex["b" + nm + "src"][sidx]], writes=[ex["b" + nm + "dst"][sidx]])

        cols = C.sb(st, "cols_s", [128, 32], F32)
        gmix = C.sb(st, "gmixs", [128, 16], F32)
        ident = C.sb(st, "idents", [128, 128], BF16)
        ones = C.sb(st, "oness", [128, 128], BF16)
        b_const = C.buf("const")
        S.dma("sp", lambda e: e.dma_start(out=cols[:], in_=cols_d), writes=[b_const])
        S.dma("sp", lambda e: e.dma_start(out=gmix[:], in_=gmix_d), writes=[b_const])
        S.dma("pool", lambda e: e.dma_start(out=ident[:], in_=ident_d), writes=[b_const])
        b_ones = C.buf("ones")
        S.op("dve", lambda e: e.memset(ones[:], 1.0), writes=[b_ones])
        inv128, sign128, inv64, sign64 = cols[:, 0:1], cols[:, 1:2], cols[:, 2:3], cols[:, 3:4]
        negpi, epsc = cols[:, 4:5], cols[:, 5:6]

        psA = [C.ps(st, f"psA{i}", [128, 512]) for i in range(7)]
        psB = C.ps(st, "psB", [128, 1024], BF16)
        b_psA = [C.buf(f"psA{i}") for i in range(7)]
        b_psB = C.buf("psB")

        def _phase1():
            with ExitStack() as sr:
                wra = C.sb(sr, "wra_s", [128, 8, 16, 128], BF16)
                wrb = C.sb(sr, "wrb_s", [128, 2, 16, 512], BF16)
                xt = C.sb(sr, "xt", [128, 16, NT], F32)
                ut = C.sb(sr, "ut", [128, 16, NT], BF16)
                sq = [C.sb(sr, f"sq{i}", [128, 4, NT], BF16) for i in range(2)]
                std = C.sb(sr, "std", [128, NT], F32)
                rstd = C.sb(sr, "rstd", [128, NT], F32)
                posi = C.sb(sr, "posi", [128, NT], I32)
                posf = C.sb(sr, "posf", [128, NT], F32)
                tv = C.sb(sr, "tv", [128, NT], F32)
                tvi = C.sb(sr, "tvi", [128, NT], I32)
                tvf = C.sb(sr, "tvf", [128, NT], F32)
                cosT = C.sb(sr, "cosT", [128, NT], F32)
                sinT = C.sb(sr, "sinT", [128, NT], F32)
                t1 = C.sb(sr, "t1", [128, NT], F32)
                t2 = C.sb(sr, "t2", [128, NT], F32)
                qT = C.sb(sr, "qT", [128, 2, NT], BF16)
                qdT = C.sb(sr, "qdT", [128, 2, NT], BF16)
                kT = C.sb(sr, "kT", [128, 2, NT], BF16)
                kd = C.sb(sr, "kd", [128, 4, 2, 128], BF16)
                vsb = C.sb(sr, "vsb", [128, 4, 512], BF16)
                sg = C.sb(sr, "sg", [128, 4, 512], BF16)
                mask2 = C.sb(sr, "mask2s", [128, 2, 128], F32)
                qdec = C.sb(sr, "qdecs", [128, 2, 512], F32)
                gret = C.sb(sr, "grets", [128, 512], F32)
                S32 = C.sb(sr, "S32", [128, 2, 256], F32)
                Sbf = [C.sb(sr, f"Sbf{i}", [128, 2, 256], BF16) for i in range(2)]
                sTm = [C.sb(sr, f"sTm{i}", [128, 128], BF16) for i in range(2)]
                stats = C.sb(sr, "stats", [128, 2, 6], F32)
                mv = C.sb(sr, "mv", [128, 2, 2], F32)
                sd = C.sb(sr, "sd", [128, 2], F32)
                rs = C.sb(sr, "rs", [128, 2], F32)
                nbias = C.sb(sr, "nbias", [128, 2], F32)
                on = C.sb(sr, "on", [128, 512], F32)
                rytm = C.sb(sr, "rytm", [128, 512], BF16)
                ryTs = C.sb(sr, "ryTs", [128, 4, NT], XDT)

                b_wra, b_wrb = C.buf("wra"), C.buf("wrb")
                b_xt, b_ut, b_rstd = C.buf("xt"), C.buf("ut"), C.buf("rstd")
                b_sq = [C.buf("sq0"), C.buf("sq1")]
                b_posf, b_tmp, b_tab = C.buf("posf"), C.buf("tmp"), C.buf("tab")
                b_t1, b_t2 = C.buf("t1"), C.buf("t2")
                b_qT, b_kT, b_kd, b_v, b_sg = C.buf("qT"), C.buf("kT"), C.buf("kd"), C.buf("v"), C.buf("sg")
                b_S32 = C.buf("S32")
                b_Sbf = [C.buf("Sbf0"), C.buf("Sbf1")]
                b_sTm = [C.buf("sTm0"), C.buf("sTm1")]
                b_st, b_on, b_rytm, b_ryTs = C.buf("stats"), C.buf("on"), C.buf("rytm"), C.buf("ryTs")
                b_ryout = C.buf("ryout")

                for ch in range(8):
                    S.dma("pool", lambda e, ch=ch: e.dma_start(out=wra[:, ch, :, :], in_=wra_d[ch].rearrange("p (k n) -> p k n", k=16)), writes=[b_wra])
                for j in range(2):
                    for hk in range(2):
                        S.dma("pool", lambda e, j=j, hk=hk: e.dma_start(out=wrb[:, j, 8 * hk:8 * hk + 8, :],
                                                                      in_=wrb_d[j].rearrange("p (k n) -> p k n", k=16)[:, 8 * hk:8 * hk + 8, :]),
                              writes=[b_wrb])
                S.dma("sp", lambda e: e.dma_start(out=mask2[:], in_=mask2_d), writes=[b_const])
                S.dma("sp", lambda e: e.dma_start(out=qdec[:], in_=qdec_d), writes=[b_const])
                S.dma("sp", lambda e: e.dma_start(out=gret[:], in_=gret_d), writes=[b_const])
                S.op("dve", lambda e: e.memset(S32[:], 0.0), writes=[b_S32])
                S.op("dve", lambda e: e.memset(Sbf[0][:], 0.0), writes=[b_Sbf[0]])

                pp = 0
                for tb in range(8):
                    make_ut(C, S, xT_d, tb, NT, xt[:], b_xt, sq, b_sq, psA[6][:], b_psA[6], ones[:], [b_const, b_ones], epsc, std[:], rstd[:], b_rstd,
                            gmix, ut, b_ut, 1.0 / D)
                    S.dma("sp", lambda e, tb=tb: e.dma_start(out=posi[:], in_=pos_d[:, tb * NT:(tb + 1) * NT]), writes=[b_posf])
                    S.op("dve", lambda e: e.tensor_copy(out=posf[:], in_=posi[:]), reads=[b_posf], writes=[b_posf])
                    rope_tables(C, S, posf[:], b_posf, inv128, sign128, negpi, (tv[:], tvi[:], tvf[:]), b_tmp, cosT[:], sinT[:], b_tab, 128, NT)
                    for hh in range(2):
                        for qk in range(2):
                            ch = hh * 4 + qk * 2
                            pa, pb = pp % 4, (pp + 1) % 4
                            pp += 2
                            mm_group(S, psA[pa][:], [(wra[:, ch, kc, :], ut[:, kc, :]) for kc in range(16)], [b_wra, b_ut], [b_psA[pa]])
                            mm_group(S, psA[pb][:], [(wra[:, ch + 1, kc, :], ut[:, kc, :]) for kc in range(16)], [b_wra, b_ut], [b_psA[pb]])
                            S.op("dve", lambda e, pa=pa: e.tensor_tensor(out=t1[:], in0=psA[pa][:], in1=cosT[:], op=ALU.mult),
                                 reads=[b_psA[pa], b_tab], writes=[b_t1])
                            S.op("dve", lambda e, pb=pb: e.tensor_tensor(out=t2[:], in0=psA[pb][:], in1=sinT[:], op=ALU.mult),
                                 reads=[b_psA[pb], b_tab], writes=[b_t2])
                            if qk == 0:
                                S.op("pool", lambda e: e.tensor_tensor(out=t1[:], in0=t1[:], in1=t2[:], op=ALU.add), reads=[b_t1, b_t2], writes=[b_t1])
                                S.op("pool", lambda e, hh=hh: e.tensor_copy(out=qT[:, hh, :], in_=t1[:]), reads=[b_t1], writes=[b_qT])
                                S.op("pool", lambda e, hh=hh: e.tensor_tensor(out=qdT[:, hh, :], in0=t1[:], in1=qdec[:, hh, :], op=ALU.mult),
                                     reads=[b_t1, b_const], writes=[b_qT])
                            else:
                                S.op("pool", lambda e, hh=hh: e.tensor_tensor(out=kT[:, hh, :], in0=t1[:], in1=t2[:], op=ALU.add),
                                     reads=[b_t1, b_t2], writes=[b_kT])
                    for tt in range(4):
                        pa, pb = pp % 4, (pp + 1) % 4
                        pp += 2
                        mm_group(S, psA[pa][:], [(ut[:, kc, tt * 128:(tt + 1) * 128], wrb[:, 0, kc, :]) for kc in range(16)], [b_wrb, b_ut], [b_psA[pa]])
                        mm_group(S, psA[pb][:], [(ut[:, kc, tt * 128:(tt + 1) * 128], wrb[:, 1, kc, :]) for kc in range(16)], [b_wrb, b_ut], [b_psA[pb]])
                        S.op("act", lambda e, pa=pa, tt=tt: e.activation(out=vsb[:, tt, :], in_=psA[pa][:], func=AF.Copy), reads=[b_psA[pa]], writes=[b_v])
                        S.op("act", lambda e, pb=pb, tt=tt: e.activation(out=sg[:, tt, :], in_=psA[pb][:], func=AF.Silu), reads=[b_psA[pb]], writes=[b_sg])
                    for tt in range(4):
                        for hh in range(2):
                            S.op("pe", lambda e, tt=tt, hh=hh: e.transpose(psB[:, 0:128], kT[:, hh, tt * 128:(tt + 1) * 128], ident[:]),
                                 reads=[b_kT, b_const], writes=[b_psB])
                            S.op("dve", lambda e, tt=tt, hh=hh: e.tensor_scalar(out=kd[:, tt, hh, :], in0=psB[:, 0:128], scalar1=cols[:, 6 + hh:7 + hh],
                                                                                  scalar2=None, op0=ALU.mult),
                                 reads=[b_psB, b_const], writes=[b_kd])
                    for tt in range(4):
                        T = tb * 4 + tt
                        cur, nxt = T % 2, (T + 1) % 2
                        tsl = slice(tt * 128, (tt + 1) * 128)
                        for hh in range(2):
                            vsl = slice(hh * 256, (hh + 1) * 256)
                            si = hh
                            mm_group(S, psA[4][:, 0:128], [(kT[:, hh, tsl], qT[:, hh, tsl])], [b_kT, b_qT], [b_psA[4]])
                            S.op("dve", lambda e, hh=hh, si=si: e.tensor_tensor(out=sTm[si][:], in0=psA[4][:, 0:128], in1=mask2[:, hh, :], op=ALU.mult),
                                 reads=[b_psA[4], b_const], writes=[b_sTm[si]])
                            mm_group(S, psA[5][:, vsl], [(sTm[si][:], vsb[:, tt, vsl]), (qdT[:, hh, tsl], Sbf[cur][:, hh, :])],
                                     [b_sTm[si], b_v, b_qT, b_Sbf[cur]], [b_psA[5]])
                            mm_group(S, psA[4][:, 256:512], [(kd[:, tt, hh, :], vsb[:, tt, vsl])], [b_kd, b_v], [b_psA[4]])
                            S.op("dve", lambda e, hh=hh: e.scalar_tensor_tensor(out=S32[:, hh, :], in0=S32[:, hh, :], scalar=cols[:, 8 + hh:9 + hh],
                                                                                 in1=psA[4][:, 256:512], op0=ALU.mult, op1=ALU.add),
                                 reads=[b_psA[4], b_const], writes=[b_S32])
                            S.op("act", lambda e, hh=hh, nxt=nxt: e.activation(out=Sbf[nxt][:, hh, :], in_=S32[:, hh, :], func=AF.Copy),
                                 reads=[b_S32], writes=[b_Sbf[nxt]])
                            S.op("dve", lambda e, hh=hh, vsl=vsl: e.bn_stats(out=stats[:, hh, :], in_=psA[5][:, vsl]), reads=[b_psA[5]], writes=[b_st])
                            S.op("dve", lambda e, hh=hh: e.bn_aggr(out=mv[:, hh, :], in_=stats[:, hh, :]), reads=[b_st], writes=[b_st])
                            S.op("act", lambda e, hh=hh: e.activation(out=sd[:, hh:hh + 1], in_=mv[:, hh, 1:2], func=AF.Sqrt, bias=epsc, scale=1.0),
                                 reads=[b_st, b_const], writes=[b_st])
                            S.op("dve", lambda e, hh=hh: e.reciprocal(out=rs[:, hh:hh + 1], in_=sd[:, hh:hh + 1]), reads=[b_st], writes=[b_st])
                            S.op("dve", lambda e, hh=hh: e.tensor_scalar(out=nbias[:, hh:hh + 1], in0=mv[:, hh, 0:1], scalar1=rs[:, hh:hh + 1], scalar2=-1.0,
                                                                          op0=ALU.mult, op1=ALU.mult), reads=[b_st], writes=[b_st])
                            S.op("act", lambda e, hh=hh, vsl=vsl: e.activation(out=on[:, vsl], in_=psA[5][:, vsl], func=AF.Identity,
                                                                                bias=nbias[:, hh:hh + 1], scale=rs[:, hh:hh + 1]),
                                 reads=[b_psA[5], b_st], writes=[b_on])
                        S.op("dve", lambda e: e.tensor_tensor(out=on[:], in0=on[:], in1=gret[:], op=ALU.mult), reads=[b_on, b_const], writes=[b_on])
                        S.op("pool", lambda e, tt=tt: e.tensor_tensor(out=rytm[:], in0=on[:], in1=sg[:, tt, :], op=ALU.mult), reads=[b_on, b_sg], writes=[b_rytm])
                        for fc in range(4):
                            S.op("pe", lambda e, fc=fc: e.transpose(psB[:, 128 + fc * 128:256 + fc * 128], rytm[:, fc * 128:(fc + 1) * 128], ident[:]),
                                 reads=[b_rytm, b_const], writes=[b_psB])
                        S.op("act", lambda e, tt=tt: e.activation(out=ryTs[:, :, tt * 128:(tt + 1) * 128],
                                                                   in_=psB[:, 128:640].rearrange("p (f t) -> p f t", f=4), func=AF.Copy),
                             reads=[b_psB], writes=[b_ryTs])
                    if not fused:
                        S.dma("sp", lambda e, tb=tb: e.dma_start(out=ryT_d.rearrange("(f p) t -> p f t", p=128)[:, :, tb * NT:(tb + 1) * NT], in_=ryTs[:]),
                              reads=[b_ryTs], writes=[b_ryout])
                    else:
                        qq_, s0 = tb // 2, (tb % 2) * 4
                        for si in range(4):
                            S.dma("sp", lambda e, si=si, qq_=qq_, s0=s0: e.dma_start(
                                out=ex["Rsrc"][s0 + si].rearrange("(f p) t -> p f t", p=128)[:, :, qq_ * 128:(qq_ + 1) * 128],
                                in_=ryTs[:, :, si * 128:(si + 1) * 128]), reads=[b_ryTs], writes=[ex["bRsrc"][s0 + si]])
                        if tb >= 6:
                            for si in range(4):
                                gather("R", s0 + si)
                S.barrier()

        _phase1()
        cqn = C.sb(st, "cqn", [128, 4, S_LEN], BF16)
        ckvn = C.sb(st, "ckvn", [128, 4, S_LEN], BF16)
        kpeT = C.sb(st, "kpeT", [64, S_LEN], BF16)
        cos64 = C.sb(st, "cos64", [64, S_LEN], BF16)
        sin64 = C.sb(st, "sin64", [64, S_LEN], BF16)
        b_cqn, b_ckvn, b_kpe, b_tab64 = C.buf("cqn"), C.buf("ckvn"), C.buf("kpe"), C.buf("tab64")

        def _phase2():
            with ExitStack() as sl:
                wl = C.sb(sl, "wl_s", [128, 9, 16, 128], BF16)
                xt = C.sb(sl, "xt2", [128, 16, NT], F32)
                ut = C.sb(sl, "ut2", [128, 16, NT], BF16)
                sq = [C.sb(sl, f"sq2{i}", [128, 4, NT], BF16) for i in range(2)]
                std = C.sb(sl, "std2", [128, NT], F32)
                rstd = C.sb(sl, "rstd2", [128, NT], F32)
                posi = C.sb(sl, "posi2", [64, NT], I32)
                posf = C.sb(sl, "posf2", [64, NT], F32)
                tv = C.sb(sl, "tv2", [64, NT], F32)
                tvi = C.sb(sl, "tvi2", [64, NT], I32)
                tvf = C.sb(sl, "tvf2", [64, NT], F32)
                c32 = C.sb(sl, "c32", [128, 4, NT], F32)
                csq = sq[0]
                std2 = std
                rstd2 = rstd
                t1 = tv
                t2 = tvf
                b_wl = C.buf("wl")
                b_xt, b_ut, b_rstd = C.buf("xt"), C.buf("ut"), C.buf("rstd")
                b_sq = [C.buf("sq0"), C.buf("sq1")]
                b_posf, b_tmp = C.buf("posf"), C.buf("tmp")
                b_c32, b_csq, b_rstd2 = C.buf("c32"), b_sq[0], b_rstd
                b_t1, b_t2 = b_tmp, b_tmp
                for ch in range(9):
                    S.dma("pool", lambda e, ch=ch: e.dma_start(out=wl[:, ch, :, :], in_=wl_d[ch].rearrange("p (k n) -> p k n", k=16)), writes=[b_wl])
                pp = 0
                for tb in range(8):
                    bsl = slice(tb * NT, (tb + 1) * NT)
                    make_ut(C, S, xT_d, tb, NT, xt[:], b_xt, sq, b_sq, psA[6][:], b_psA[6], ones[:], [b_const, b_ones], epsc, std[:], rstd[:], b_rstd,
                            gmix, ut, b_ut, 1.0 / D)
                    S.dma("sp", lambda e, bsl=bsl: e.dma_start(out=posi[:], in_=pos_d[0:64, bsl]), writes=[b_posf])
                    S.op("dve", lambda e: e.tensor_copy(out=posf[:], in_=posi[:]), reads=[b_posf], writes=[b_posf])
                    rope_tables(C, S, posf[:], b_posf, inv64[0:64, :], sign64[0:64, :], negpi[0:64, :], (tv[:], tvi[:], tvf[:]), b_tmp,
                                cos64[:, bsl], sin64[:, bsl], b_tab64, 64, NT)
                    for which in range(2):
                        dst, b_dst, gbase = (cqn, b_cqn, 10) if which == 0 else (ckvn, b_ckvn, 14)
                        for c4 in range(4):
                            ch = which * 4 + c4
                            pa = pp % 4
                            pp += 1
                            mm_group(S, psA[pa][:], [(wl[:, ch, kc, :], ut[:, kc, :]) for kc in range(16)], [b_wl, b_ut], [b_psA[pa]])
                            S.op("act", lambda e, pa=pa, c4=c4: e.activation(out=c32[:, c4, :], in_=psA[pa][:], func=AF.Copy), reads=[b_psA[pa]], writes=[b_c32])
                            S.op("pool", lambda e, c4=c4: e.tensor_tensor(out=csq[:, c4, :], in0=c32[:, c4, :], in1=c32[:, c4, :], op=ALU.mult),
                                 reads=[b_c32], writes=[b_csq])
                        mm_group(S, psA[5][:], [(ones[:], csq[:, c4, :]) for c4 in range(4)], [b_csq, b_ones], [b_psA[5]])
                        S.op("act", lambda e: e.activation(out=std2[:], in_=psA[5][:], func=AF.Sqrt, bias=epsc, scale=1.0 / 512), reads=[b_psA[5], b_const],
                             writes=[b_rstd2])
                        S.op("dve", lambda e: e.reciprocal(out=rstd2[:], in_=std2[:]), reads=[b_rstd2], writes=[b_rstd2])
                        for c4 in range(4):
                            S.op("dve", lambda e, c4=c4, dst=dst, gbase=gbase, bsl=bsl: e.scalar_tensor_tensor(
                                out=dst[:, c4, bsl], in0=c32[:, c4, :], scalar=cols[:, gbase + c4:gbase + c4 + 1], in1=rstd2[:], op0=ALU.mult, op1=ALU.mult),
                                reads=[b_c32, b_rstd2, b_const], writes=[b_dst])
                    mm_group(S, psA[4][0:64, :], [(wl[:, 8, kc, 0:64], ut[:, kc, :]) for kc in range(16)], [b_wl, b_ut], [b_psA[4]])
                    S.op("dve", lambda e, bsl=bsl: e.tensor_tensor(out=t1[:], in0=psA[4][0:64, :], in1=cos64[:, bsl], op=ALU.mult),
                         reads=[b_psA[4], b_tab64], writes=[b_t1])
                    mm_group(S, psA[4][0:64, :], [(wl[:, 8, kc, 64:128], ut[:, kc, :]) for kc in range(16)], [b_wl, b_ut], [b_psA[4]])
                    S.op("dve", lambda e, bsl=bsl: e.tensor_tensor(out=t2[:], in0=psA[4][0:64, :], in1=sin64[:, bsl], op=ALU.mult),
                         reads=[b_psA[4], b_tab64], writes=[b_t2])
                    S.op("pool", lambda e, bsl=bsl: e.tensor_tensor(out=kpeT[:, bsl], in0=t1[:], in1=t2[:], op=ALU.add), reads=[b_t1, b_t2], writes=[b_kpe])
                S.barrier()

        _phase2()
        def _phase3():
            with ExitStack() as sm:
                wq = C.sb(sm, "wq_s", [128, 4, 1024], BF16)
                wkv = C.sb(sm, "wkv_s", [128, 4, 1024], BF16)
                vall = C.sb(sm, "vall", [128, 32, 512], BF16)
                knT = [C.sb(sm, f"knT{i}", [128, S_LEN], BF16) for i in range(2)]
                qn = [C.sb(sm, f"qn{i}", [128, 512], BF16) for i in range(2)]
                qp = [C.sb(sm, f"qp{i}", [64, 512], BF16) for i in range(2)]
                t1 = C.sb(sm, "t1c", [64, 512], F32)
                t2 = C.sb(sm, "t2c", [64, 512], F32)
                pT = [C.sb(sm, f"pT{i}", [128, 512], BF16) for i in range(3)]
                rec = C.sb(sm, "rec", [128, 512], F32)
                stage = [C.sb(sm, f"stage{i}", [128, 512], XDT) for i in range(2)]
                b_wq, b_wkv, b_vall = C.buf("wq"), C.buf("wkv"), C.buf("vall")
                b_knT = [C.buf("knT0"), C.buf("knT1")]
                b_qn = [C.buf("qn0"), C.buf("qn1")]
                b_qp = [C.buf("qp0"), C.buf("qp1")]
                b_t1, b_t2, b_rec = C.buf("t1"), C.buf("t2"), C.buf("rec")
                b_pT = [C.buf("pT0"), C.buf("pT1"), C.buf("pT2")]
                b_stage = [C.buf("stage0"), C.buf("stage1")]
                b_myout = C.buf("myout")
                S.dma("pool", lambda e: e.dma_start(out=wq[:], in_=wq_d.rearrange("(k p) n -> p k n", p=128)), writes=[b_wq])
                S.dma("pool", lambda e: e.dma_start(out=wkv[:], in_=wkv_d.rearrange("(k p) n -> p k n", p=128)), writes=[b_wkv])
                for T in range(32):
                    pa = T % 2
                    mm_group(S, psA[pa][:], [(ckvn[:, kc, T * 128:(T + 1) * 128], wkv[:, kc, 512:1024]) for kc in range(4)], [b_ckvn, b_wkv], [b_psA[pa]])
                    S.op("act", lambda e, pa=pa, T=T: e.activation(out=vall[:, T, :], in_=psA[pa][:], func=AF.Copy), reads=[b_psA[pa]], writes=[b_vall])
                sc = 192.0 ** -0.5
                it = 0
                pi = 0
                for h in range(4):
                    kb_ = h % 2
                    for tb in range(8):
                        pa = tb % 2
                        mm_group(S, psA[pa][:], [(wkv[:, kc, h * 128:(h + 1) * 128], ckvn[:, kc, tb * 512:(tb + 1) * 512]) for kc in range(4)],
                                 [b_ckvn, b_wkv], [b_psA[pa]])
                        S.op("dve", lambda e, pa=pa, tb=tb, kb_=kb_: e.tensor_copy(out=knT[kb_][:, tb * 512:(tb + 1) * 512], in_=psA[pa][:]),
                             reads=[b_psA[pa]], writes=[b_knT[kb_]])
                    for qb in range(8):
                        qi = it % 2
                        it += 1
                        qsl = slice(qb * 512, (qb + 1) * 512)
                        mm_group(S, psA[2][:], [(wq[:, kc, h * 256:h * 256 + 128], cqn[:, kc, qsl]) for kc in range(4)], [b_cqn, b_wq], [b_psA[2]])
                        S.op("act", lambda e, qi=qi: e.activation(out=qn[qi][:], in_=psA[2][:], func=AF.Copy), reads=[b_psA[2]], writes=[b_qn[qi]])
                        mm_group(S, psA[3][0:64, :], [(wq[:, kc, h * 256 + 128:h * 256 + 192], cqn[:, kc, qsl]) for kc in range(4)], [b_cqn, b_wq], [b_psA[3]])
                        S.op("dve", lambda e, qsl=qsl: e.tensor_tensor(out=t1[:], in0=psA[3][0:64, :], in1=cos64[:, qsl], op=ALU.mult),
                             reads=[b_psA[3], b_tab64], writes=[b_t1])
                        mm_group(S, psA[3][0:64, :], [(wq[:, kc, h * 256 + 192:h * 256 + 256], cqn[:, kc, qsl]) for kc in range(4)], [b_cqn, b_wq], [b_psA[3]])
                        S.op("dve", lambda e, qsl=qsl: e.tensor_tensor(out=t2[:], in0=psA[3][0:64, :], in1=sin64[:, qsl], op=ALU.mult),
                             reads=[b_psA[3], b_tab64], writes=[b_t2])
                        S.op("pool", lambda e, qi=qi: e.tensor_tensor(out=qp[qi][:], in0=t1[:], in1=t2[:], op=ALU.add), reads=[b_t1, b_t2], writes=[b_qp[qi]])
                        nkb = 4 * qb + 4

                        def c0_of(kb, qb=qb):
                            j = kb - 4 * qb
                            return 128 * j if j > 0 else 0

                        def s_mm(kb, qi=qi, kb_=kb_):
                            c0 = c0_of(kb)
                            ksl = slice(kb * 128, (kb + 1) * 128)
                            ps = psA[4 + kb % 2]
                            mm_group(S, ps[:, c0:512], [(knT[kb_][:, ksl], qn[qi][:, c0:512]), (kpeT[:, ksl], qp[qi][:, c0:512])],
                                     [b_knT[kb_], b_qn[qi], b_kpe, b_qp[qi]], [b_psA[4 + kb % 2]])
                        s_mm(0)
                        for kb in range(nkb):
                            if kb + 1 < nkb:
                                s_mm(kb + 1)
                            c0 = c0_of(kb)
                            j = kb - 4 * qb
                            pidx = pi % 3
                            pi += 1
                            S.op("act", lambda e, kb=kb, c0=c0, pidx=pidx: e.activation(out=pT[pidx][:, c0:512], in_=psA[4 + kb % 2][:, c0:512], func=AF.Exp, scale=sc),
                                 reads=[b_psA[4 + kb % 2]], writes=[b_pT[pidx]])
                            if j >= 0:
                                S.op("pool", lambda e, c0=c0, pidx=pidx: e.memset(pT[pidx][64:128, c0:c0 + 64], 0.0), writes=[b_pT[pidx]])
                            first, last = (kb == 0), (kb == nkb - 1)

                            def pv(e, kb=kb, c0=c0, pidx=pidx, first=first, last=last, h=h):
                                e.matmul(psA[0][:, c0:512], lhsT=vall[:, kb, h * 128:(h + 1) * 128], rhs=pT[pidx][:, c0:512], start=first, stop=last)
                                return e.matmul(psA[1][:, c0:512], lhsT=ones[:], rhs=pT[pidx][:, c0:512], start=first, stop=last)
                            S.op("pe", pv, reads=[b_vall, b_pT[pidx], b_ones], writes=[b_psA[0], b_psA[1]])
                        si = (h * 8 + qb) % 2
                        S.op("dve", lambda e: e.reciprocal(out=rec[:], in_=psA[1][:]), reads=[b_psA[1]], writes=[b_rec])
                        S.op("dve", lambda e, si=si: e.tensor_tensor(out=stage[si][:], in0=psA[0][:], in1=rec[:], op=ALU.mult),
                             reads=[b_psA[0], b_rec], writes=[b_stage[si]])
                        if not fused:
                            S.dma("sp", lambda e, si=si, h=h, qsl=qsl: e.dma_start(out=myT_d[h * 128:(h + 1) * 128, qsl], in_=stage[si][:]),
                                  reads=[b_stage[si]], writes=[b_myout], sembuf=b_stage[si])
                        else:
                            qq_, s0 = qb // 2, (qb % 2) * 4
                            for sj in range(4):
                                S.dma("sp", lambda e, si=si, sj=sj, h=h, qq_=qq_, s0=s0: e.dma_start(
                                    out=ex["Msrc"][s0 + sj][h * 128:(h + 1) * 128, qq_ * 128:(qq_ + 1) * 128],
                                    in_=stage[si][:, sj * 128:(sj + 1) * 128]), reads=[b_stage[si]], writes=[ex["bMsrc"][s0 + sj]])
                            if h == 3 and qb >= 6:
                                for sj in range(4):
                                    gather("M", s0 + sj)
                S.barrier()
        _phase3()
    return ex


def build1():
    nc = bass.Bass("TRN2", target_bir_lowering=False)
    with ExitStack() as st0:
        C = Ctx(nc, st0)
        _p1(nc, C, False)
        C.S.emit()
    return nc


def _chunk_major(w):
    K, N = w.shape
    n = N // 128
    return np.ascontiguousarray(w.reshape(16, 128, n, 128).transpose(2, 1, 0, 3).reshape(n, 128, 16 * 128))


def _block_major(w, bw):
    K, N = w.shape
    n = N // bw
    return np.ascontiguousarray(w.reshape(16, 128, n, bw).transpose(2, 1, 0, 3).reshape(n, 128, 16 * bw))


def _swap_halves(w, d):
    K, N = w.shape
    w4 = w.reshape(K, N // d, 2, d // 2)
    return np.ascontiguousarray(w4[:, :, ::-1, :].reshape(K, N))


def _consts1(g):
    cols = np.zeros((128, 32), np.float64)
    p = np.arange(128)
    cols[:, 0] = (10000.0 ** (-(p % 64) / 64.0)) / (2 * np.pi)
    cols[:, 1] = np.where(p < 64, -1.0, 1.0)
    cols[:, 2] = (10000.0 ** (-(p % 32) / 32.0)) / (2 * np.pi)
    cols[:, 3] = np.where((p % 64) < 32, -1.0, 1.0)
    cols[:, 4] = -np.pi
    cols[:, 5] = EPS
    mask2 = np.zeros((128, 2, 128), np.float64)
    qdec = np.zeros((128, 2, 512), np.float64)
    for hh in range(2):
        h = 2 * g + hh
        gam = 1.0 - 2.0 ** (-5.0 - h)
        cols[:, 6 + hh] = gam ** (127 - p) * (128 ** -0.5)
        cols[:, 8 + hh] = gam ** 128
        m = p[:, None]
        n = p[None, :]
        same = (m // 64) == (n // 64)
        val = np.where(same, gam ** np.abs(n - m), np.where(m < n, gam ** (n - m).clip(0), 0.0))
        mask2[:, hh, :] = val * (128 ** -0.5)
        t = np.arange(512)
        qdec[:, hh, :] = (gam ** ((t % 128) + 1))[None, :]
    return cols, mask2.astype(np.float32), qdec.astype(np.float32)


def _launch1(inp, prep_only=False):
    x = inp["x"]
    pos = inp["positions"]
    w_in = inp["w_in"][0]
    w_q_b = inp["w_q_b"][0]
    w_kv_b = inp["w_kv_b"][0]
    in_maps = []
    xTs = [np.ascontiguousarray(x[b].T) for b in range(2)]
    for c in range(8):
        b, g = c // 4, c % 4
        cols, mask2, qdec = _consts1(g)
        cols[:, 10:14] = inp["q_a_norm_g"][0].reshape(4, 128).T
        cols[:, 14:18] = inp["kv_a_norm_g"][0].reshape(4, 128).T
        chunks = []
        for hh in range(2):
            h = 2 * g + hh
            wq_h = w_in[:, h * 128:(h + 1) * 128]
            wk_h = w_in[:, 1024 + h * 128:1024 + (h + 1) * 128]
            chunks += [wq_h, _swap_halves(wq_h, 128), wk_h, _swap_halves(wk_h, 128)]
        wra = _chunk_major(np.concatenate(chunks, axis=1))
        rv = w_in[:, 2048 + g * 512:2048 + (g + 1) * 512]
        rg = w_in[:, 4096 + g * 512:4096 + (g + 1) * 512]
        wrb = _block_major(np.concatenate([rv, rg], axis=1), 512)
        cq = w_in[:, 6144:6656]
        ckv = w_in[:, 6656:7168]
        kpe = w_in[:, 7168:7232]
        wl = _chunk_major(np.concatenate([cq, ckv, kpe, _swap_halves(kpe, 64)], axis=1))
        wq_parts = []
        for hm in range(4 * g, 4 * g + 4):
            qh = w_q_b[:, hm * 192:(hm + 1) * 192]
            wq_parts += [qh[:, 0:128], qh[:, 128:192], _swap_halves(qh[:, 128:192], 64)]
        wq = np.ascontiguousarray(np.concatenate(wq_parts, axis=1))
        kn = [w_kv_b[:, hm * 256:hm * 256 + 128] for hm in range(4 * g, 4 * g + 4)]
        vv = [w_kv_b[:, hm * 256 + 128:hm * 256 + 256] for hm in range(4 * g, 4 * g + 4)]
        wkv = np.ascontiguousarray(np.concatenate(kn + vv, axis=1))
        in_maps.append({
            "xT": xTs[b],
            "pos": np.ascontiguousarray(np.broadcast_to(pos[b].astype(np.int32)[None, :], (128, S_LEN))),
            "gmix": np.ascontiguousarray(inp["norm_mix_g"][0].reshape(16, 128).T),
            "wra": wra, "wrb": wrb, "wl": wl, "wq": wq, "wkv": wkv,
            "cols": cols.astype(np.float32),
            "ident": np.eye(128, dtype=np.float32),
            "mask2": mask2, "qdec": qdec,
            "gret": np.ascontiguousarray(np.broadcast_to(inp["ret_norm_g"][0][g * 512:(g + 1) * 512][None, :], (128, 512))),
        })
    if prep_only:
        return xTs, in_maps
    nc = build1()
    res = run_bass_kernel_spmd(nc, in_maps, core_ids=list(range(8)))
    ryT = [np.concatenate([res.results[b * 4 + g]["ryT"] for g in range(4)], axis=0) for b in range(2)]
    myT = [np.concatenate([res.results[b * 4 + g]["myT"] for g in range(4)], axis=0) for b in range(2)]
    return xTs, ryT, myT


def norm_from_sb(S, src, b_src, sq, b_sq, ps_ss, b_ps_ss, ones, b_cl, epsc, std, rstd, b_rstd, NT):
    for q in range(4):
        S.op("act", lambda e, q=q: e.activation(out=sq[q % 2][:], in_=src[:, 4 * q:4 * q + 4, :], func=AF.Square),
             reads=[b_src], writes=[b_sq[q % 2]])

        def fn(e, q=q):
            ins = None
            for i in range(4):
                ins = e.matmul(ps_ss, lhsT=ones, rhs=sq[q % 2][:, i, :], start=(q == 0 and i == 0), stop=(q == 3 and i == 3))
            return ins
        S.op("pe", fn, reads=[b_sq[q % 2]] + b_cl, writes=[b_ps_ss])
    S.op("act", lambda e: e.activation(out=std, in_=ps_ss, func=AF.Sqrt, bias=epsc, scale=1.0 / D), reads=[b_ps_ss] + b_cl, writes=[b_rstd])
    S.op("dve", lambda e: e.reciprocal(out=rstd, in_=std), reads=[b_rstd], writes=[b_rstd])


_JC = {}


def _jofs(e):
    if "v" not in _JC:
        _JC["v"] = (e.partition_id() % 4) * 128
    return _JC["v"]


def _p2(nc, C, ex):
    _JC.clear()
    S = C.S
    fused = ex is not None
    C.pfx = "p2_"
    with ExitStack() as st:
        NT = 512
        xT_d = C.dram("xTo" if fused else "xT", [D, 1024], F32, "ExternalInput")
        if not fused:
            ryT_d = C.dram("ryT", [D, 1024], F32, "ExternalInput")
            myT_d = C.dram("myT", [D, 1024], F32, "ExternalInput")
        gcols_d = C.dram("gcols", [128, 64], F32, "ExternalInput")
        wg_d = C.dram("wg", [32, 128, 2048], F32, "ExternalInput")
        wro_d = C.dram("wro", [16, 128, 2048], F32, "ExternalInput")
        wmo_d = C.dram("wmo", [16, 128, 2048], F32, "ExternalInput")
        wout_d = C.dram("wout", [16, 128, 2048], F32, "ExternalInput")
        wup_d = C.dram("wup", [64, 128, 2048], F32, "ExternalInput")
        wdn_d = C.dram("wdn", [16, 128, 8192], F32, "ExternalInput")
        outT_d = C.dram("outT", [D, 1024], F32, "ExternalOutput")
        if DEBUG2:
            hdbg_d = C.dram("hdbg", [D, 1024], F32, "ExternalOutput")
            hndbg_d = C.dram("hndbg", [D, 1024], BF16, "ExternalOutput")
            adbg_d = C.dram("adbg", [D, 1024], BF16, "ExternalOutput")
            b_dbg = C.buf("dbg")

        gcols = C.sb(st, "gcols_s", [128, 64], F32)
        ones = C.sb(st, "ones_s", [128, 128], BF16)
        b_const, b_ones = C.buf("const"), C.buf("ones")
        S.dma("sp", lambda e: e.dma_start(out=gcols[:], in_=gcols_d), writes=[b_const])
        S.op("dve", lambda e: e.memset(ones[:], 1.0), writes=[b_ones])
        b_cl = [b_const, b_ones]
        epsc = gcols[:, 48:49]
        ps = [C.ps(st, f"ps{i}", [128, 512]) for i in range(8)]
        b_ps = [C.buf(f"ps{i}") for i in range(8)]
        hT = C.sb(st, "hT", [128, 16, 1024], F32)
        b_hT = C.buf("hT")

        def phase_ac():
            with ExitStack() as sa:
                xt = C.sb(sa, "xt", [128, 16, NT], F32)
                ut = C.sb(sa, "ut", [128, 16, NT], BF16)
                sq = [C.sb(sa, f"sq{i}", [128, 4, NT], BF16) for i in range(2)]
                std = C.sb(sa, "std", [128, NT], F32)
                rstd = C.sb(sa, "rstd", [128, NT], F32)
                rys = C.sb(sa, "rys", [128, 4, 16, 128], BF16)
                mys = C.sb(sa, "mys", [128, 4, 16, 128], BF16)
                merged = C.sb(sa, "merged", [128, 16, NT], BF16)
                wB = [[C.sb(sa, f"wB{i}_{j}", [128, 16, 128], BF16) for j in range(4)] for i in range(2)]
                wC = [wB[0][0], wB[1][0]]
                s1 = xt[:, 0:1, :].rearrange("p a t -> p (a t)")
                s2 = xt[:, 1:2, :].rearrange("p a t -> p (a t)")
                m1 = s1
                m2 = s2
                xc = [xt[:, 2:3, :].rearrange("p a t -> p (a t)"), xt[:, 3:4, :].rearrange("p a t -> p (a t)")]
                b_xt, b_ut, b_rstd = C.buf("xt"), C.buf("ut"), C.buf("rstd")
                b_sq = [C.buf("sq0"), C.buf("sq1")]
                b_rys, b_mys, b_merged = C.buf("rys"), C.buf("mys"), C.buf("merged")
                b_wB = [[C.buf(f"wB{i}{j}") for j in range(4)] for i in range(2)]
                b_wC = [b_wB[0][0], b_wB[1][0]]
                b_s1, b_s2 = C.buf("s1"), C.buf("s2")
                b_m1, b_m2 = b_s1, b_s2
                b_xc = [C.buf("xc0"), C.buf("xc1")]
                for hf in range(2):
                    hsl = slice(hf * NT, (hf + 1) * NT)
                    make_ut(C, S, xT_d, hf, NT, xt[:], b_xt, sq, b_sq, ps[6][:], b_ps[6], ones[:], b_cl, epsc, std[:], rstd[:], b_rstd,
                            gcols, ut, b_ut, 1.0 / D, xw=[b_s1, b_s2, b_xc[0], b_xc[1]])
                    if not fused:
                        S.dma("pool", lambda e, hsl=hsl: e.dma_start(out=rys[:].rearrange("p s k t -> p k s t"), in_=ryT_d[:, hsl].rearrange("(k p) (s t) -> p k s t", p=128, s=4)), writes=[b_rys])
                        S.dma("pool", lambda e, hsl=hsl: e.dma_start(out=mys[:].rearrange("p s k t -> p k s t"), in_=myT_d[:, hsl].rearrange("(k p) (s t) -> p k s t", p=128, s=4)), writes=[b_mys])
                    else:
                        for nm, dstt, b_d in (("R", rys, b_rys), ("M", mys, b_mys)):
                            for si in range(4):
                                sidx = hf * 4 + si
                                S.dma("pool", lambda e, nm=nm, dstt=dstt, si=si, sidx=sidx: e.dma_start(
                                    out=dstt[:, si, :, :],
                                    in_=ex[nm + "dst"][sidx].rearrange("(k p) t -> p k t", p=128)[:, :, bass.ds(_jofs(e), 128)]),
                                    reads=[ex["b" + nm + "dst"][sidx]], writes=[b_d])
                    for oc in range(16):
                        r = oc % 2
                        srcs = (wg_d[oc], wg_d[16 + oc], wro_d[oc], wmo_d[oc])
                        for j in range(4):
                            S.dma("pool", lambda e, r=r, j=j, src=srcs[j]: e.dma_start(out=wB[r][j][:], in_=src.rearrange("p (k n) -> p k n", k=16)),
                                  writes=[b_wB[r][j]])
                        pb = 0 if r == 0 else 2
                        mm_group(S, ps[pb][:], [(wB[r][0][:, kc, :], ut[:, kc, :]) for kc in range(16)], [b_wB[r][0], b_ut], [b_ps[pb]])
                        S.op("act", lambda e, pb=pb: e.activation(out=s1, in_=ps[pb][:], func=AF.Sigmoid), reads=[b_ps[pb]], writes=[b_s1])
                        mm_group(S, ps[pb + 1][:], [(wB[r][1][:, kc, :], ut[:, kc, :]) for kc in range(16)], [b_wB[r][1], b_ut], [b_ps[pb + 1]])
                        S.op("act", lambda e, pb=pb: e.activation(out=s2, in_=ps[pb + 1][:], func=AF.Sigmoid), reads=[b_ps[pb + 1]], writes=[b_s2])
                        mm_group(S, ps[pb + 4][:].rearrange("p (s t) -> p s t", s=4), [(wB[r][2][:, kc, :], rys[:, :, kc, :]) for kc in range(16)], [b_wB[r][2], b_rys], [b_ps[pb + 4]])
                        S.op("dve", lambda e, pb=pb: e.tensor_tensor(out=m1, in0=ps[pb + 4][:], in1=s1, op=ALU.mult),
                             reads=[b_ps[pb + 4], b_s1], writes=[b_m1])
                        mm_group(S, ps[pb + 5][:].rearrange("p (s t) -> p s t", s=4), [(wB[r][3][:, kc, :], mys[:, :, kc, :]) for kc in range(16)], [b_wB[r][3], b_mys], [b_ps[pb + 5]])
                        S.op("dve", lambda e, pb=pb: e.tensor_tensor(out=m2, in0=ps[pb + 5][:], in1=s2, op=ALU.mult),
                             reads=[b_ps[pb + 5], b_s2], writes=[b_m2])
                        S.op("dve", lambda e, oc=oc: e.tensor_tensor(out=merged[:, oc, :], in0=m1, in1=m2, op=ALU.add),
                             reads=[b_m1, b_m2], writes=[b_merged])
                    for oc in range(16):
                        r = oc % 2
                        S.dma("pool", lambda e, r=r, oc=oc: e.dma_start(out=wC[r][:], in_=wout_d[oc].rearrange("p (k n) -> p k n", k=16)), writes=[b_wC[r]])
                        S.dma("sp", lambda e, r=r, oc=oc, hsl=hsl: e.dma_start(out=xc[r], in_=xT_d[oc * 128:(oc + 1) * 128, hsl]), writes=[b_xc[r], b_xt])
                        mm_group(S, ps[r][:], [(wC[r][:, kc, :], merged[:, kc, :]) for kc in range(16)], [b_wC[r], b_merged], [b_ps[r]])
                        S.op("dve", lambda e, r=r, oc=oc, hsl=hsl: e.tensor_tensor(out=hT[:, oc, hsl], in0=ps[r][:], in1=xc[r], op=ALU.add),
                             reads=[b_ps[r], b_xc[r]], writes=[b_hT])
                if DEBUG2:
                    S.dma("sp", lambda e: e.dma_start(out=hdbg_d.rearrange("(k p) t -> p k t", p=128), in_=hT[:]), reads=[b_hT], writes=[b_dbg])
                S.barrier()
        phase_ac()

        def phase_mlp():
            with ExitStack() as sa:
                hn = C.sb(sa, "hn", [128, 16, 1024], BF16)
                aT = C.sb(sa, "aT", [128, 16, 1024], BF16)
                sq = [C.sb(sa, f"sqm{i}", [128, 4, NT], BF16) for i in range(2)]
                std = C.sb(sa, "stdm", [128, NT], F32)
                rstd = C.sb(sa, "rstdm", [128, NT], F32)
                wU = [C.sb(sa, f"wU{i}", [128, 16, 128], BF16) for i in range(4)]
                wD = [C.sb(sa, f"wD{i}", [128, 16, 128], BF16) for i in range(3)]
                rl = [C.sb(sa, f"rl{i}", [128, NT], F32) for i in range(2)]
                b_hn, b_aT, b_rstd = C.buf("hn"), C.buf("aT"), C.buf("rstd")
                b_sq = [C.buf("sq0"), C.buf("sq1")]
                b_wU = [C.buf(f"wU{i}") for i in range(4)]
                b_wD = [C.buf(f"wD{i}") for i in range(3)]
                b_rl = [C.buf("rl0"), C.buf("rl1")]
                b_out = C.buf("out")
                for hf in range(2):
                    hsl = slice(hf * NT, (hf + 1) * NT)
                    norm_from_sb(S, hT[:, :, hsl], b_hT, sq, b_sq, ps[6][:], b_ps[6], ones[:], b_cl, epsc, std[:], rstd[:], b_rstd, NT)
                    for kc in range(16):
                        S.op("dve", lambda e, kc=kc, hsl=hsl: e.scalar_tensor_tensor(out=hn[:, kc, hsl], in0=hT[:, kc, hsl], scalar=gcols[:, 16 + kc:17 + kc],
                                                                                    in1=rstd[:], op0=ALU.mult, op1=ALU.mult),
                             reads=[b_hT, b_rstd, b_const], writes=[b_hn])
                ui = 0
                di = 0
                zi = 0
                for qq in range(4):
                    for fcl in range(16):
                        fc = qq * 16 + fcl
                        r = ui % 4
                        ui += 1
                        S.dma("pool", lambda e, r=r, fc=fc: e.dma_start(out=wU[r][:], in_=wup_d[fc].rearrange("p (k n) -> p k n", k=16)), writes=[b_wU[r]])
                        for hf in range(2):
                            hsl = slice(hf * NT, (hf + 1) * NT)
                            pz = zi % 4
                            zi += 1
                            mm_group(S, ps[pz][:], [(wU[r][:, kc, :], hn[:, kc, hsl]) for kc in range(16)], [b_wU[r], b_hn], [b_ps[pz]])
                            S.op("act", lambda e, pz=pz: e.activation(out=rl[pz % 2][:], in_=ps[pz][:], func=AF.Relu), reads=[b_ps[pz]], writes=[b_rl[pz % 2]])
                            S.op("dve", lambda e, pz=pz, fcl=fcl, hsl=hsl: e.tensor_tensor(out=aT[:, fcl, hsl], in0=rl[pz % 2][:], in1=rl[pz % 2][:], op=ALU.mult),
                                 reads=[b_rl[pz % 2]], writes=[b_aT])
                    for oc in range(16):
                        r = di % 3
                        di += 1
                        S.dma("pool", lambda e, r=r, oc=oc, qq=qq: e.dma_start(out=wD[r][:], in_=wdn_d[oc][:, qq * 2048:(qq + 1) * 2048].rearrange("p (k n) -> p k n", k=16)),
                              writes=[b_wD[r]])
                        for hf in range(2):
                            hsl = slice(hf * NT, (hf + 1) * NT)
                            pd = 4 + (oc * 2 + hf) % 2
                            mm_group(S, ps[pd][:], [(wD[r][:, fcl, :], aT[:, fcl, hsl]) for fcl in range(16)], [b_wD[r], b_aT], [b_ps[pd]])
                            S.op("dve", lambda e, pd=pd, oc=oc, hsl=hsl: e.tensor_tensor(out=hT[:, oc, hsl], in0=ps[pd][:], in1=hT[:, oc, hsl], op=ALU.add),
                                 reads=[b_ps[pd]], writes=[b_hT])
                if DEBUG2:
                    S.dma("sp", lambda e: e.dma_start(out=hndbg_d.rearrange("(k p) t -> p k t", p=128), in_=hn[:]), reads=[b_hn], writes=[b_dbg])
                    S.dma("sp", lambda e: e.dma_start(out=adbg_d.rearrange("(k p) t -> p k t", p=128), in_=aT[:]), reads=[b_aT], writes=[b_dbg])
                for hf in range(2):
                    hsl = slice(hf * NT, (hf + 1) * NT)
                    norm_from_sb(S, hT[:, :, hsl], b_hT, sq, b_sq, ps[6][:], b_ps[6], ones[:], b_cl, epsc, std[:], rstd[:], b_rstd, NT)
                    for kc in range(16):
                        S.op("dve", lambda e, kc=kc, hsl=hsl: e.scalar_tensor_tensor(out=hT[:, kc, hsl], in0=hT[:, kc, hsl], scalar=gcols[:, 32 + kc:33 + kc],
                                                                                    in1=rstd[:], op0=ALU.mult, op1=ALU.mult),
                             reads=[b_rstd, b_const], writes=[b_hT])
                for kq in range(4):
                    S.dma("sp", lambda e, kq=kq: e.dma_start(out=outT_d.rearrange("(k p) t -> p k t", p=128)[:, 4 * kq:4 * kq + 4, :], in_=hT[:, 4 * kq:4 * kq + 4, :]),
                          reads=[b_hT], writes=[b_out])
                S.barrier()
        phase_mlp()


def build2():
    nc = bass.Bass("TRN2", target_bir_lowering=False)
    with ExitStack() as st0:
        C = Ctx(nc, st0)
        _p2(nc, C, None)
        C.S.emit()
    return nc


def build_fused():
    nc = bass.Bass("TRN2", target_bir_lowering=False)
    with ExitStack() as st0:
        C = Ctx(nc, st0)
        ex = _p1(nc, C, True)
        _p2(nc, C, ex)
        C.S.emit()
    return nc


def _launch2(inp, xTs, ryT, myT, prep_only=False):
    w_in = inp["w_in"][0]
    gcols = np.zeros((128, 64), np.float32)
    gcols[:, 0:16] = inp["norm_mix_g"][0].reshape(16, 128).T
    gcols[:, 16:32] = inp["norm_mlp_g"][0].reshape(16, 128).T
    gcols[:, 32:48] = inp["norm_f_g"].reshape(16, 128).T
    gcols[:, 48] = EPS
    wg = _chunk_major(w_in[:, 7232:11328])
    wro = _chunk_major(inp["w_ret_o"][0])
    wmo = _chunk_major(inp["w_mla_o"][0])
    wout = _chunk_major(inp["w_out"][0])
    wup = _chunk_major(inp["w_up"][0])
    wd = inp["w_down"][0]
    wdn = np.ascontiguousarray(wd.reshape(64, 128, 16, 128).transpose(2, 1, 0, 3).reshape(16, 128, 8192))
    in_maps = []
    for c in range(8):
        b, j = c // 4, c % 4
        tsl = slice(j * 1024, (j + 1) * 1024)
        m2 = {"xTo": np.ascontiguousarray(xTs[b][:, tsl])} if prep_only else {
            "xT": np.ascontiguousarray(xTs[b][:, tsl]),
            "ryT": np.ascontiguousarray(ryT[b][:, tsl]),
            "myT": np.ascontiguousarray(myT[b][:, tsl])}
        in_maps.append({
            **m2,
            "gcols": gcols, "wg": wg, "wro": wro, "wmo": wmo, "wout": wout, "wup": wup, "wdn": wdn,
        })
    if prep_only:
        return in_maps
    nc = build2()
    res = run_bass_kernel_spmd(nc, in_maps, core_ids=list(range(8)))
    out = np.empty((2, S_LEN, D), np.float32)
    if DEBUG2:
        DBG_OUT.update({k: np.asarray(v) for k, v in res.results[0].items()})
    for c in range(8):
        b, j = c // 4, c % 4
        out[b, j * 1024:(j + 1) * 1024, :] = res.results[c]["outT"].T
    return out


FUSED = True


def kernel(**inp):
    inp = {k: np.asarray(v) for k, v in inp.items()}
    if not FUSED:
        xTs, ryT, myT = _launch1(inp)
        return _launch2(inp, xTs, ryT, myT)
    xTs, maps1 = _launch1(inp, prep_only=True)
    maps2 = _launch2(inp, xTs, None, None, prep_only=True)
    in_maps = [{**maps1[c], **maps2[c]} for c in range(8)]
    nc = build_fused()
    res = run_bass_kernel_spmd(nc, in_maps, core_ids=list(range(8)))
    out = np.empty((2, S_LEN, D), np.float32)
    for c in range(8):
        b, j = c // 4, c % 4
        out[b, j * 1024:(j + 1) * 1024, :] = res.results[c]["outT"].T
    return out
```

```python
import math
from contextlib import ExitStack

import numpy as np
import concourse.bass as bass
import concourse.mybir as mybir
from concourse.bass_utils import run_bass_kernel_spmd

F32 = mybir.dt.float32
BF16 = mybir.dt.bfloat16
I32 = mybir.dt.int32
AF = mybir.ActivationFunctionType
ALU = mybir.AluOpType

D = 2048
S_LEN = 4096
EPS = 1e-6
TWO_PI = 2.0 * math.pi
DEBUG2 = False
DBG_OUT = {}


class Buf:
    __slots__ = ("name", "w", "r", "dsem", "dcnt", "nobar")

    def __init__(self, name):
        self.name = name
        self.w = None
        self.r = []
        self.dsem = None
        self.dcnt = 0
        self.nobar = False


class Sched:
    ENG = ("pe", "act", "dve", "pool", "sp")

    def __init__(self, nc, stack):
        self.nc = nc
        self.stack = stack
        self.prog = {e: [] for e in self.ENG}
        self.cnt = {e: 0 for e in self.ENG}
        self.sem = {e: stack.enter_context(nc.semaphore("s_" + e)) for e in self.ENG}
        self.waited = {e: {} for e in self.ENG}
        self.dbufs = []

    def _dsem(self, b):
        if b.dsem is None:
            b.dsem = self.stack.enter_context(self.nc.semaphore("d_" + b.name))
            self.dbufs.append(b)
        return b.dsem

    def _collect(self, E, reads, writes):
        waits = {}

        def need(dep):
            if dep is None:
                return
            if dep[0] == "eng":
                _, e, idx = dep
                if e == E and e == "pe":
                    return
                key = ("eng", e)
                sem = self.sem[e]
                val = idx
            else:
                _, sem, val, key = dep
            if self.waited[E].get(key, 0) >= val:
                return
            if key not in waits or waits[key][1] < val:
                waits[key] = (sem, val)

        for b in reads:
            need(b.w)
        for b in writes:
            need(b.w)
            for r in b.r:
                need(r)
        for key, (sem, val) in waits.items():
            self.waited[E][key] = val
            self.prog[E].append(lambda eng, sem=sem, val=val: eng.wait_ge(sem, val))

    def op(self, E, fn, reads=(), writes=()):
        self._collect(E, reads, writes)
        self.cnt[E] += 1
        idx = self.cnt[E]
        sem = self.sem[E]
        self.prog[E].append(lambda eng, fn=fn, sem=sem: fn(eng).then_inc(sem, 1))
        ev = ("eng", E, idx)
        for b in writes:
            b.w = ev
            b.r = []
        for b in reads:
            if b not in writes:
                b.r.append(ev)
        return ev

    def dma(self, Q, fn, reads=(), writes=(), sembuf=None):
        self._collect(Q, reads, writes)
        sb = sembuf if sembuf is not None else writes[0]
        sem = self._dsem(sb)
        sb.dcnt += 16
        val = sb.dcnt
        self.prog[Q].append(lambda eng, fn=fn, sem=sem: fn(eng).then_inc(sem, 16))
        ev = ("dma", sem, val, ("dma", sb.name))
        for b in writes:
            b.w = ev
            b.r = []
        for b in reads:
            if b not in writes:
                b.r.append(ev)
        return ev

    def cc(self, Q, fn, reads=(), writes=()):
        self._collect(Q, reads, writes)
        sb = writes[0]
        sem = self._dsem(sb)
        sb.dcnt += 1
        val = sb.dcnt
        self.prog[Q].append(lambda eng, fn=fn, sem=sem: fn(eng).then_inc(sem, 1))
        ev = ("dma", sem, val, ("dma", sb.name))
        for b in writes:
            b.w = ev
            b.r = []
        for b in reads:
            if b not in writes:
                b.r.append(ev)
        return ev

    def barrier(self):
        for E in self.ENG:
            for e in self.ENG:
                if e == E:
                    continue
                v = self.cnt[e]
                if v > 0 and self.waited[E].get(("eng", e), 0) < v:
                    self.waited[E][("eng", e)] = v
                    self.prog[E].append(lambda eng, sem=self.sem[e], val=v: eng.wait_ge(sem, val))
            for b in self.dbufs:
                key = ("dma", b.name)
                if b.nobar:
                    continue
                if b.dcnt > 0 and self.waited[E].get(key, 0) < b.dcnt:
                    self.waited[E][key] = b.dcnt
                    self.prog[E].append(lambda eng, sem=b.dsem, val=b.dcnt: eng.wait_ge(sem, val))

    def wait_all(self, E, bufs):
        self._collect(E, bufs, ())

    def emit(self):
        nc = self.nc
        with nc.Block() as block:
            @block.tensor
            def _(eng):
                for f in self.prog["pe"]:
                    f(eng)

            @block.scalar
            def _(eng):
                for f in self.prog["act"]:
                    f(eng)

            @block.vector
            def _(eng):
                for f in self.prog["dve"]:
                    f(eng)

            @block.gpsimd
            def _(eng):
                for f in self.prog["pool"]:
                    f(eng)

            @block.sync
            def _(eng):
                for f in self.prog["sp"]:
                    f(eng)


class Ctx:
    def __init__(self, nc, st):
        self.nc = nc
        self.st = st
        self.S = Sched(nc, st)
        self.nb = 0
        self.pfx = ""

    def dram(self, name, shape, dt, kind):
        return self.nc.dram_tensor(name, list(shape), dt, kind=kind).ap()

    def sb(self, stack, name, shape, dt):
        return stack.enter_context(self.nc.sbuf_tensor(self.pfx + name, list(shape), dt))

    def ps(self, stack, name, shape, dt=F32):
        return stack.enter_context(self.nc.psum_tensor(self.pfx + name, list(shape), dt))

    def buf(self, name):
        self.nb += 1
        return Buf(f"{name}_{self.nb}")


def mm_group(S, ps_ap, pairs, reads, writes):
    n = len(pairs)

    def fn(e):
        ins = None
        for i, (l, r) in enumerate(pairs):
            ins = e.matmul(ps_ap, lhsT=l, rhs=r, start=(i == 0), stop=(i == n - 1))
        return ins
    S.op("pe", fn, reads=reads, writes=writes)


def rope_tables(C, S, posf, b_posf, inv_col, sign_col, negpi, tmp, b_tmp, cos_out, sin_out, b_tab, P, N):
    v, vi, vf = tmp
    for which, off, dst in (("sin", 0.5, sin_out), ("cos", 0.75, cos_out)):
        S.op("dve", lambda e, off=off: e.tensor_scalar(out=v, in0=posf, scalar1=inv_col, scalar2=off, op0=ALU.mult, op1=ALU.add),
             reads=[b_posf], writes=[b_tmp])
        S.op("dve", lambda e: e.tensor_copy(out=vi, in_=v), reads=[b_tmp], writes=[b_tmp])
        S.op("dve", lambda e: e.tensor_copy(out=vf, in_=vi), reads=[b_tmp], writes=[b_tmp])
        S.op("dve", lambda e: e.tensor_sub(out=v, in0=v, in1=vf), reads=[b_tmp], writes=[b_tmp])
        S.op("dve", lambda e: e.scalar_tensor_tensor(out=vf, in0=v, scalar=0.0, in1=v, op0=ALU.is_lt, op1=ALU.add),
             reads=[b_tmp], writes=[b_tmp])
        S.op("dve", lambda e: e.tensor_scalar(out=vf, in0=vf, scalar1=0.0, scalar2=1.0, op0=ALU.max, op1=ALU.min),
             reads=[b_tmp], writes=[b_tmp])
        if which == "sin":
            S.op("act", lambda e, dst=dst: e.activation(out=dst, in_=vf, func=AF.Sin, bias=negpi, scale=TWO_PI),
                 reads=[b_tmp], writes=[b_tab])
            S.op("dve", lambda e, dst=dst: e.tensor_scalar(out=dst, in0=dst, scalar1=sign_col, scalar2=None, op0=ALU.mult),
                 reads=[b_tab], writes=[b_tab])
        else:
            S.op("act", lambda e, dst=dst: e.activation(out=dst, in_=vf, func=AF.Sin, bias=negpi, scale=TWO_PI),
                 reads=[b_tmp], writes=[b_tab])


def make_ut(C, S, xT_d, tb, NT, xt, b_xt, sq, b_sq, ps_ss, b_ps_ss, ones, b_const, epsc, std, rstd, b_rstd, gcol, ut, b_ut, inv_n, xw=()):
    S.dma("sp", lambda e: e.dma_start(out=xt, in_=xT_d.rearrange("(kc p) t -> p kc t", p=128)[:, :, tb * NT:(tb + 1) * NT]),
          writes=[b_xt] + list(xw))
    for q in range(4):
        S.op("act", lambda e, q=q: e.activation(out=sq[q % 2][:], in_=xt[:, 4 * q:4 * q + 4, :], func=AF.Square),
             reads=[b_xt], writes=[b_sq[q % 2]])

        def fn(e, q=q):
            ins = None
            for i in range(4):
                ins = e.matmul(ps_ss, lhsT=ones, rhs=sq[q % 2][:, i, :], start=(q == 0 and i == 0), stop=(q == 3 and i == 3))
            return ins
        S.op("pe", fn, reads=[b_sq[q % 2]] + b_const, writes=[b_ps_ss])
    S.op("act", lambda e: e.activation(out=std, in_=ps_ss, func=AF.Sqrt, bias=epsc, scale=inv_n), reads=[b_ps_ss] + b_const, writes=[b_rstd])
    S.op("dve", lambda e: e.reciprocal(out=rstd, in_=std), reads=[b_rstd], writes=[b_rstd])
    for kc in range(16):
        S.op("dve", lambda e, kc=kc: e.scalar_tensor_tensor(out=ut[:, kc, :], in0=xt[:, kc, :], scalar=gcol[:, kc:kc + 1], in1=rstd,
                                                            op0=ALU.mult, op1=ALU.mult),
             reads=[b_xt, b_rstd] + b_const, writes=[b_ut])


def _p1(nc, C, fused):
    S = C.S
    ex = {}
    with ExitStack() as st:
        NT = 512
        xT_d = C.dram("xT", [D, S_LEN], F32, "ExternalInput")
        pos_d = C.dram("pos", [128, S_LEN], I32, "ExternalInput")
        gmix_d = C.dram("gmix", [128, 16], F32, "ExternalInput")
        wra_d = C.dram("wra", [8, 128, 16 * 128], F32, "ExternalInput")
        wrb_d = C.dram("wrb", [2, 128, 16 * 512], F32, "ExternalInput")
        wl_d = C.dram("wl", [9, 128, 16 * 128], F32, "ExternalInput")
        wq_d = C.dram("wq", [512, 1024], F32, "ExternalInput")
        wkv_d = C.dram("wkv", [512, 1024], F32, "ExternalInput")
        cols_d = C.dram("cols", [128, 32], F32, "ExternalInput")
        ident_d = C.dram("ident", [128, 128], F32, "ExternalInput")
        mask2_d = C.dram("mask2", [128, 2, 128], F32, "ExternalInput")
        qdec_d = C.dram("qdec", [128, 2, 512], F32, "ExternalInput")
        gret_d = C.dram("gret", [128, 512], F32, "ExternalInput")
        if not fused:
            ryT_d = C.dram("ryT", [512, S_LEN], F32, "ExternalOutput")
            myT_d = C.dram("myT", [512, S_LEN], F32, "ExternalOutput")
        else:
            for nm in ("R", "M"):
                ex[nm + "src"] = [nc.dram_tensor(f"ex{nm}src{i}", [512, 512], BF16).ap() for i in range(8)]
                ex[nm + "dst"] = [nc.dram_tensor(f"ex{nm}dst{i}", [2048, 512], BF16).ap() for i in range(8)]
                ex["b" + nm + "src"] = [C.buf(f"ex{nm}src{i}") for i in range(8)]
                ex["b" + nm + "dst"] = [C.buf(f"ex{nm}dst{i}") for i in range(8)]
                for b_ in ex["b" + nm + "dst"]:
                    b_.nobar = True
        XDT = BF16 if fused else F32
        GROUPS = [[0, 1, 2, 3], [4, 5, 6, 7]]

        def gather(nm, sidx):
            S.cc("pool", lambda e: e.collective_compute("AllGather", ALU.bypass, replica_groups=GROUPS,
                                                        ins=[ex[nm + "src"][sidx].opt()], outs=[ex[nm + "dst"][sidx].opt()]),
                 reads=[ex["b" + nm + "src"][sidx]], writes=[ex["b" + nm + "dst"][sidx]])

        cols = C.sb(st, "cols_s", [128, 32], F32)
        gmix = C.sb(st, "gmixs", [128, 16], F32)
        ident = C.sb(st, "idents", [128, 128], BF16)
        ones = C.sb(st, "oness", [128, 128], BF16)
        b_const = C.buf("const")
        S.dma("sp", lambda e: e.dma_start(out=cols[:], in_=cols_d), writes=[b_const])
        S.dma("sp", lambda e: e.dma_start(out=gmix[:], in_=gmix_d), writes=[b_const])
        b_ident = C.buf("ident")
        S.dma("pool", lambda e: e.dma_start(out=ident[:], in_=ident_d), writes=[b_ident])
        b_ones = C.buf("ones")
        S.op("dve", lambda e: e.memset(ones[:], 1.0), writes=[b_ones])
        inv128, sign128, inv64, sign64 = cols[:, 0:1], cols[:, 1:2], cols[:, 2:3], cols[:, 3:4]
        negpi, epsc = cols[:, 4:5], cols[:, 5:6]

        psA = [C.ps(st, f"psA{i}", [128, 512]) for i in range(7)]
        psB = C.ps(st, "psB", [128, 1024], BF16)
        b_psA = [C.buf(f"psA{i}") for i in range(7)]
        b_psB = C.buf("psB")

        def _phase1():
            with ExitStack() as sr:
                wra = C.sb(sr, "wra_s", [128, 8, 16, 128], BF16)
                wrb = C.sb(sr, "wrb_s", [128, 2, 16, 512], BF16)
                xt = C.sb(sr, "xt", [128, 16, NT], F32)
                ut = C.sb(sr, "ut", [128, 16, NT], BF16)
                sq = [C.sb(sr, f"sq{i}", [128, 4, NT], BF16) for i in range(2)]
                std = C.sb(sr, "std", [128, NT], F32)
                rstd = C.sb(sr, "rstd", [128, NT], F32)
                posi = C.sb(sr, "posi", [128, NT], I32)
                posf = C.sb(sr, "posf", [128, NT], F32)
                tv = C.sb(sr, "tv", [128, NT], F32)
                tvi = C.sb(sr, "tvi", [128, NT], I32)
                tvf = C.sb(sr, "tvf", [128, NT], F32)
                cosT = C.sb(sr, "cosT", [128, NT], F32)
                sinT = C.sb(sr, "sinT", [128, NT], F32)
                t1 = C.sb(sr, "t1", [128, NT], F32)
                t2 = C.sb(sr, "t2", [128, NT], F32)
                qT = C.sb(sr, "qT", [128, 2, NT], BF16)
                qdT = C.sb(sr, "qdT", [128, 2, NT], BF16)
                kT = C.sb(sr, "kT", [128, 2, NT], BF16)
                kd = C.sb(sr, "kd", [128, 4, 2, 128], BF16)
                vsb = C.sb(sr, "vsb", [128, 4, 512], BF16)
                sg = C.sb(sr, "sg", [128, 4, 512], BF16)
                mask2 = C.sb(sr, "mask2s", [128, 2, 128], F32)
                qdec = C.sb(sr, "qdecs", [128, 2, 512], F32)
                gret = C.sb(sr, "grets", [128, 512], F32)
                S32 = C.sb(sr, "S32", [128, 2, 256], F32)
                Sbf = [C.sb(sr, f"Sbf{i}", [128, 2, 256], BF16) for i in range(2)]
                sTm = [C.sb(sr, f"sTm{i}", [128, 128], BF16) for i in range(2)]
                stats = C.sb(sr, "stats", [128, 2, 6], F32)
                mv = C.sb(sr, "mv", [128, 2, 2], F32)
                sd = C.sb(sr, "sd", [128, 2], F32)
                rs = C.sb(sr, "rs", [128, 2], F32)
                nbias = C.sb(sr, "nbias", [128, 2], F32)
                on = C.sb(sr, "on", [128, 512], F32)
                rytm = C.sb(sr, "rytm", [128, 512], BF16)
                ryTs = C.sb(sr, "ryTs", [128, 4, NT], XDT)

                b_wra, b_wrb = C.buf("wra"), C.buf("wrb")
                b_xt, b_ut, b_rstd = C.buf("xt"), C.buf("ut"), C.buf("rstd")
                b_sq = [C.buf("sq0"), C.buf("sq1")]
                b_posf, b_tmp, b_tab = C.buf("posf"), C.buf("tmp"), C.buf("tab")
                b_t1, b_t2 = C.buf("t1"), C.buf("t2")
                b_qT, b_kT, b_kd, b_v, b_sg = C.buf("qT"), C.buf("kT"), C.buf("kd"), C.buf("v"), C.buf("sg")
                b_S32 = C.buf("S32")
                b_Sbf = [C.buf("Sbf0"), C.buf("Sbf1")]
                b_sTm = [C.buf("sTm0"), C.buf("sTm1")]
                b_st, b_on, b_rytm, b_ryTs = C.buf("stats"), C.buf("on"), C.buf("rytm"), C.buf("ryTs")
                b_ryout = C.buf("ryout")

                for ch in range(8):
                    S.dma("pool", lambda e, ch=ch: e.dma_start(out=wra[:, ch, :, :], in_=wra_d[ch].rearrange("p (k n) -> p k n", k=16)), writes=[b_wra])
                for j in range(2):
                    for hk in range(2):
                        S.dma("pool", lambda e, j=j, hk=hk: e.dma_start(out=wrb[:, j, 8 * hk:8 * hk + 8, :],
                                                                      in_=wrb_d[j].rearrange("p (k n) -> p k n", k=16)[:, 8 * hk:8 * hk + 8, :]),
                              writes=[b_wrb])
                S.dma("sp", lambda e: e.dma_start(out=mask2[:], in_=mask2_d), writes=[b_const])
                S.dma("sp", lambda e: e.dma_start(out=qdec[:], in_=qdec_d), writes=[b_const])
                S.dma("sp", lambda e: e.dma_start(out=gret[:], in_=gret_d), writes=[b_const])
                S.op("dve", lambda e: e.memset(S32[:], 0.0), writes=[b_S32])
                S.op("dve", lambda e: e.memset(Sbf[0][:], 0.0), writes=[b_Sbf[0]])

                ppc = [0]
                def prep_(tb):
                    make_ut(C, S, xT_d, tb, NT, xt[:], b_xt, sq, b_sq, psA[6][:], b_psA[6], ones[:], [b_const, b_ones], epsc, std[:], rstd[:], b_rstd,
                            gmix, ut, b_ut, 1.0 / D)
                    S.dma("sp", lambda e, tb=tb: e.dma_start(out=posi[:], in_=pos_d[:, tb * NT:(tb + 1) * NT]), writes=[b_posf])
                    S.op("dve", lambda e: e.tensor_copy(out=posf[:], in_=posi[:]), reads=[b_posf], writes=[b_posf])
                    rope_tables(C, S, posf[:], b_posf, inv128, sign128, negpi, (tv[:], tvi[:], tvf[:]), b_tmp, cosT[:], sinT[:], b_tab, 128, NT)

                def proj_(tb):
                    pp = ppc[0]
                    for hh in range(2):
                        for qk in range(2):
                            ch = hh * 4 + qk * 2
                            pa, pb = pp % 4, (pp + 1) % 4
                            pp += 2
                            mm_group(S, psA[pa][:], [(wra[:, ch, kc, :], ut[:, kc, :]) for kc in range(16)], [b_wra, b_ut], [b_psA[pa]])
                            mm_group(S, psA[pb][:], [(wra[:, ch + 1, kc, :], ut[:, kc, :]) for kc in range(16)], [b_wra, b_ut], [b_psA[pb]])
                            S.op("dve", lambda e, pa=pa: e.tensor_tensor(out=t1[:], in0=psA[pa][:], in1=cosT[:], op=ALU.mult),
                                 reads=[b_psA[pa], b_tab], writes=[b_t1])
                            S.op("dve", lambda e, pb=pb: e.tensor_tensor(out=t2[:], in0=psA[pb][:], in1=sinT[:], op=ALU.mult),
                                 reads=[b_psA[pb], b_tab], writes=[b_t2])
                            if qk == 0:
                                S.op("pool", lambda e: e.tensor_tensor(out=t1[:], in0=t1[:], in1=t2[:], op=ALU.add), reads=[b_t1, b_t2], writes=[b_t1])
                                S.op("pool", lambda e, hh=hh: e.tensor_copy(out=qT[:, hh, :], in_=t1[:]), reads=[b_t1], writes=[b_qT])
                                S.op("pool", lambda e, hh=hh: e.tensor_tensor(out=qdT[:, hh, :], in0=t1[:], in1=qdec[:, hh, :], op=ALU.mult),
                                     reads=[b_t1, b_const], writes=[b_qT])
                            else:
                                S.op("pool", lambda e, hh=hh: e.tensor_tensor(out=kT[:, hh, :], in0=t1[:], in1=t2[:], op=ALU.add),
                                     reads=[b_t1, b_t2], writes=[b_kT])
                    for tt in range(4):
                        pa, pb = pp % 4, (pp + 1) % 4
                        pp += 2
                        mm_group(S, psA[pa][:], [(ut[:, kc, tt * 128:(tt + 1) * 128], wrb[:, 0, kc, :]) for kc in range(16)], [b_wrb, b_ut], [b_psA[pa]])
                        mm_group(S, psA[pb][:], [(ut[:, kc, tt * 128:(tt + 1) * 128], wrb[:, 1, kc, :]) for kc in range(16)], [b_wrb, b_ut], [b_psA[pb]])
                        S.op("act", lambda e, pa=pa, tt=tt: e.activation(out=vsb[:, tt, :], in_=psA[pa][:], func=AF.Copy), reads=[b_psA[pa]], writes=[b_v])
                        S.op("act", lambda e, pb=pb, tt=tt: e.activation(out=sg[:, tt, :], in_=psA[pb][:], func=AF.Silu), reads=[b_psA[pb]], writes=[b_sg])
                    for tt in range(4):
                        for hh in range(2):
                            S.op("pe", lambda e, tt=tt, hh=hh: e.transpose(psB[:, 0:128], kT[:, hh, tt * 128:(tt + 1) * 128], ident[:]),
                                 reads=[b_kT, b_ident], writes=[b_psB])
                            S.op("dve", lambda e, tt=tt, hh=hh: e.tensor_scalar(out=kd[:, tt, hh, :], in0=psB[:, 0:128], scalar1=cols[:, 6 + hh:7 + hh],
                                                                                  scalar2=None, op0=ALU.mult),
                                 reads=[b_psB, b_const], writes=[b_kd])
                    ppc[0] = pp

                def ret_(tb):
                    for tt in range(4):
                        T = tb * 4 + tt
                        cur, nxt = T % 2, (T + 1) % 2
                        tsl = slice(tt * 128, (tt + 1) * 128)
                        for hh in range(2):
                            vsl = slice(hh * 256, (hh + 1) * 256)
                            si = hh
                            mm_group(S, psA[4][:, 0:128], [(kT[:, hh, tsl], qT[:, hh, tsl])], [b_kT, b_qT], [b_psA[4]])
                            S.op("dve", lambda e, hh=hh, si=si: e.tensor_tensor(out=sTm[si][:], in0=psA[4][:, 0:128], in1=mask2[:, hh, :], op=ALU.mult),
                                 reads=[b_psA[4], b_const], writes=[b_sTm[si]])
                            mm_group(S, psA[5][:, vsl], [(sTm[si][:], vsb[:, tt, vsl]), (qdT[:, hh, tsl], Sbf[cur][:, hh, :])],
                                     [b_sTm[si], b_v, b_qT, b_Sbf[cur]], [b_psA[5]])
                            mm_group(S, psA[4][:, 256:512], [(kd[:, tt, hh, :], vsb[:, tt, vsl])], [b_kd, b_v], [b_psA[4]])
                            S.op("dve", lambda e, hh=hh: e.scalar_tensor_tensor(out=S32[:, hh, :], in0=S32[:, hh, :], scalar=cols[:, 8 + hh:9 + hh],
                                                                                 in1=psA[4][:, 256:512], op0=ALU.mult, op1=ALU.add),
                                 reads=[b_psA[4], b_const], writes=[b_S32])
                            S.op("act", lambda e, hh=hh, nxt=nxt: e.activation(out=Sbf[nxt][:, hh, :], in_=S32[:, hh, :], func=AF.Copy),
                                 reads=[b_S32], writes=[b_Sbf[nxt]])
                            S.op("dve", lambda e, hh=hh, vsl=vsl: e.bn_stats(out=stats[:, hh, :], in_=psA[5][:, vsl]), reads=[b_psA[5]], writes=[b_st])
                            S.op("dve", lambda e, hh=hh: e.bn_aggr(out=mv[:, hh, :], in_=stats[:, hh, :]), reads=[b_st], writes=[b_st])
                            S.op("act", lambda e, hh=hh: e.activation(out=sd[:, hh:hh + 1], in_=mv[:, hh, 1:2], func=AF.Sqrt, bias=epsc, scale=1.0),
                                 reads=[b_st, b_const], writes=[b_st])
                            S.op("dve", lambda e, hh=hh: e.reciprocal(out=rs[:, hh:hh + 1], in_=sd[:, hh:hh + 1]), reads=[b_st], writes=[b_st])
                            S.op("dve", lambda e, hh=hh: e.tensor_scalar(out=nbias[:, hh:hh + 1], in0=mv[:, hh, 0:1], scalar1=rs[:, hh:hh + 1], scalar2=-1.0,
                                                                          op0=ALU.mult, op1=ALU.mult), reads=[b_st], writes=[b_st])
                            S.op("act", lambda e, hh=hh, vsl=vsl: e.activation(out=on[:, vsl], in_=psA[5][:, vsl], func=AF.Identity,
                                                                                bias=nbias[:, hh:hh + 1], scale=rs[:, hh:hh + 1]),
                                 reads=[b_psA[5], b_st], writes=[b_on])
                        S.op("dve", lambda e: e.tensor_tensor(out=on[:], in0=on[:], in1=gret[:], op=ALU.mult), reads=[b_on, b_const], writes=[b_on])
                        S.op("pool", lambda e, tt=tt: e.tensor_tensor(out=rytm[:], in0=on[:], in1=sg[:, tt, :], op=ALU.mult), reads=[b_on, b_sg], writes=[b_rytm])
                        for fc in range(4):
                            S.op("pe", lambda e, fc=fc: e.transpose(psB[:, 128 + fc * 128:256 + fc * 128], rytm[:, fc * 128:(fc + 1) * 128], ident[:]),
                                 reads=[b_rytm, b_ident], writes=[b_psB])
                        S.op("act", lambda e, tt=tt: e.activation(out=ryTs[:, :, tt * 128:(tt + 1) * 128],
                                                                   in_=psB[:, 128:640].rearrange("p (f t) -> p f t", f=4), func=AF.Copy),
                             reads=[b_psB], writes=[b_ryTs])
                    if not fused:
                        S.dma("sp", lambda e, tb=tb: e.dma_start(out=ryT_d.rearrange("(f p) t -> p f t", p=128)[:, :, tb * NT:(tb + 1) * NT], in_=ryTs[:]),
                              reads=[b_ryTs], writes=[b_ryout])
                    else:
                        qq_, s0 = tb // 2, (tb % 2) * 4
                        for si in range(4):
                            S.dma("sp", lambda e, si=si, qq_=qq_, s0=s0: e.dma_start(
                                out=ex["Rsrc"][s0 + si].rearrange("(f p) t -> p f t", p=128)[:, :, qq_ * 128:(qq_ + 1) * 128],
                                in_=ryTs[:, :, si * 128:(si + 1) * 128]), reads=[b_ryTs], writes=[ex["bRsrc"][s0 + si]])
                        if tb >= 6:
                            for si in range(4):
                                gather("R", s0 + si)

                prep_(0)
                for tb in range(8):
                    proj_(tb)
                    if tb + 1 < 8:
                        prep_(tb + 1)
                    ret_(tb)
                S.barrier()

        _phase1()
        cqn = C.sb(st, "cqn", [128, 4, S_LEN], BF16)
        ckvn = C.sb(st, "ckvn", [128, 4, S_LEN], BF16)
        kpeT = C.sb(st, "kpeT", [64, S_LEN], BF16)
        cos64 = C.sb(st, "cos64", [64, S_LEN], BF16)
        sin64 = C.sb(st, "sin64", [64, S_LEN], BF16)
        b_cqn, b_ckvn, b_kpe, b_tab64 = C.buf("cqn"), C.buf("ckvn"), C.buf("kpe"), C.buf("tab64")

        def _phase2():
            with ExitStack() as sl:
                wl = C.sb(sl, "wl_s", [128, 9, 16, 128], BF16)
                xt = C.sb(sl, "xt2", [128, 16, NT], F32)
                ut = C.sb(sl, "ut2", [128, 16, NT], BF16)
                sq = [C.sb(sl, f"sq2{i}", [128, 4, NT], BF16) for i in range(2)]
                std = C.sb(sl, "std2", [128, NT], F32)
                rstd = C.sb(sl, "rstd2", [128, NT], F32)
                posi = C.sb(sl, "posi2", [64, NT], I32)
                posf = C.sb(sl, "posf2", [64, NT], F32)
                tv = C.sb(sl, "tv2", [64, NT], F32)
                tvi = C.sb(sl, "tvi2", [64, NT], I32)
                tvf = C.sb(sl, "tvf2", [64, NT], F32)
                c32 = C.sb(sl, "c32", [128, 4, NT], F32)
                csq = sq[0]
                std2 = std
                rstd2 = rstd
                t1 = tv
                t2 = tvf
                b_wl = C.buf("wl")
                b_xt, b_ut, b_rstd = C.buf("xt"), C.buf("ut"), C.buf("rstd")
                b_sq = [C.buf("sq0"), C.buf("sq1")]
                b_posf, b_tmp = C.buf("posf"), C.buf("tmp")
                b_c32, b_csq, b_rstd2 = C.buf("c32"), b_sq[0], b_rstd
                b_t1, b_t2 = b_tmp, b_tmp
                for ch in range(9):
                    S.dma("pool", lambda e, ch=ch: e.dma_start(out=wl[:, ch, :, :], in_=wl_d[ch].rearrange("p (k n) -> p k n", k=16)), writes=[b_wl])
                pp = 0
                for tb in range(8):
                    bsl = slice(tb * NT, (tb + 1) * NT)
                    make_ut(C, S, xT_d, tb, NT, xt[:], b_xt, sq, b_sq, psA[6][:], b_psA[6], ones[:], [b_const, b_ones], epsc, std[:], rstd[:], b_rstd,
                            gmix, ut, b_ut, 1.0 / D)
                    S.dma("sp", lambda e, bsl=bsl: e.dma_start(out=posi[:], in_=pos_d[0:64, bsl]), writes=[b_posf])
                    S.op("dve", lambda e: e.tensor_copy(out=posf[:], in_=posi[:]), reads=[b_posf], writes=[b_posf])
                    rope_tables(C, S, posf[:], b_posf, inv64[0:64, :], sign64[0:64, :], negpi[0:64, :], (tv[:], tvi[:], tvf[:]), b_tmp,
                                cos64[:, bsl], sin64[:, bsl], b_tab64, 64, NT)
                    for which in range(2):
                        dst, b_dst, gbase = (cqn, b_cqn, 10) if which == 0 else (ckvn, b_ckvn, 14)
                        for c4 in range(4):
                            ch = which * 4 + c4
                            pa = pp % 4
                            pp += 1
                            mm_group(S, psA[pa][:], [(wl[:, ch, kc, :], ut[:, kc, :]) for kc in range(16)], [b_wl, b_ut], [b_psA[pa]])
                            S.op("act", lambda e, pa=pa, c4=c4: e.activation(out=c32[:, c4, :], in_=psA[pa][:], func=AF.Copy), reads=[b_psA[pa]], writes=[b_c32])
                            S.op("pool", lambda e, c4=c4: e.tensor_tensor(out=csq[:, c4, :], in0=c32[:, c4, :], in1=c32[:, c4, :], op=ALU.mult),
                                 reads=[b_c32], writes=[b_csq])
                        mm_group(S, psA[5][:], [(ones[:], csq[:, c4, :]) for c4 in range(4)], [b_csq, b_ones], [b_psA[5]])
                        S.op("act", lambda e: e.activation(out=std2[:], in_=psA[5][:], func=AF.Sqrt, bias=epsc, scale=1.0 / 512), reads=[b_psA[5], b_const],
                             writes=[b_rstd2])
                        S.op("dve", lambda e: e.reciprocal(out=rstd2[:], in_=std2[:]), reads=[b_rstd2], writes=[b_rstd2])
                        for c4 in range(4):
                            S.op("dve", lambda e, c4=c4, dst=dst, gbase=gbase, bsl=bsl: e.scalar_tensor_tensor(
                                out=dst[:, c4, bsl], in0=c32[:, c4, :], scalar=cols[:, gbase + c4:gbase + c4 + 1], in1=rstd2[:], op0=ALU.mult, op1=ALU.mult),
                                reads=[b_c32, b_rstd2, b_const], writes=[b_dst])
                    mm_group(S, psA[4][0:64, :], [(wl[:, 8, kc, 0:64], ut[:, kc, :]) for kc in range(16)], [b_wl, b_ut], [b_psA[4]])
                    S.op("dve", lambda e, bsl=bsl: e.tensor_tensor(out=t1[:], in0=psA[4][0:64, :], in1=cos64[:, bsl], op=ALU.mult),
                         reads=[b_psA[4], b_tab64], writes=[b_t1])
                    mm_group(S, psA[4][0:64, :], [(wl[:, 8, kc, 64:128], ut[:, kc, :]) for kc in range(16)], [b_wl, b_ut], [b_psA[4]])
                    S.op("dve", lambda e, bsl=bsl: e.tensor_tensor(out=t2[:], in0=psA[4][0:64, :], in1=sin64[:, bsl], op=ALU.mult),
                         reads=[b_psA[4], b_tab64], writes=[b_t2])
                    S.op("pool", lambda e, bsl=bsl: e.tensor_tensor(out=kpeT[:, bsl], in0=t1[:], in1=t2[:], op=ALU.add), reads=[b_t1, b_t2], writes=[b_kpe])
                S.barrier()

        _phase2()
        def _phase3():
            with ExitStack() as sm:
                wq = C.sb(sm, "wq_s", [128, 4, 1024], BF16)
                wkv = C.sb(sm, "wkv_s", [128, 4, 1024], BF16)
                vall = C.sb(sm, "vall", [128, 32, 512], BF16)
                knT = [C.sb(sm, f"knT{i}", [128, S_LEN], BF16) for i in range(2)]
                qn = [C.sb(sm, f"qn{i}", [128, 512], BF16) for i in range(2)]
                qp = [C.sb(sm, f"qp{i}", [64, 512], BF16) for i in range(2)]
                t1 = C.sb(sm, "t1c", [64, 512], F32)
                t2 = C.sb(sm, "t2c", [64, 512], F32)
                pT = [C.sb(sm, f"pT{i}", [128, 512], BF16) for i in range(3)]
                rec = C.sb(sm, "rec", [128, 512], F32)
                stage = [C.sb(sm, f"stage{i}", [128, 512], XDT) for i in range(2)]
                b_wq, b_wkv, b_vall = C.buf("wq"), C.buf("wkv"), C.buf("vall")
                b_knT = [C.buf("knT0"), C.buf("knT1")]
                b_qn = [C.buf("qn0"), C.buf("qn1")]
                b_qp = [C.buf("qp0"), C.buf("qp1")]
                b_t1, b_t2, b_rec = C.buf("t1"), C.buf("t2"), C.buf("rec")
                b_pT = [C.buf("pT0"), C.buf("pT1"), C.buf("pT2")]
                b_stage = [C.buf("stage0"), C.buf("stage1")]
                b_myout = C.buf("myout")
                S.dma("pool", lambda e: e.dma_start(out=wq[:], in_=wq_d.rearrange("(k p) n -> p k n", p=128)), writes=[b_wq])
                S.dma("pool", lambda e: e.dma_start(out=wkv[:], in_=wkv_d.rearrange("(k p) n -> p k n", p=128)), writes=[b_wkv])
                for T in range(32):
                    pa = T % 2
                    mm_group(S, psA[pa][:], [(ckvn[:, kc, T * 128:(T + 1) * 128], wkv[:, kc, 512:1024]) for kc in range(4)], [b_ckvn, b_wkv], [b_psA[pa]])
                    S.op("act", lambda e, pa=pa, T=T: e.activation(out=vall[:, T, :], in_=psA[pa][:], func=AF.Copy), reads=[b_psA[pa]], writes=[b_vall])
                sc = 192.0 ** -0.5
                it = 0
                pi = 0
                for h in range(4):
                    kb_ = h % 2
                    for tb in range(8):
                        pa = tb % 2
                        mm_group(S, psA[pa][:], [(wkv[:, kc, h * 128:(h + 1) * 128], ckvn[:, kc, tb * 512:(tb + 1) * 512]) for kc in range(4)],
                                 [b_ckvn, b_wkv], [b_psA[pa]])
                        S.op("dve", lambda e, pa=pa, tb=tb, kb_=kb_: e.tensor_copy(out=knT[kb_][:, tb * 512:(tb + 1) * 512], in_=psA[pa][:]),
                             reads=[b_psA[pa]], writes=[b_knT[kb_]])
                    for qb in range(8):
                        qi = it % 2
                        it += 1
                        qsl = slice(qb * 512, (qb + 1) * 512)
                        mm_group(S, psA[2][:], [(wq[:, kc, h * 256:h * 256 + 128], cqn[:, kc, qsl]) for kc in range(4)], [b_cqn, b_wq], [b_psA[2]])
                        S.op("act", lambda e, qi=qi: e.activation(out=qn[qi][:], in_=psA[2][:], func=AF.Copy), reads=[b_psA[2]], writes=[b_qn[qi]])
                        mm_group(S, psA[3][0:64, :], [(wq[:, kc, h * 256 + 128:h * 256 + 192], cqn[:, kc, qsl]) for kc in range(4)], [b_cqn, b_wq], [b_psA[3]])
                        S.op("dve", lambda e, qsl=qsl: e.tensor_tensor(out=t1[:], in0=psA[3][0:64, :], in1=cos64[:, qsl], op=ALU.mult),
                             reads=[b_psA[3], b_tab64], writes=[b_t1])
                        mm_group(S, psA[3][0:64, :], [(wq[:, kc, h * 256 + 192:h * 256 + 256], cqn[:, kc, qsl]) for kc in range(4)], [b_cqn, b_wq], [b_psA[3]])
                        S.op("dve", lambda e, qsl=qsl: e.tensor_tensor(out=t2[:], in0=psA[3][0:64, :], in1=sin64[:, qsl], op=ALU.mult),
                             reads=[b_psA[3], b_tab64], writes=[b_t2])
                        S.op("pool", lambda e, qi=qi: e.tensor_tensor(out=qp[qi][:], in0=t1[:], in1=t2[:], op=ALU.add), reads=[b_t1, b_t2], writes=[b_qp[qi]])
                        nkb = 4 * qb + 4

                        def c0_of(kb, qb=qb):
                            j = kb - 4 * qb
                            return 128 * j if j > 0 else 0

                        def s_mm(kb, qi=qi, kb_=kb_):
                            c0 = c0_of(kb)
                            ksl = slice(kb * 128, (kb + 1) * 128)
                            ps = psA[4 + kb % 2]
                            mm_group(S, ps[:, c0:512], [(knT[kb_][:, ksl], qn[qi][:, c0:512]), (kpeT[:, ksl], qp[qi][:, c0:512])],
                                     [b_knT[kb_], b_qn[qi], b_kpe, b_qp[qi]], [b_psA[4 + kb % 2]])
                        s_mm(0)
                        for kb in range(nkb):
                            if kb + 1 < nkb:
                                s_mm(kb + 1)
                            c0 = c0_of(kb)
                            j = kb - 4 * qb
                            pidx = pi % 3
                            pi += 1
                            S.op("act", lambda e, kb=kb, c0=c0, pidx=pidx: e.activation(out=pT[pidx][:, c0:512], in_=psA[4 + kb % 2][:, c0:512], func=AF.Exp, scale=sc),
                                 reads=[b_psA[4 + kb % 2]], writes=[b_pT[pidx]])
                            if j >= 0:
                                S.op("pool", lambda e, c0=c0, pidx=pidx: e.memset(pT[pidx][64:128, c0:c0 + 64], 0.0), writes=[b_pT[pidx]])
                            first, last = (kb == 0), (kb == nkb - 1)

                            def pv(e, kb=kb, c0=c0, pidx=pidx, first=first, last=last, h=h):
                                e.matmul(psA[0][:, c0:512], lhsT=vall[:, kb, h * 128:(h + 1) * 128], rhs=pT[pidx][:, c0:512], start=first, stop=last)
                                return e.matmul(psA[1][:, c0:512], lhsT=ones[:], rhs=pT[pidx][:, c0:512], start=first, stop=last)
                            S.op("pe", pv, reads=[b_vall, b_pT[pidx], b_ones], writes=[b_psA[0], b_psA[1]])
                        si = (h * 8 + qb) % 2
                        S.op("dve", lambda e: e.reciprocal(out=rec[:], in_=psA[1][:]), reads=[b_psA[1]], writes=[b_rec])
                        S.op("dve", lambda e, si=si: e.tensor_tensor(out=stage[si][:], in0=psA[0][:], in1=rec[:], op=ALU.mult),
                             reads=[b_psA[0], b_rec], writes=[b_stage[si]])
                        if not fused:
                            S.dma("sp", lambda e, si=si, h=h, qsl=qsl: e.dma_start(out=myT_d[h * 128:(h + 1) * 128, qsl], in_=stage[si][:]),
                                  reads=[b_stage[si]], writes=[b_myout], sembuf=b_stage[si])
                        else:
                            qq_, s0 = qb // 2, (qb % 2) * 4
                            for sj in range(4):
                                S.dma("sp", lambda e, si=si, sj=sj, h=h, qq_=qq_, s0=s0: e.dma_start(
                                    out=ex["Msrc"][s0 + sj][h * 128:(h + 1) * 128, qq_ * 128:(qq_ + 1) * 128],
                                    in_=stage[si][:, sj * 128:(sj + 1) * 128]), reads=[b_stage[si]], writes=[ex["bMsrc"][s0 + sj]])
                            if h == 3 and qb >= 6:
                                for sj in range(4):
                                    gather("M", s0 + sj)
                S.barrier()
        _phase3()
    return ex


def build1():
    nc = bass.Bass("TRN2", target_bir_lowering=False)
    with ExitStack() as st0:
        C = Ctx(nc, st0)
        _p1(nc, C, False)
        C.S.emit()
    return nc


def _chunk_major(w):
    K, N = w.shape
    n = N // 128
    return np.ascontiguousarray(w.reshape(16, 128, n, 128).transpose(2, 1, 0, 3).reshape(n, 128, 16 * 128))


def _block_major(w, bw):
    K, N = w.shape
    n = N // bw
    return np.ascontiguousarray(w.reshape(16, 128, n, bw).transpose(2, 1, 0, 3).reshape(n, 128, 16 * bw))


def _swap_halves(w, d):
    K, N = w.shape
    w4 = w.reshape(K, N // d, 2, d // 2)
    return np.ascontiguousarray(w4[:, :, ::-1, :].reshape(K, N))


def _consts1(g):
    cols = np.zeros((128, 32), np.float64)
    p = np.arange(128)
    cols[:, 0] = (10000.0 ** (-(p % 64) / 64.0)) / (2 * np.pi)
    cols[:, 1] = np.where(p < 64, -1.0, 1.0)
    cols[:, 2] = (10000.0 ** (-(p % 32) / 32.0)) / (2 * np.pi)
    cols[:, 3] = np.where((p % 64) < 32, -1.0, 1.0)
    cols[:, 4] = -np.pi
    cols[:, 5] = EPS
    mask2 = np.zeros((128, 2, 128), np.float64)
    qdec = np.zeros((128, 2, 512), np.float64)
    for hh in range(2):
        h = 2 * g + hh
        gam = 1.0 - 2.0 ** (-5.0 - h)
        cols[:, 6 + hh] = gam ** (127 - p) * (128 ** -0.5)
        cols[:, 8 + hh] = gam ** 128
        m = p[:, None]
        n = p[None, :]
        same = (m // 64) == (n // 64)
        val = np.where(same, gam ** np.abs(n - m), np.where(m < n, gam ** (n - m).clip(0), 0.0))
        mask2[:, hh, :] = val * (128 ** -0.5)
        t = np.arange(512)
        qdec[:, hh, :] = (gam ** ((t % 128) + 1))[None, :]
    return cols, mask2.astype(np.float32), qdec.astype(np.float32)


def _launch1(inp, prep_only=False):
    x = inp["x"]
    pos = inp["positions"]
    w_in = inp["w_in"][0]
    w_q_b = inp["w_q_b"][0]
    w_kv_b = inp["w_kv_b"][0]
    in_maps = []
    xTs = [np.ascontiguousarray(x[b].T) for b in range(2)]
    for c in range(8):
        b, g = c // 4, c % 4
        cols, mask2, qdec = _consts1(g)
        cols[:, 10:14] = inp["q_a_norm_g"][0].reshape(4, 128).T
        cols[:, 14:18] = inp["kv_a_norm_g"][0].reshape(4, 128).T
        chunks = []
        for hh in range(2):
            h = 2 * g + hh
            wq_h = w_in[:, h * 128:(h + 1) * 128]
            wk_h = w_in[:, 1024 + h * 128:1024 + (h + 1) * 128]
            chunks += [wq_h, _swap_halves(wq_h, 128), wk_h, _swap_halves(wk_h, 128)]
        wra = _chunk_major(np.concatenate(chunks, axis=1))
        rv = w_in[:, 2048 + g * 512:2048 + (g + 1) * 512]
        rg = w_in[:, 4096 + g * 512:4096 + (g + 1) * 512]
        wrb = _block_major(np.concatenate([rv, rg], axis=1), 512)
        cq = w_in[:, 6144:6656]
        ckv = w_in[:, 6656:7168]
        kpe = w_in[:, 7168:7232]
        wl = _chunk_major(np.concatenate([cq, ckv, kpe, _swap_halves(kpe, 64)], axis=1))
        wq_parts = []
        for hm in range(4 * g, 4 * g + 4):
            qh = w_q_b[:, hm * 192:(hm + 1) * 192]
            wq_parts += [qh[:, 0:128], qh[:, 128:192], _swap_halves(qh[:, 128:192], 64)]
        wq = np.ascontiguousarray(np.concatenate(wq_parts, axis=1))
        kn = [w_kv_b[:, hm * 256:hm * 256 + 128] for hm in range(4 * g, 4 * g + 4)]
        vv = [w_kv_b[:, hm * 256 + 128:hm * 256 + 256] for hm in range(4 * g, 4 * g + 4)]
        wkv = np.ascontiguousarray(np.concatenate(kn + vv, axis=1))
        in_maps.append({
            "xT": xTs[b],
            "pos": np.ascontiguousarray(np.broadcast_to(pos[b].astype(np.int32)[None, :], (128, S_LEN))),
            "gmix": np.ascontiguousarray(inp["norm_mix_g"][0].reshape(16, 128).T),
            "wra": wra, "wrb": wrb, "wl": wl, "wq": wq, "wkv": wkv,
            "cols": cols.astype(np.float32),
            "ident": np.eye(128, dtype=np.float32),
            "mask2": mask2, "qdec": qdec,
            "gret": np.ascontiguousarray(np.broadcast_to(inp["ret_norm_g"][0][g * 512:(g + 1) * 512][None, :], (128, 512))),
        })
    if prep_only:
        return xTs, in_maps
    nc = build1()
    res = run_bass_kernel_spmd(nc, in_maps, core_ids=list(range(8)))
    ryT = [np.concatenate([res.results[b * 4 + g]["ryT"] for g in range(4)], axis=0) for b in range(2)]
    myT = [np.concatenate([res.results[b * 4 + g]["myT"] for g in range(4)], axis=0) for b in range(2)]
    return xTs, ryT, myT


def norm_from_sb(S, src, b_src, sq, b_sq, ps_ss, b_ps_ss, ones, b_cl, epsc, std, rstd, b_rstd, NT):
    for q in range(4):
        S.op("act", lambda e, q=q: e.activation(out=sq[q % 2][:], in_=src[:, 4 * q:4 * q + 4, :], func=AF.Square),
             reads=[b_src], writes=[b_sq[q % 2]])

        def fn(e, q=q):
            ins = None
            for i in range(4):
                ins = e.matmul(ps_ss, lhsT=ones, rhs=sq[q % 2][:, i, :], start=(q == 0 and i == 0), stop=(q == 3 and i == 3))
            return ins
        S.op("pe", fn, reads=[b_sq[q % 2]] + b_cl, writes=[b_ps_ss])
    S.op("act", lambda e: e.activation(out=std, in_=ps_ss, func=AF.Sqrt, bias=epsc, scale=1.0 / D), reads=[b_ps_ss] + b_cl, writes=[b_rstd])
    S.op("dve", lambda e: e.reciprocal(out=rstd, in_=std), reads=[b_rstd], writes=[b_rstd])


_JC = {}


def _jofs(e):
    if "v" not in _JC:
        _JC["v"] = (e.partition_id() % 4) * 128
    return _JC["v"]


def _p2(nc, C, ex):
    _JC.clear()
    S = C.S
    fused = ex is not None
    C.pfx = "p2_"
    with ExitStack() as st:
        NT = 512
        xT_d = C.dram("xTo" if fused else "xT", [D, 1024], F32, "ExternalInput")
        if not fused:
            ryT_d = C.dram("ryT", [D, 1024], F32, "ExternalInput")
            myT_d = C.dram("myT", [D, 1024], F32, "ExternalInput")
        gcols_d = C.dram("gcols", [128, 64], F32, "ExternalInput")
        wg_d = C.dram("wg", [32, 128, 2048], F32, "ExternalInput")
        wro_d = C.dram("wro", [16, 128, 2048], F32, "ExternalInput")
        wmo_d = C.dram("wmo", [16, 128, 2048], F32, "ExternalInput")
        wout_d = C.dram("wout", [16, 128, 2048], F32, "ExternalInput")
        wup_d = C.dram("wup", [64, 128, 2048], F32, "ExternalInput")
        wdn_d = C.dram("wdn", [16, 128, 8192], F32, "ExternalInput")
        outT_d = C.dram("outT", [D, 1024], F32, "ExternalOutput")
        if DEBUG2:
            hdbg_d = C.dram("hdbg", [D, 1024], F32, "ExternalOutput")
            hndbg_d = C.dram("hndbg", [D, 1024], BF16, "ExternalOutput")
            adbg_d = C.dram("adbg", [D, 1024], BF16, "ExternalOutput")
            b_dbg = C.buf("dbg")

        gcols = C.sb(st, "gcols_s", [128, 64], F32)
        ones = C.sb(st, "ones_s", [128, 128], BF16)
        b_const, b_ones = C.buf("const"), C.buf("ones")
        S.dma("sp", lambda e: e.dma_start(out=gcols[:], in_=gcols_d), writes=[b_const])
        S.op("dve", lambda e: e.memset(ones[:], 1.0), writes=[b_ones])
        b_cl = [b_const, b_ones]
        epsc = gcols[:, 48:49]
        ps = [C.ps(st, f"ps{i}", [128, 512]) for i in range(8)]
        b_ps = [C.buf(f"ps{i}") for i in range(8)]
        hT = C.sb(st, "hT", [128, 16, 1024], F32)
        b_hT = C.buf("hT")

        def phase_ac():
            with ExitStack() as sa:
                xt = C.sb(sa, "xt", [128, 16, NT], F32)
                ut = C.sb(sa, "ut", [128, 16, NT], BF16)
                sq = [C.sb(sa, f"sq{i}", [128, 4, NT], BF16) for i in range(2)]
                std = C.sb(sa, "std", [128, NT], F32)
                rstd = C.sb(sa, "rstd", [128, NT], F32)
                rys = C.sb(sa, "rys", [128, 4, 16, 128], BF16)
                mys = C.sb(sa, "mys", [128, 4, 16, 128], BF16)
                merged = C.sb(sa, "merged", [128, 16, NT], BF16)
                wB = [[C.sb(sa, f"wB{i}_{j}", [128, 16, 128], BF16) for j in range(4)] for i in range(2)]
                wC = [wB[0][0], wB[1][0]]
                s1 = xt[:, 0:1, :].rearrange("p a t -> p (a t)")
                s2 = xt[:, 1:2, :].rearrange("p a t -> p (a t)")
                m1 = s1
                m2 = s2
                xc = [xt[:, 2:3, :].rearrange("p a t -> p (a t)"), xt[:, 3:4, :].rearrange("p a t -> p (a t)")]
                b_xt, b_ut, b_rstd = C.buf("xt"), C.buf("ut"), C.buf("rstd")
                b_sq = [C.buf("sq0"), C.buf("sq1")]
                b_rys, b_mys, b_merged = C.buf("rys"), C.buf("mys"), C.buf("merged")
                b_wB = [[C.buf(f"wB{i}{j}") for j in range(4)] for i in range(2)]
                b_wC = [b_wB[0][0], b_wB[1][0]]
                b_s1, b_s2 = C.buf("s1"), C.buf("s2")
                b_m1, b_m2 = b_s1, b_s2
                b_xc = [C.buf("xc0"), C.buf("xc1")]
                for hf in range(2):
                    hsl = slice(hf * NT, (hf + 1) * NT)
                    make_ut(C, S, xT_d, hf, NT, xt[:], b_xt, sq, b_sq, ps[6][:], b_ps[6], ones[:], b_cl, epsc, std[:], rstd[:], b_rstd,
                            gcols, ut, b_ut, 1.0 / D, xw=[b_s1, b_s2, b_xc[0], b_xc[1]])
                    if not fused:
                        S.dma("pool", lambda e, hsl=hsl: e.dma_start(out=rys[:].rearrange("p s k t -> p k s t"), in_=ryT_d[:, hsl].rearrange("(k p) (s t) -> p k s t", p=128, s=4)), writes=[b_rys])
                        S.dma("pool", lambda e, hsl=hsl: e.dma_start(out=mys[:].rearrange("p s k t -> p k s t"), in_=myT_d[:, hsl].rearrange("(k p) (s t) -> p k s t", p=128, s=4)), writes=[b_mys])
                    else:
                        for nm, dstt, b_d in (("R", rys, b_rys), ("M", mys, b_mys)):
                            for si in range(4):
                                sidx = hf * 4 + si
                                S.dma("pool", lambda e, nm=nm, dstt=dstt, si=si, sidx=sidx: e.dma_start(
                                    out=dstt[:, si, :, :],
                                    in_=ex[nm + "dst"][sidx].rearrange("(k p) t -> p k t", p=128)[:, :, bass.ds(_jofs(e), 128)]),
                                    reads=[ex["b" + nm + "dst"][sidx]], writes=[b_d])
                    for oc in range(16):
                        r = oc % 2
                        srcs = (wg_d[oc], wg_d[16 + oc], wro_d[oc], wmo_d[oc])
                        for j in range(4):
                            S.dma("pool", lambda e, r=r, j=j, src=srcs[j]: e.dma_start(out=wB[r][j][:], in_=src.rearrange("p (k n) -> p k n", k=16)),
                                  writes=[b_wB[r][j]])
                        pb = 0 if r == 0 else 2
                        mm_group(S, ps[pb][:], [(wB[r][0][:, kc, :], ut[:, kc, :]) for kc in range(16)], [b_wB[r][0], b_ut], [b_ps[pb]])
                        S.op("act", lambda e, pb=pb: e.activation(out=s1, in_=ps[pb][:], func=AF.Sigmoid), reads=[b_ps[pb]], writes=[b_s1])
                        mm_group(S, ps[pb + 1][:], [(wB[r][1][:, kc, :], ut[:, kc, :]) for kc in range(16)], [b_wB[r][1], b_ut], [b_ps[pb + 1]])
                        S.op("act", lambda e, pb=pb: e.activation(out=s2, in_=ps[pb + 1][:], func=AF.Sigmoid), reads=[b_ps[pb + 1]], writes=[b_s2])
                        mm_group(S, ps[pb + 4][:].rearrange("p (s t) -> p s t", s=4), [(wB[r][2][:, kc, :], rys[:, :, kc, :]) for kc in range(16)], [b_wB[r][2], b_rys], [b_ps[pb + 4]])
                        S.op("dve", lambda e, pb=pb: e.tensor_tensor(out=m1, in0=ps[pb + 4][:], in1=s1, op=ALU.mult),
                             reads=[b_ps[pb + 4], b_s1], writes=[b_m1])
                        mm_group(S, ps[pb + 5][:].rearrange("p (s t) -> p s t", s=4), [(wB[r][3][:, kc, :], mys[:, :, kc, :]) for kc in range(16)], [b_wB[r][3], b_mys], [b_ps[pb + 5]])
                        S.op("dve", lambda e, pb=pb: e.tensor_tensor(out=m2, in0=ps[pb + 5][:], in1=s2, op=ALU.mult),
                             reads=[b_ps[pb + 5], b_s2], writes=[b_m2])
                        S.op("dve", lambda e, oc=oc: e.tensor_tensor(out=merged[:, oc, :], in0=m1, in1=m2, op=ALU.add),
                             reads=[b_m1, b_m2], writes=[b_merged])
                    for oc in range(16):
                        r = oc % 2
                        S.dma("pool", lambda e, r=r, oc=oc: e.dma_start(out=wC[r][:], in_=wout_d[oc].rearrange("p (k n) -> p k n", k=16)), writes=[b_wC[r]])
                        S.dma("sp", lambda e, r=r, oc=oc, hsl=hsl: e.dma_start(out=xc[r], in_=xT_d[oc * 128:(oc + 1) * 128, hsl]), writes=[b_xc[r], b_xt])
                        mm_group(S, ps[r][:], [(wC[r][:, kc, :], merged[:, kc, :]) for kc in range(16)], [b_wC[r], b_merged], [b_ps[r]])
                        S.op("dve", lambda e, r=r, oc=oc, hsl=hsl: e.tensor_tensor(out=hT[:, oc, hsl], in0=ps[r][:], in1=xc[r], op=ALU.add),
                             reads=[b_ps[r], b_xc[r]], writes=[b_hT])
                if DEBUG2:
                    S.dma("sp", lambda e: e.dma_start(out=hdbg_d.rearrange("(k p) t -> p k t", p=128), in_=hT[:]), reads=[b_hT], writes=[b_dbg])
                S.barrier()
        phase_ac()

        def phase_mlp():
            with ExitStack() as sa:
                hn = C.sb(sa, "hn", [128, 16, 1024], BF16)
                aT = C.sb(sa, "aT", [128, 16, 1024], BF16)
                sq = [C.sb(sa, f"sqm{i}", [128, 4, NT], BF16) for i in range(2)]
                std = C.sb(sa, "stdm", [128, NT], F32)
                rstd = C.sb(sa, "rstdm", [128, NT], F32)
                wU = [C.sb(sa, f"wU{i}", [128, 16, 128], BF16) for i in range(4)]
                wD = [C.sb(sa, f"wD{i}", [128, 16, 128], BF16) for i in range(3)]
                rl = [C.sb(sa, f"rl{i}", [128, NT], F32) for i in range(2)]
                b_hn, b_aT, b_rstd = C.buf("hn"), C.buf("aT"), C.buf("rstd")
                b_sq = [C.buf("sq0"), C.buf("sq1")]
                b_wU = [C.buf(f"wU{i}") for i in range(4)]
                b_wD = [C.buf(f"wD{i}") for i in range(3)]
                b_rl = [C.buf("rl0"), C.buf("rl1")]
                b_out = C.buf("out")
                for hf in range(2):
                    hsl = slice(hf * NT, (hf + 1) * NT)
                    norm_from_sb(S, hT[:, :, hsl], b_hT, sq, b_sq, ps[6][:], b_ps[6], ones[:], b_cl, epsc, std[:], rstd[:], b_rstd, NT)
                    for kc in range(16):
                        S.op("dve", lambda e, kc=kc, hsl=hsl: e.scalar_tensor_tensor(out=hn[:, kc, hsl], in0=hT[:, kc, hsl], scalar=gcols[:, 16 + kc:17 + kc],
                                                                                    in1=rstd[:], op0=ALU.mult, op1=ALU.mult),
                             reads=[b_hT, b_rstd, b_const], writes=[b_hn])
                ui = 0
                di = 0
                zi = 0
                for qq in range(4):
                    for fcl in range(16):
                        fc = qq * 16 + fcl
                        r = ui % 4
                        ui += 1
                        S.dma("pool", lambda e, r=r, fc=fc: e.dma_start(out=wU[r][:], in_=wup_d[fc].rearrange("p (k n) -> p k n", k=16)), writes=[b_wU[r]])
                        for hf in range(2):
                            hsl = slice(hf * NT, (hf + 1) * NT)
                            pz = zi % 4
                            zi += 1
                            mm_group(S, ps[pz][:], [(wU[r][:, kc, :], hn[:, kc, hsl]) for kc in range(16)], [b_wU[r], b_hn], [b_ps[pz]])
                            S.op("act", lambda e, pz=pz: e.activation(out=rl[pz % 2][:], in_=ps[pz][:], func=AF.Relu), reads=[b_ps[pz]], writes=[b_rl[pz % 2]])
                            S.op("dve", lambda e, pz=pz, fcl=fcl, hsl=hsl: e.tensor_tensor(out=aT[:, fcl, hsl], in0=rl[pz % 2][:], in1=rl[pz % 2][:], op=ALU.mult),
                                 reads=[b_rl[pz % 2]], writes=[b_aT])
                    for oc in range(16):
                        r = di % 3
                        di += 1
                        S.dma("pool", lambda e, r=r, oc=oc, qq=qq: e.dma_start(out=wD[r][:], in_=wdn_d[oc][:, qq * 2048:(qq + 1) * 2048].rearrange("p (k n) -> p k n", k=16)),
                              writes=[b_wD[r]])
                        for hf in range(2):
                            hsl = slice(hf * NT, (hf + 1) * NT)
                            pd = 4 + (oc * 2 + hf) % 2
                            mm_group(S, ps[pd][:], [(wD[r][:, fcl, :], aT[:, fcl, hsl]) for fcl in range(16)], [b_wD[r], b_aT], [b_ps[pd]])
                            S.op("dve", lambda e, pd=pd, oc=oc, hsl=hsl: e.tensor_tensor(out=hT[:, oc, hsl], in0=ps[pd][:], in1=hT[:, oc, hsl], op=ALU.add),
                                 reads=[b_ps[pd]], writes=[b_hT])
                if DEBUG2:
                    S.dma("sp", lambda e: e.dma_start(out=hndbg_d.rearrange("(k p) t -> p k t", p=128), in_=hn[:]), reads=[b_hn], writes=[b_dbg])
                    S.dma("sp", lambda e: e.dma_start(out=adbg_d.rearrange("(k p) t -> p k t", p=128), in_=aT[:]), reads=[b_aT], writes=[b_dbg])
                for hf in range(2):
                    hsl = slice(hf * NT, (hf + 1) * NT)
                    norm_from_sb(S, hT[:, :, hsl], b_hT, sq, b_sq, ps[6][:], b_ps[6], ones[:], b_cl, epsc, std[:], rstd[:], b_rstd, NT)
                    for kc in range(16):
                        S.op("dve", lambda e, kc=kc, hsl=hsl: e.scalar_tensor_tensor(out=hT[:, kc, hsl], in0=hT[:, kc, hsl], scalar=gcols[:, 32 + kc:33 + kc],
                                                                                    in1=rstd[:], op0=ALU.mult, op1=ALU.mult),
                             reads=[b_rstd, b_const], writes=[b_hT])
                for kq in range(4):
                    S.dma("sp", lambda e, kq=kq: e.dma_start(out=outT_d.rearrange("(k p) t -> p k t", p=128)[:, 4 * kq:4 * kq + 4, :], in_=hT[:, 4 * kq:4 * kq + 4, :]),
                          reads=[b_hT], writes=[b_out])
                S.barrier()
        phase_mlp()


def build2():
    nc = bass.Bass("TRN2", target_bir_lowering=False)
    with ExitStack() as st0:
        C = Ctx(nc, st0)
        _p2(nc, C, None)
        C.S.emit()
    return nc


def build_fused():
    nc = bass.Bass("TRN2", target_bir_lowering=False)
    with ExitStack() as st0:
        C = Ctx(nc, st0)
        ex = _p1(nc, C, True)
        _p2(nc, C, ex)
        C.S.emit()
    return nc


def _launch2(inp, xTs, ryT, myT, prep_only=False):
    w_in = inp["w_in"][0]
    gcols = np.zeros((128, 64), np.float32)
    gcols[:, 0:16] = inp["norm_mix_g"][0].reshape(16, 128).T
    gcols[:, 16:32] = inp["norm_mlp_g"][0].reshape(16, 128).T
    gcols[:, 32:48] = inp["norm_f_g"].reshape(16, 128).T
    gcols[:, 48] = EPS
    wg = _chunk_major(w_in[:, 7232:11328])
    wro = _chunk_major(inp["w_ret_o"][0])
    wmo = _chunk_major(inp["w_mla_o"][0])
    wout = _chunk_major(inp["w_out"][0])
    wup = _chunk_major(inp["w_up"][0])
    wd = inp["w_down"][0]
    wdn = np.ascontiguousarray(wd.reshape(64, 128, 16, 128).transpose(2, 1, 0, 3).reshape(16, 128, 8192))
    in_maps = []
    for c in range(8):
        b, j = c // 4, c % 4
        tsl = slice(j * 1024, (j + 1) * 1024)
        m2 = {"xTo": np.ascontiguousarray(xTs[b][:, tsl])} if prep_only else {
            "xT": np.ascontiguousarray(xTs[b][:, tsl]),
            "ryT": np.ascontiguousarray(ryT[b][:, tsl]),
            "myT": np.ascontiguousarray(myT[b][:, tsl])}
        in_maps.append({
            **m2,
            "gcols": gcols, "wg": wg, "wro": wro, "wmo": wmo, "wout": wout, "wup": wup, "wdn": wdn,
        })
    if prep_only:
        return in_maps
    nc = build2()
    res = run_bass_kernel_spmd(nc, in_maps, core_ids=list(range(8)))
    out = np.empty((2, S_LEN, D), np.float32)
    if DEBUG2:
        DBG_OUT.update({k: np.asarray(v) for k, v in res.results[0].items()})
    for c in range(8):
        b, j = c // 4, c % 4
        out[b, j * 1024:(j + 1) * 1024, :] = res.results[c]["outT"].T
    return out


FUSED = True


def kernel(**inp):
    inp = {k: np.asarray(v) for k, v in inp.items()}
    if not FUSED:
        xTs, ryT, myT = _launch1(inp)
        return _launch2(inp, xTs, ryT, myT)
    xTs, maps1 = _launch1(inp, prep_only=True)
    maps2 = _launch2(inp, xTs, None, None, prep_only=True)
    in_maps = [{**maps1[c], **maps2[c]} for c in range(8)]
    nc = build_fused()
    res = run_bass_kernel_spmd(nc, in_maps, core_ids=list(range(8)))
    out = np.empty((2, S_LEN, D), np.float32)
    for c in range(8):
        b, j = c // 4, c % 4
        out[b, j * 1024:(j + 1) * 1024, :] = res.results[c]["outT"].T
    return out
```
